# Optimizing a Trainium2 kernel written in Bass

```python
import math
import jax, jax.numpy as jnp
from jax import lax
import numpy as np


D_MODEL = 2048
BATCH = 1
SEQ = 16384
DEPTH = 2

CHUNK = 64
QBLOCK = 128
N_MIXERS = 2
CONV_KERNEL = 31
HEAD_DIM = 64
N_HEADS = D_MODEL // (2 * HEAD_DIM)
D_FF = ((8 * D_MODEL // 3 + 255) // 256) * 256
FFN_KERNEL = 3
NUM_BUCKETS = 32
MAX_DISTANCE = 128
EPS = 1e-6
N_CONV_LAYERS = (DEPTH + 1) // 2
N_ATTN_LAYERS = DEPTH // 2

kernel_name = 'hybrid_conformer_diffattn_convffn_encoder'


def _rmsnorm(x, g):
    xf = x.astype(jnp.float32)
    y = xf * lax.rsqrt(jnp.mean(xf * xf, axis=-1, keepdims=True) + EPS)
    return (y * g.astype(jnp.float32)).astype(x.dtype)


def _layernorm(x, g, b):
    xf = x.astype(jnp.float32)
    mu = jnp.mean(xf, axis=-1, keepdims=True)
    var = jnp.mean(jnp.square(xf - mu), axis=-1, keepdims=True)
    y = (xf - mu) * lax.rsqrt(var + EPS)
    return (y * g.astype(jnp.float32) + b.astype(jnp.float32)).astype(x.dtype)


def _causal_dwconv(x, w, b):
    K, C = w.shape
    y = lax.conv_general_dilated(
        x, w[:, None, :].astype(x.dtype), window_strides=(1,), padding=[(K - 1, 0)],
        dimension_numbers=('NWC', 'WIO', 'NWC'), feature_group_count=C)
    return y + b.astype(x.dtype)


def _t5_bucket(rel):
    half = NUM_BUCKETS // 2
    max_exact = half // 2
    ret = jnp.where(rel > 0, half, 0)
    n = jnp.abs(rel)
    nf = jnp.maximum(n, 1).astype(jnp.float32)
    large = max_exact + (jnp.log(nf / max_exact) / math.log(MAX_DISTANCE / max_exact)
                         * (half - max_exact)).astype(jnp.int32)
    large = jnp.minimum(large, half - 1)
    return ret + jnp.where(n < max_exact, n, large)


def _conformer_conv(h, w_in, b_in, dw_w, dw_b, ln_g, ln_b, w_out, b_out):
    u = h @ w_in + b_in
    a, gate = jnp.split(u, 2, axis=-1)
    u = a * jax.nn.sigmoid(gate)
    u = _causal_dwconv(u, dw_w, dw_b)
    u = jax.nn.silu(_layernorm(u, ln_g, ln_b))
    return u @ w_out + b_out


def _diff_attention(h, w_qkv, lq1, lk1, lq2, lk2, subln_g, w_o, rel_bias, lambda_init):
    B, S, _ = h.shape
    qkv = h @ w_qkv
    q, k, v = jnp.split(qkv, 3, axis=-1)
    q = q.reshape(B, S, N_HEADS, 2, HEAD_DIM) * (HEAD_DIM ** -0.5)
    k = k.reshape(B, S, N_HEADS, 2, HEAD_DIM)
    v = v.reshape(B, S, N_HEADS, 2 * HEAD_DIM)
    lam = (jnp.exp(jnp.sum(lq1.astype(jnp.float32) * lk1.astype(jnp.float32)))
           - jnp.exp(jnp.sum(lq2.astype(jnp.float32) * lk2.astype(jnp.float32)))
           + lambda_init)
    nblk = S // QBLOCK
    q_blocks = q.reshape(B, nblk, QBLOCK, N_HEADS, 2, HEAD_DIM).transpose(1, 0, 2, 3, 4, 5)
    k_pos = jnp.arange(S, dtype=jnp.int32)
    k_chunk = k_pos // CHUNK
    bias_table = rel_bias.T.astype(jnp.float32)

    def one_block(args):
        qb, bi = args
        q_pos = bi * QBLOCK + jnp.arange(QBLOCK, dtype=jnp.int32)
        s = jnp.einsum('bqhcd,bkhcd->bhcqk', qb, k).astype(jnp.float32)
        bucket = _t5_bucket(k_pos[None, :] - q_pos[:, None])
        bias = bias_table[:, bucket]
        allowed = k_chunk[None, :] <= (q_pos // CHUNK)[:, None]
        s = jnp.where(allowed, s + bias[None, :, None], -jnp.inf)
        p = jax.nn.softmax(s, axis=-1)
        a = p[:, :, 0] - lam * p[:, :, 1]
        return jnp.einsum('bhqk,bkhe->bqhe', a.astype(v.dtype), v)

    o = lax.map(one_block, (q_blocks, jnp.arange(nblk, dtype=jnp.int32)))
    o = o.transpose(1, 0, 2, 3, 4).reshape(B, S, N_HEADS, 2 * HEAD_DIM)
    o = _rmsnorm(o, subln_g) * (1.0 - lambda_init)
    return o.reshape(B, S, D_MODEL) @ w_o


def _conv_ffn(h, w_up, dw_w, dw_b, w_down):
    u = h @ w_up
    u = _causal_dwconv(u, dw_w, dw_b)
    gate, val = jnp.split(u, 2, axis=-1)
    return (jax.nn.silu(gate) * val) @ w_down


def setup_inputs(seed: int = 0) -> dict:
    key = jax.random.key(seed)
    ks = jax.random.split(key, 24)
    D, F = D_MODEL, D_FF
    nc, na = N_CONV_LAYERS, N_ATTN_LAYERS
    nrm = lambda k, shape, scale: jax.random.normal(k, shape, jnp.float32) * scale
    return {
        'x': nrm(ks[0], (BATCH, SEQ, D), 1.0),
        'mix_norm': 1.0 + nrm(ks[1], (DEPTH, D), 0.02),
        'ffn_norm': 1.0 + nrm(ks[2], (DEPTH, D), 0.02),
        'conv_w_in': nrm(ks[3], (nc, D, 2 * D), D ** -0.5),
        'conv_b_in': nrm(ks[4], (nc, 2 * D), 0.02),
        'conv_dw_w': nrm(ks[5], (nc, CONV_KERNEL, D), CONV_KERNEL ** -0.5),
        'conv_dw_b': nrm(ks[6], (nc, D), 0.02),
        'conv_ln_g': 1.0 + nrm(ks[7], (nc, D), 0.02),
        'conv_ln_b': nrm(ks[8], (nc, D), 0.02),
        'conv_w_out': nrm(ks[9], (nc, D, D), D ** -0.5),
        'conv_b_out': nrm(ks[10], (nc, D), 0.02),
        'attn_w_qkv': nrm(ks[11], (na, D, 3 * D), D ** -0.5),
        'attn_lambda_q1': nrm(ks[12], (na, HEAD_DIM), 0.1),
        'attn_lambda_k1': nrm(ks[13], (na, HEAD_DIM), 0.1),
        'attn_lambda_q2': nrm(ks[14], (na, HEAD_DIM), 0.1),
        'attn_lambda_k2': nrm(ks[15], (na, HEAD_DIM), 0.1),
        'attn_subln_g': 1.0 + nrm(ks[16], (na, 2 * HEAD_DIM), 0.02),
        'attn_w_o': nrm(ks[17], (na, D, D), D ** -0.5),
        'rel_bias': nrm(ks[18], (NUM_BUCKETS, N_HEADS), 0.2),
        'ffn_w_up': nrm(ks[19], (DEPTH, D, 2 * F), D ** -0.5),
        'ffn_dw_w': nrm(ks[20], (DEPTH, FFN_KERNEL, 2 * F), FFN_KERNEL ** -0.5),
        'ffn_dw_b': nrm(ks[21], (DEPTH, 2 * F), 0.02),
        'ffn_w_down': nrm(ks[22], (DEPTH, F, D), F ** -0.5),
        'final_norm_g': 1.0 + nrm(ks[23], (D,), 0.02),
    }


def reference(x, mix_norm, ffn_norm, conv_w_in, conv_b_in, conv_dw_w, conv_dw_b, conv_ln_g,
              conv_ln_b, conv_w_out, conv_b_out, attn_w_qkv, attn_lambda_q1, attn_lambda_k1,
              attn_lambda_q2, attn_lambda_k2, attn_subln_g, attn_w_o, rel_bias, ffn_w_up,
              ffn_dw_w, ffn_dw_b, ffn_w_down, final_norm_g):
    for i in range(DEPTH):
        j = i // N_MIXERS
        h = _rmsnorm(x, mix_norm[i])
        if i % N_MIXERS == 0:
            m = _conformer_conv(h, conv_w_in[j], conv_b_in[j], conv_dw_w[j], conv_dw_b[j],
                                conv_ln_g[j], conv_ln_b[j], conv_w_out[j], conv_b_out[j])
        else:
            lambda_init = 0.8 - 0.6 * math.exp(-0.3 * i)
            m = _diff_attention(h, attn_w_qkv[j], attn_lambda_q1[j], attn_lambda_k1[j],
                                attn_lambda_q2[j], attn_lambda_k2[j], attn_subln_g[j],
                                attn_w_o[j], rel_bias, lambda_init)
        x = x + m
        h = _rmsnorm(x, ffn_norm[i])
        x = x + _conv_ffn(h, ffn_w_up[i], ffn_dw_w[i], ffn_dw_b[i], ffn_w_down[i])
    return _rmsnorm(x, final_norm_g)
```

```python
import math
import contextlib
import numpy as np
import ml_dtypes
import concourse.bass as bass
import concourse.mybir as mybir
from concourse.bass_utils import run_bass_kernel_spmd

F32 = mybir.dt.float32
BF16 = mybir.dt.bfloat16
AF = mybir.ActivationFunctionType
ALU = mybir.AluOpType
AX = mybir.AxisListType

D = 2048
S = 16384
NCORE = 8
TPC = S // NCORE
W = 512
NT = TPC // W
HALO = 34
KC = D // 128
FF = 5632
NHT = 2 * FF // 128
NPAIR = FF // 128
G = 4
PPG = NPAIR // G
CK = 31
EPS = 1e-6
NEG = -30000.0
ENGS = ("tensor", "vector", "scalar", "gpsimd", "sync")


class Buf:
    __slots__ = ("name", "writer", "readers")

    def __init__(self, name):
        self.name = name
        self.writer = None
        self.readers = []


class Op:
    __slots__ = ("eng", "fn", "deps", "signaled", "sigval", "is_dma", "dkey", "dcount")

    def __init__(self, eng, fn, is_dma=False, dkey=None):
        self.eng = eng
        self.fn = fn
        self.deps = []
        self.signaled = False
        self.sigval = 0
        self.is_dma = is_dma
        self.dkey = dkey
        self.dcount = 0


class Prog:
    def __init__(self, nc):
        self.nc = nc
        self.ops = {e: [] for e in ENGS}
        self.dma_counts = {}

    def _hazards(self, op, reads, writes):
        deps = []
        for b in reads:
            if b.writer is not None:
                deps.append(b.writer)
        for b in writes:
            if b.writer is not None:
                deps.append(b.writer)
            deps.extend(b.readers)
        seen = set()
        for d in deps:
            if d is op or id(d) in seen:
                continue
            seen.add(id(d))
            if (not d.is_dma) and (not op.is_dma) and d.eng == op.eng:
                if op.eng == "tensor" or not any(b.writer is d for b in reads):
                    continue
            op.deps.append(d)
            d.signaled = True
        for b in reads:
            b.readers.append(op)
        for b in writes:
            b.writer = op
            b.readers = []

    def add(self, eng, fn, reads=(), writes=()):
        op = Op(eng, fn)
        self._hazards(op, reads, writes)
        self.ops[eng].append(op)
        return op

    def dma(self, eng, pairs, key, reads=(), writes=()):
        def fn(e, pairs=pairs):
            return [e.dma_start(out=o, in_=i) for (o, i) in pairs]
        op = Op(eng, fn, is_dma=True, dkey=key)
        self._hazards(op, reads, writes)
        c = self.dma_counts.get(key, 0) + len(pairs)
        self.dma_counts[key] = c
        op.dcount = c
        self.ops[eng].append(op)
        return op

    def emit(self, final_waits=()):
        nc = self.nc
        EPOCH = 1500
        DEPOCH = 64
        with contextlib.ExitStack() as st:
            for d in final_waits:
                d.signaled = True
            nep = {}
            for e in ENGS:
                c = 0
                for op in self.ops[e]:
                    if op.signaled and not op.is_dma:
                        op.sigval = c
                        c += 1
                nep[e] = (c + EPOCH - 1) // EPOCH
            esem = {e: [st.enter_context(nc.semaphore("es_%s_%d" % (e, i))) for i in range(nep[e])] for e in ENGS}
            dsem = {k: [st.enter_context(nc.semaphore("ds_%s_%d" % (k, i)))
                        for i in range((n + DEPOCH - 1) // DEPOCH)] for k, n in self.dma_counts.items()}
            if _DBG.get("verbose"):
                print("epochs", nep, "nops", {e: len(self.ops[e]) for e in ENGS}, "dma", max(self.dma_counts.values()))
            block = st.enter_context(nc.Block())

            def run(e, eng):
                waited = {}

                def wait(d):
                    if d.is_dma:
                        ep = (d.dcount - 1) // DEPOCH
                        sem, val = dsem[d.dkey][ep], 16 * (d.dcount - ep * DEPOCH)
                    else:
                        ep = d.sigval // EPOCH
                        sem, val = esem[d.eng][ep], d.sigval - ep * EPOCH + 1
                    if waited.get(id(sem), 0) >= val:
                        return
                    waited[id(sem)] = val
                    eng.wait_ge(sem, val)

                for op in self.ops[e]:
                    for d in op.deps:
                        wait(d)
                    r = op.fn(eng)
                    if op.is_dma:
                        ep = (op.dcount - 1) // DEPOCH
                        for ins in r:
                            ins.then_inc(dsem[op.dkey][ep], 16)
                    elif op.signaled:
                        r.then_inc(esem[e][op.sigval // EPOCH], 1)
                if e == "sync":
                    for d in final_waits:
                        wait(d)

            block.tensor(lambda eng: run("tensor", eng))
            block.vector(lambda eng: run("vector", eng))
            block.scalar(lambda eng: run("scalar", eng))
            block.gpsimd(lambda eng: run("gpsimd", eng))
            block.sync(lambda eng: run("sync", eng))


class Ring:
    def __init__(self, items):
        self.items = items
        self.i = 0

    def next(self):
        it = self.items[self.i % len(self.items)]
        self.i += 1
        return it


def _cols(v):
    v = np.asarray(v, np.float32)
    return np.ascontiguousarray(v.reshape(-1, 128).T)


def _fm(seg):
    return np.ascontiguousarray(seg.T.reshape(KC, 128, seg.shape[0]).transpose(1, 0, 2))


def _tile_w(Wm, kc):
    K, M = Wm.shape
    return np.ascontiguousarray(Wm.reshape(kc, 128, M // 128, 128).transpose(2, 1, 0, 3))


class CstPack:
    def __init__(self):
        self.parts = []
        self.off = {}
        self.n = 0

    def put(self, name, arr):
        arr = np.asarray(arr, np.float32)
        assert arr.shape[0] == 128
        arr = arr.reshape(128, -1)
        self.off[name] = self.n
        self.n += arr.shape[1]
        self.parts.append(arr)

    def build(self):
        return np.ascontiguousarray(np.concatenate(self.parts, axis=1))


class Dense:
    def __init__(self, nc, P, st, cst_ap, ncst, coff):
        self.nc, self.P, self.st, self.coff = nc, P, st, coff
        sb = self.sb
        self.cst = sb("cst_sb", [128, ncst], F32)
        self.Bcst = Buf("cst")
        P.dma("sync", [(self.cst[:], cst_ap)], "cst", writes=[self.Bcst])
        self.ones = sb("ones", [128, 128], BF16)
        self.Bones = Buf("ones")
        P.add("vector", lambda e: e.memset(self.ones[:], 1.0), writes=[self.Bones])
        self.banks = Ring([(st.enter_context(nc.psum_tensor("pb%d" % i, [128, 512], F32)), Buf("pb%d" % i))
                           for i in range(8)])
        self.wslots = Ring([(sb("ws%d" % i, [128, KC, 128], BF16), Buf("ws%d" % i), "ws%d" % i) for i in range(_DBG.get("nws", 6))])
        self.wdslots = Ring([(sb("wd%d" % i, [128, PPG, 128], BF16), Buf("wd%d" % i), "wd%d" % i) for i in range(_DBG.get("nwd", 4))])
        self.x = sb("x", [128, KC, W], F32)
        self.Bx = [Buf("x%d" % j) for j in range(KC)]
        self.h = sb("h", [128, KC, W], BF16)
        self.Bh = Buf("h")
        self.sq = [(sb("sq%d" % i, [128, W], BF16), Buf("sq%d" % i)) for i in range(2)]
        self.rstd = sb("rstd", [128, W], F32)
        self.Brstd = Buf("rstd")
        self.tmp = [(sb("tmp%d" % i, [128, W], F32), Buf("tmp%d" % i)) for i in range(3)]
        self.hid = sb("hid", [128, PPG, W], BF16)
        self.Bhid = [Buf("hid%d" % j) for j in range(PPG)]
        self.eb = Ring([(sb("e%d" % i, [128, W + 2], F32), Buf("e%d" % i)) for i in range(4)])
        self.yb = Ring([(sb("y%d" % i, [128, W], F32), Buf("y%d" % i)) for i in range(4)])
        self.fst = sb("fst", [128, NHT, 2], F32)
        self.Bfst = [Buf("fst%d" % j) for j in range(NHT)]
        Bf = self.Bfst
        P.add("vector", lambda e: e.memset(self.fst[:], 0.0), writes=Bf)

    def sb(self, name, shape, dt):
        return self.st.enter_context(self.nc.sbuf_tensor(name, shape, dt))

    def c(self, name, j=0, n=1):
        o = self.coff[name] + j
        return self.cst[:, o:o + n]

    def load_w(self, dram_tile):
        slot, B, key = self.wslots.next()
        self.P.dma("gpsimd", [(slot[:], dram_tile)], key, writes=[B])
        return slot, B

    def load_wd(self, dram_tile):
        slot, B, key = self.wdslots.next()
        self.P.dma("gpsimd", [(slot[:], dram_tile)], key, writes=[B])
        return slot, B

    def mm_group(self, ps, wslot, act, nk, Wc, reads, Bps):
        def fn(e):
            ins = None
            for k in range(nk):
                ins = e.matmul(ps[:, 0:Wc], lhsT=wslot[:, k, :], rhs=act[:, k, 0:Wc], start=(k == 0), stop=(k == nk - 1))
            return ins
        return self.P.add("tensor", fn, reads=reads, writes=[Bps])

    def colsum_rstd(self, srcs, Wc, scale, out_ap, Bout, sqfunc=AF.Square, extra_reads=()):
        P = self.P
        ps, Bps = self.banks.next()
        n = len(srcs)
        for j, (src, Bsrc) in enumerate(srcs):
            sq, Bsq = self.sq[j % 2]
            P.add("scalar", lambda e, sq=sq, src=src: e.activation(out=sq[:, 0:Wc], in_=src, func=sqfunc),
                  reads=[Bsrc], writes=[Bsq])
            P.add("tensor", lambda e, sq=sq, j=j: e.matmul(ps[:, 0:Wc], lhsT=self.ones[:], rhs=sq[:, 0:Wc],
                                                          start=(j == 0), stop=(j == n - 1)),
                  reads=[Bsq, self.Bones], writes=[Bps])
        t, Bt = self.tmp[0]
        P.add("scalar", lambda e: e.activation(out=t[:, 0:Wc], in_=ps[:, 0:Wc], func=AF.Sqrt, bias=self.c("eps"),
                                               scale=scale),
              reads=[Bps, self.Bcst], writes=[Bt])
        P.add("vector", lambda e: e.reciprocal(out=out_ap, in_=t[:, 0:Wc]), reads=[Bt], writes=[Bout])

    def rmsnorm_to_h(self, gname, Wc):
        P = self.P
        self.colsum_rstd([(self.x[:, j, 0:Wc], self.Bx[j]) for j in range(KC)], Wc, 1.0 / D,
                         self.rstd[:, 0:Wc], self.Brstd)
        for j in range(KC):
            P.add("vector", lambda e, j=j: e.scalar_tensor_tensor(
                out=self.h[:, j, 0:Wc], in0=self.x[:, j, 0:Wc], scalar=self.c(gname, j), in1=self.rstd[:, 0:Wc],
                op0=ALU.mult, op1=ALU.mult), reads=[self.Bx[j], self.Brstd, self.Bcst], writes=[self.Bh])

    def ffn(self, layer, Wc, w_up, w_dn, gname, flag_state):
        P = self.P
        self.rmsnorm_to_h(gname, Wc)
        dww, dwb = "fdw%d" % layer, "fdb%d" % layer
        for g in range(G):
            for jj in range(PPG):
                pj = g * PPG + jj
                ys = []
                for half in range(2):
                    t = pj + half * NPAIR
                    ws, Bw = self.load_w(w_up[0 if _DBG.get("tinyw") else t])
                    ps, Bps = self.banks.next()
                    self.mm_group(ps, ws, self.h, KC, Wc, [Bw, self.Bh], Bps)
                    eb, Be = self.eb.next()
                    yb, By = self.yb.next()
                    st_ap = self.fst[:, t, :]
                    P.add("vector", lambda e, eb=eb, st_ap=st_ap: e.tensor_copy(out=eb[:, 0:2], in_=st_ap),
                          reads=[self.Bfst[t]], writes=[Be])
                    P.add("scalar", lambda e, eb=eb, ps=ps: e.activation(out=eb[:, 2:2 + Wc], in_=ps[:, 0:Wc],
                                                                         func=AF.Copy),
                          reads=[Bps], writes=[Be])
                    if flag_state:
                        P.add("vector", lambda e, eb=eb, st_ap=st_ap: e.tensor_scalar(
                            out=st_ap, in0=eb[:, Wc:Wc + 2], scalar1=self.c("flag"), scalar2=None, op0=ALU.mult),
                            reads=[Be, self.Bcst], writes=[self.Bfst[t]])
                    else:
                        P.add("vector", lambda e, eb=eb, st_ap=st_ap: e.tensor_copy(out=st_ap, in_=eb[:, Wc:Wc + 2]),
                              reads=[Be], writes=[self.Bfst[t]])
                    P.add("vector", lambda e, eb=eb, yb=yb, t=t: e.tensor_scalar(
                        out=yb[:, 0:Wc], in0=eb[:, 2:2 + Wc], scalar1=self.c(dww, 3 * t + 2), scalar2=self.c(dwb, t),
                        op0=ALU.mult, op1=ALU.add), reads=[Be, self.Bcst], writes=[By])
                    for tap in (1, 0):
                        P.add("vector", lambda e, eb=eb, yb=yb, t=t, tap=tap: e.scalar_tensor_tensor(
                            out=yb[:, 0:Wc], in0=eb[:, tap:tap + Wc], scalar=self.c(dww, 3 * t + tap), in1=yb[:, 0:Wc],
                            op0=ALU.mult, op1=ALU.add), reads=[Be, By, self.Bcst], writes=[By])
                    ys.append((yb, By))
                (yg, Byg), (yv, Byv) = ys
                P.add("scalar", lambda e, yg=yg: e.activation(out=yg[:, 0:Wc], in_=yg[:, 0:Wc], func=AF.Silu),
                      reads=[Byg], writes=[Byg])
                P.add("vector", lambda e, yg=yg, yv=yv, jj=jj: e.tensor_tensor(
                    out=self.hid[:, jj, 0:Wc], in0=yg[:, 0:Wc], in1=yv[:, 0:Wc], op=ALU.mult),
                    reads=[Byg, Byv], writes=[self.Bhid[jj]])
            for m in range(0 if (_DBG.get("ffn_mode", 0) & 4) else KC):
                wd, Bwd = self.load_wd(w_dn[0, 0] if _DBG.get("tinyw") else w_dn[g, m])
                ps, Bps = self.banks.next()
                self.mm_group(ps, wd, self.hid, PPG, Wc, [Bwd] + self.Bhid, Bps)
                P.add("vector", lambda e, ps=ps, m=m: e.tensor_tensor(
                    out=self.x[:, m, 0:Wc], in0=ps[:, 0:Wc], in1=self.x[:, m, 0:Wc], op=ALU.add),
                    reads=[Bps, self.Bx[m]], writes=[self.Bx[m]])


def _cst_common(cp, inputs):
    cp.put("eps", np.full((128, 1), EPS, np.float32))
    for L in range(2):
        cp.put("fdw%d" % L, np.ascontiguousarray(
            np.asarray(inputs["ffn_dw_w"][L], np.float32).T.reshape(NHT, 128, 3).transpose(1, 0, 2)))
        cp.put("fdb%d" % L, _cols(inputs["ffn_dw_b"][L]))
        cp.put("fng%d" % L, _cols(inputs["ffn_norm"][L]))
        cp.put("mng%d" % L, _cols(inputs["mix_norm"][L]))
    cp.put("fin", _cols(inputs["final_norm_g"]))


def build_A(ncst, coff):
    nc = bass.Bass("TRN2", target_bir_lowering=False)
    xh = nc.dram_tensor("xh", [128, KC, HALO], F32, kind="ExternalInput").ap()
    xt = nc.dram_tensor("xt", [NT, 128, KC, W], F32, kind="ExternalInput").ap()
    cst = nc.dram_tensor("cst", [128, ncst], F32, kind="ExternalInput").ap()
    tw = bool(_DBG.get("tinyw"))
    w_in = nc.dram_tensor("w_in", [1 if tw else 32, 128, KC, 128], F32, kind="ExternalInput").ap()
    w_out = nc.dram_tensor("w_out", [1 if tw else 16, 128, KC, 128], F32, kind="ExternalInput").ap()
    w_up = nc.dram_tensor("w_up", [1 if tw else NHT, 128, KC, 128], F32, kind="ExternalInput").ap()
    w_dn = nc.dram_tensor("w_dn", [1 if tw else G, 1 if tw else 16, 128, PPG, 128], F32, kind="ExternalInput").ap()
    x1h = nc.dram_tensor("x1h", [128, KC, 2], F32, kind="ExternalOutput").ap()
    x1t = nc.dram_tensor("x1t", [NT, 128, KC, W], F32, kind="ExternalOutput").ap()
    h1t = nc.dram_tensor("h1t", [NT, 128, KC, W], BF16, kind="ExternalOutput").ap()
    P = Prog(nc)
    finals = []
    with contextlib.ExitStack() as st:
        dn = Dense(nc, P, st, cst, ncst, coff)
        sb = dn.sb
        uext = sb("uext", [128, KC, W + 30], BF16)
        Bu = [Buf("u%d" % j) for j in range(KC)]
        cv = sb("cv", [128, KC, W], F32)
        Bcv = [Buf("cv%d" % j) for j in range(KC)]
        sg = [(sb("sg%d" % i, [128, W], F32), Buf("sg%d" % i)) for i in range(2)]
        mu = sb("mu", [128, W], F32)
        Bmu = Buf("mu")
        nmr = sb("nmr", [128, W], F32)
        Bnmr = Buf("nmr")
        h1s, Bh1s = dn.h, dn.Bh
        P.add("vector", lambda e: e.memset(uext[:, :, 0:30], 0.0), writes=Bu)
        hst = sb("hst", [128, KC, HALO], F32)
        Bhst = Buf("hst")
        hso = sb("hso", [128, KC, 2], F32)
        Bhso = Buf("hso")

        tiles = [(0, HALO, True)] + [(HALO + i * W, W, False) for i in range(NT)]
        tiles = tiles[:_DBG.get("tiles", 99)]
        stage = _DBG.get("stage", 99)
        def tile_body(off, Wc, is_halo):
            ti = (off - HALO) // W
            if is_halo:
                P.dma("sync", [(hst[:], xh)], "xhin", writes=[Bhst])
                P.add("vector", lambda e: e.tensor_copy(out=dn.x[:, :, 0:Wc], in_=hst[:]), reads=[Bhst],
                      writes=dn.Bx)
            else:
                P.dma("sync", [(dn.x[:], xt[ti])], "xin", writes=dn.Bx)
            dn.rmsnorm_to_h("mng0", Wc)
            for m in range(KC):
                wa, Bwa = dn.load_w(w_in[0 if tw else m])
                pa, Bpa = dn.banks.next()
                dn.mm_group(pa, wa, dn.h, KC, Wc, [Bwa, dn.Bh], Bpa)
                wg, Bwg = dn.load_w(w_in[0 if tw else m + KC])
                pg, Bpg = dn.banks.next()
                dn.mm_group(pg, wg, dn.h, KC, Wc, [Bwg, dn.Bh], Bpg)
                s, Bs = sg[m % 2]
                P.add("scalar", lambda e, s=s, pg=pg, m=m: e.activation(
                    out=s[:, 0:Wc], in_=pg[:, 0:Wc], func=AF.Sigmoid, bias=dn.c("b_in", KC + m)),
                    reads=[Bpg, dn.Bcst], writes=[Bs])
                P.add("vector", lambda e, s=s, pa=pa, m=m: e.scalar_tensor_tensor(
                    out=uext[:, m, 30:30 + Wc], in0=pa[:, 0:Wc], scalar=dn.c("b_in", m), in1=s[:, 0:Wc],
                    op0=ALU.add, op1=ALU.mult), reads=[Bpa, Bs, dn.Bcst], writes=[Bu[m]])
            for m in range(KC if stage >= 2 else 0):
                P.add("vector", lambda e, m=m: e.tensor_scalar(
                    out=cv[:, m, 0:Wc], in0=uext[:, m, 0:Wc], scalar1=dn.c("cdw", m * CK), scalar2=dn.c("cdb", m),
                    op0=ALU.mult, op1=ALU.add), reads=[Bu[m], dn.Bcst], writes=[Bcv[m]])
                for tap in range(1, CK):
                    P.add("vector", lambda e, m=m, tap=tap: e.scalar_tensor_tensor(
                        out=cv[:, m, 0:Wc], in0=uext[:, m, tap:tap + Wc], scalar=dn.c("cdw", m * CK + tap),
                        in1=cv[:, m, 0:Wc], op0=ALU.mult, op1=ALU.add),
                        reads=[Bu[m], Bcv[m], dn.Bcst], writes=[Bcv[m]])
            if is_halo:
                P.add("vector", lambda e: e.tensor_scalar(
                    out=uext[:, :, 0:30], in0=uext[:, :, Wc:Wc + 30], scalar1=dn.c("flag"), scalar2=None,
                    op0=ALU.mult), reads=Bu + [dn.Bcst], writes=Bu)
            else:
                P.add("vector", lambda e: e.tensor_copy(out=uext[:, :, 0:30], in_=uext[:, :, Wc:Wc + 30]),
                      reads=Bu, writes=Bu)
            ps_s, Bps_s = dn.banks.next()
            ps_q, Bps_q = dn.banks.next()
            for j in range(KC):
                sq, Bsq = dn.sq[0]
                cb, Bcb = dn.sq[1]
                P.add("scalar", lambda e, j=j, cb=cb: e.activation(out=cb[:, 0:Wc], in_=cv[:, j, 0:Wc], func=AF.Copy),
                      reads=[Bcv[j]], writes=[Bcb])
                P.add("scalar", lambda e, j=j, sq=sq: e.activation(out=sq[:, 0:Wc], in_=cv[:, j, 0:Wc],
                                                                   func=AF.Square),
                      reads=[Bcv[j]], writes=[Bsq])
                P.add("tensor", lambda e, j=j, cb=cb: e.matmul(ps_s[:, 0:Wc], lhsT=dn.ones[:], rhs=cb[:, 0:Wc],
                                                              start=(j == 0), stop=(j == KC - 1)),
                      reads=[Bcb, dn.Bones], writes=[Bps_s])
                P.add("tensor", lambda e, j=j, sq=sq: e.matmul(ps_q[:, 0:Wc], lhsT=dn.ones[:], rhs=sq[:, 0:Wc],
                                                              start=(j == 0), stop=(j == KC - 1)),
                      reads=[Bsq, dn.Bones], writes=[Bps_q])
            t1, Bt1 = dn.tmp[1]
            t2, Bt2 = dn.tmp[2]
            t0, Bt0 = dn.tmp[0]
            P.add("vector", lambda e: e.tensor_scalar(out=mu[:, 0:Wc], in0=ps_s[:, 0:Wc], scalar1=1.0 / D, scalar2=None,
                                                      op0=ALU.mult), reads=[Bps_s], writes=[Bmu])
            P.add("vector", lambda e: e.tensor_tensor(out=t1[:, 0:Wc], in0=mu[:, 0:Wc], in1=mu[:, 0:Wc], op=ALU.mult),
                  reads=[Bmu], writes=[Bt1])
            P.add("vector", lambda e: e.scalar_tensor_tensor(out=t2[:, 0:Wc], in0=ps_q[:, 0:Wc], scalar=1.0 / D,
                                                             in1=t1[:, 0:Wc], op0=ALU.mult, op1=ALU.subtract),
                  reads=[Bps_q, Bt1], writes=[Bt2])
            P.add("scalar", lambda e: e.activation(out=t0[:, 0:Wc], in_=t2[:, 0:Wc], func=AF.Sqrt, bias=dn.c("eps"),
                                                   scale=1.0), reads=[Bt2, dn.Bcst], writes=[Bt0])
            P.add("vector", lambda e: e.reciprocal(out=dn.rstd[:, 0:Wc], in_=t0[:, 0:Wc]), reads=[Bt0],
                  writes=[dn.Brstd])
            P.add("vector", lambda e: e.scalar_tensor_tensor(out=nmr[:, 0:Wc], in0=mu[:, 0:Wc], scalar=-1.0,
                                                             in1=dn.rstd[:, 0:Wc], op0=ALU.mult, op1=ALU.mult),
                  reads=[Bmu, dn.Brstd], writes=[Bnmr])
            for j in range(KC):
                P.add("vector", lambda e, j=j: e.tensor_tensor(out=cv[:, j, 0:Wc], in0=cv[:, j, 0:Wc],
                                                               in1=dn.rstd[:, 0:Wc], op=ALU.mult),
                      reads=[Bcv[j], dn.Brstd], writes=[Bcv[j]])
                P.add("vector", lambda e, j=j: e.tensor_tensor(out=cv[:, j, 0:Wc], in0=cv[:, j, 0:Wc],
                                                               in1=nmr[:, 0:Wc], op=ALU.add),
                      reads=[Bcv[j], Bnmr], writes=[Bcv[j]])
                P.add("scalar", lambda e, j=j: e.activation(out=dn.h[:, j, 0:Wc], in_=cv[:, j, 0:Wc], func=AF.Silu,
                                                            bias=dn.c("lnb", j), scale=dn.c("lng", j)),
                      reads=[Bcv[j], dn.Bcst], writes=[dn.Bh])
            for m in range(KC):
                ws, Bw = dn.load_w(w_out[0 if tw else m])
                ps, Bps = dn.banks.next()
                dn.mm_group(ps, ws, dn.h, KC, Wc, [Bw, dn.Bh], Bps)
                P.add("vector", lambda e, ps=ps, m=m: e.scalar_tensor_tensor(
                    out=dn.x[:, m, 0:Wc], in0=ps[:, 0:Wc], scalar=dn.c("b_out", m), in1=dn.x[:, m, 0:Wc],
                    op0=ALU.add, op1=ALU.add), reads=[Bps, dn.Bx[m], dn.Bcst], writes=[dn.Bx[m]])
            if stage >= 5:
                dn.ffn(0, Wc, w_up, w_dn, "fng0", is_halo)
            if is_halo:
                P.add("vector", lambda e: e.tensor_copy(out=hso[:], in_=dn.x[:, :, Wc - 2:Wc]), reads=dn.Bx,
                      writes=[Bhso])
                finals.append(P.dma("sync", [(x1h, hso[:])], "x1ho", reads=[Bhso]))
            else:
                finals.append(P.dma("sync", [(x1t[ti], dn.x[:])], "x1o", reads=dn.Bx))
                dn.colsum_rstd([(dn.x[:, j, 0:Wc], dn.Bx[j]) for j in range(KC)], Wc, 1.0 / D,
                               dn.rstd[:, 0:Wc], dn.Brstd)
                for j in range(KC):
                    P.add("vector", lambda e, j=j: e.scalar_tensor_tensor(
                        out=h1s[:, j, 0:Wc], in0=dn.x[:, j, 0:Wc], scalar=dn.c("mng1", j), in1=dn.rstd[:, 0:Wc],
                        op0=ALU.mult, op1=ALU.mult), reads=[dn.Bx[j], dn.Brstd, dn.Bcst], writes=[Bh1s])
                finals.append(P.dma("sync", [(h1t[ti], h1s[:])], "h1o", reads=[Bh1s]))
        for (off_, Wc_, ih_) in tiles:
            tile_body(off_, Wc_, ih_)
        P.emit(final_waits=finals)
    return nc


def _bucket_planes():
    kk = np.arange(128)[:, None]
    m = np.arange(1024)[None, :]
    rel = kk - m + 384
    allowed = (kk // 64) <= (m // 64 - 6)
    n = np.abs(rel)
    nf = np.maximum(n, 1).astype(np.float32)
    large = 8 + (np.log(nf / np.float32(8)) / np.float32(math.log(16.0)) * np.float32(8)).astype(np.int32)
    large = np.minimum(large, 15)
    bucket = np.where(rel > 0, 16, 0) + np.where(n < 8, n, large)
    used = sorted(set(bucket[allowed].tolist()))
    planes = np.zeros((len(used) + 1, 128, 1024), np.float32)
    planes[0] = np.where(allowed, 0.0, NEG)
    for i, b in enumerate(used):
        planes[i + 1] = ((bucket == b) & allowed).astype(np.float32)
    return used, planes


def build_B(nB, boff, used, lambda_init):
    nc = bass.Bass("TRN2", target_bir_lowering=False)
    hall = nc.dram_tensor("hall", [NCORE * NT, 128, KC, W], BF16, kind="ExternalInput").ap()
    cstb = nc.dram_tensor("cstb", [128, nB], F32, kind="ExternalInput").ap()
    wqkv = nc.dram_tensor("wqkv", [2, 3, 128, KC, 128], F32, kind="ExternalInput").ap()
    planes = nc.dram_tensor("planes", [len(used) + 1, 128, 1024], F32, kind="ExternalInput").ap()
    oT = nc.dram_tensor("oT", [2, 128, S], BF16, kind="ExternalOutput").ap()
    P = Prog(nc)
    finals = []
    NQ = S // W
    with contextlib.ExitStack() as st:
        def sb(name, shape, dt):
            return st.enter_context(nc.sbuf_tensor(name, shape, dt))
        cst = sb("cst", [128, nB], F32)
        Bcst = Buf("cst")
        P.dma("sync", [(cst[:], cstb)], "cst", writes=[Bcst])

        def c(name, j=0, n=1):
            o = boff[name] + j
            return cst[:, o:o + n]
        ones = sb("ones", [128, 128], BF16)
        Bones = Buf("ones")
        P.add("vector", lambda e: e.memset(ones[:], 1.0), writes=[Bones])
        spairs = Ring([(st.enter_context(nc.psum_tensor("sp%d" % i, [128, 2 * W], F32)), Buf("sp%d" % i))
                       for i in range(2)])
        obank = [(st.enter_context(nc.psum_tensor("ob%d" % i, [128, 512], F32)), Buf("ob%d" % i)) for i in range(2)]
        zbank = [(st.enter_context(nc.psum_tensor("zb%d" % i, [128, 512], F32)), Buf("zb%d" % i)) for i in range(2)]
        Kt = sb("Kt", [128, S], BF16)
        BK = [Buf("K%d" % i) for i in range(NQ)]
        Vt = sb("Vt", [128, S // 128, 128], BF16)
        BV = [Buf("V%d" % i) for i in range(NQ)]
        qts = Ring([(sb("qt%d" % i, [128, W], BF16), Buf("qt%d" % i)) for i in range(2)])
        hts = Ring([(sb("ht%d" % i, [128, KC, W], BF16), Buf("ht%d" % i), "ht%d" % i) for i in range(2)])
        wq = [(sb("wq%d" % i, [128, KC, 128], BF16), Buf("wq%d" % i)) for i in range(3)]
        strip = sb("strip", [128, 1024], F32)
        Bstrip = Buf("strip")
        pls = Ring([(sb("pl%d" % i, [128, 1024], F32), Buf("pl%d" % i), "pl%d" % i) for i in range(2)])
        pts = Ring([(sb("pt%d" % i, [128, 2 * W], BF16), Buf("pt%d" % i)) for i in range(3)])
        tns = Ring([(sb("tn%d" % i, [128, 2 * W], F32), Buf("tn%d" % i)) for i in range(2)])
        ep = [(sb("ep%d" % i, [128, W], F32), Buf("ep%d" % i)) for i in range(5)]
        sqb = sb("sqb", [128, W], BF16)
        Bsqb = Buf("sqb")
        ons = Ring([(sb("on%d" % i, [128, W], BF16), Buf("on%d" % i), "on%d" % i) for i in range(2)])
        zacc = [(sb("zacc%d" % i, [128, W], F32), Buf("zacc%d" % i)) for i in range(2)]
        ones32 = sb("ones32", [128, 128], F32)
        Bones32 = Buf("ones32")
        P.add("vector", lambda e: e.memset(ones32[:], 1.0), writes=[Bones32])
        lamt = sb("lamt", [128, 8], F32)
        Blam = Buf("lam")
        lprod = sb("lprod", [128, 2, 64], F32)
        Blprod = Buf("lprod")
        P.add("vector", lambda e: e.tensor_tensor(out=lprod[:, 0, :], in0=c("lam", 0, 64), in1=c("lam", 64, 64),
                                                  op=ALU.mult), reads=[Bcst], writes=[Blprod])
        P.add("vector", lambda e: e.tensor_tensor(out=lprod[:, 1, :], in0=c("lam", 128, 64), in1=c("lam", 192, 64),
                                                  op=ALU.mult), reads=[Bcst], writes=[Blprod])
        P.add("vector", lambda e: e.tensor_reduce(out=lamt[:, 0:2], in_=lprod[:], axis=AX.X, op=ALU.add),
              reads=[Blprod], writes=[Blam])
        P.add("scalar", lambda e: e.activation(out=lamt[:, 2:4], in_=lamt[:, 0:2], func=AF.Exp), reads=[Blam],
              writes=[Blam])
        P.add("vector", lambda e: e.scalar_tensor_tensor(out=lamt[:, 4:5], in0=lamt[:, 3:4], scalar=-lambda_init,
                                                         in1=lamt[:, 2:3], op0=ALU.add, op1=ALU.subtract),
              reads=[Blam], writes=[Blam])

        def head_body(hd):
            for i in range(3):
                P.dma("gpsimd", [(wq[i][0][:], wqkv[hd, i])], "wq%d" % i, writes=[wq[i][1]])
            pl, Bpl, kpl = pls.next()
            P.dma("sync", [(pl[:], planes[0])], kpl, writes=[Bpl])
            P.add("vector", lambda e, pl=pl: e.tensor_copy(out=strip[:], in_=pl[:]), reads=[Bpl], writes=[Bstrip])
            for bi, b in enumerate(used):
                pl, Bpl, kpl = pls.next()
                P.dma("sync", [(pl[:], planes[bi + 1])], kpl, writes=[Bpl])
                P.add("vector", lambda e, pl=pl, b=b: e.scalar_tensor_tensor(
                    out=strip[:], in0=pl[:], scalar=c("tab", hd * 32 + b), in1=strip[:], op0=ALU.mult, op1=ALU.add),
                    reads=[Bpl, Bstrip, Bcst], writes=[Bstrip])
            def proj(i):
                ht, Bht, kht = hts.next()
                P.dma("sync", [(ht[:], hall[i])], kht, writes=[Bht])
                pp, Bpp = spairs.next()
                qt, Bqt = qts.next()

                def kqmm(e, pp=pp, ht=ht):
                    ins = None
                    for k in range(KC):
                        ins = e.matmul(pp[:, 0:W], lhsT=wq[1][0][:, k, :], rhs=ht[:, k, :], start=(k == 0),
                                       stop=(k == KC - 1))
                    for k in range(KC):
                        ins = e.matmul(pp[:, W:2 * W], lhsT=wq[0][0][:, k, :], rhs=ht[:, k, :], start=(k == 0),
                                       stop=(k == KC - 1))
                    return ins
                P.add("tensor", kqmm, reads=[wq[1][1], wq[0][1], Bht], writes=[Bpp])
                P.add("scalar", lambda e, pp=pp, i=i: e.activation(out=Kt[:, i * W:(i + 1) * W], in_=pp[:, 0:W],
                                                                   func=AF.Copy),
                      reads=[Bpp], writes=[BK[i]])
                P.add("scalar", lambda e, pp=pp, qt=qt: e.activation(out=qt[:], in_=pp[:, W:2 * W], func=AF.Copy,
                                                                     scale=0.125),
                      reads=[Bpp], writes=[Bqt])
                ps, Bps = spairs.next()

                def vmm(e, ps=ps, ht=ht):
                    ins = None
                    for sub in range(4):
                        for k in range(KC):
                            ins = e.matmul(ps[:, sub * 128:(sub + 1) * 128], lhsT=ht[:, k, sub * 128:(sub + 1) * 128],
                                           rhs=wq[2][0][:, k, :], start=(k == 0), stop=(k == KC - 1))
                    return ins
                P.add("tensor", vmm, reads=[wq[2][1], Bht], writes=[Bps])
                P.add("vector", lambda e, ps=ps, i=i: e.tensor_copy(
                    out=Vt[:, 4 * i:4 * i + 4, :], in_=ps[:, 0:W].rearrange("p (a b) -> p a b", a=4)),
                    reads=[Bps], writes=[BV[i]])
                return qt, Bqt

            def scores(i, j, qt, Bqt):
                d = 128 * j - W * i
                c0 = max(d, 0)
                pp, Bpp = spairs.next()

                def smm(e, pp=pp, j=j, qt=qt, c0=c0):
                    ins = None
                    for comp in range(2):
                        lo = 64 * comp
                        ins = e.matmul(pp[:, comp * W + c0:(comp + 1) * W], lhsT=Kt[lo:lo + 64, j * 128:(j + 1) * 128],
                                       rhs=qt[lo:lo + 64, c0:W], start=True, stop=True)
                    return ins
                P.add("tensor", smm, reads=[BK[j // 4], Bqt], writes=[Bpp])
                return pp, Bpp

            def expo(i, j, sc):
                d = 128 * j - W * i
                c0 = max(d, 0)
                near = j >= 4 * i - 1
                pp, Bpp = sc
                pt, Bpt = pts.next()
                if near:
                    tn, Btn = tns.next()
                    so = 384 - d
                    for comp in range(2):
                        P.add("vector", lambda e, pp=pp, tn=tn, so=so, c0=c0, comp=comp: e.tensor_tensor(
                            out=tn[:, comp * W + c0:(comp + 1) * W], in0=pp[:, comp * W + c0:(comp + 1) * W],
                            in1=strip[:, so + c0:so + W], op=ALU.add), reads=[Bpp, Bstrip], writes=[Btn])
                    for comp in range(2):
                        P.add("scalar", lambda e, tn=tn, pt=pt, c0=c0, comp=comp: e.activation(
                            out=pt[:, comp * W + c0:(comp + 1) * W], in_=tn[:, comp * W + c0:(comp + 1) * W],
                            func=AF.Exp), reads=[Btn], writes=[Bpt])
                else:
                    P.add("scalar", lambda e, pp=pp, pt=pt: e.activation(
                        out=pt[:], in_=pp[:], func=AF.Exp, bias=c("farb", hd)), reads=[Bpp, Bcst], writes=[Bpt])
                return pt, Bpt

            def pv(i, j, pcs):
                d = 128 * j - W * i
                c0 = max(d, 0)
                nk = 4 * (i + 1)
                ptf, Bpt = pcs
                for comp in range(2):
                    pt = ptf[:, comp * W:(comp + 1) * W]
                    ob, Bob = obank[comp]
                    P.add("tensor", lambda e, pt=pt, ob=ob, j=j, c0=c0, nk=nk: e.matmul(
                        ob[:, c0:W], lhsT=Vt[:, j, :], rhs=pt[:, c0:W], start=(j == 0), stop=(j == nk - 1)),
                        reads=[BV[j // 4], Bpt], writes=[Bob])
                    if comp == 0:
                        za, Bza = zacc[0]
                        if j == 0:
                            P.add("vector", lambda e, pt=pt, za=za: e.tensor_copy(out=za[:], in_=pt[:, 0:W]),
                                  reads=[Bpt], writes=[Bza])
                        else:
                            P.add("vector", lambda e, pt=pt, za=za, c0=c0: e.tensor_tensor(
                                out=za[:, c0:W], in0=za[:, c0:W], in1=pt[:, c0:W], op=ALU.add),
                                reads=[Bpt, Bza], writes=[Bza])
                    else:
                        zb, Bzb = zbank[1]
                        P.add("tensor", lambda e, pt=pt, zb=zb, j=j, c0=c0, nk=nk: e.matmul(
                            zb[:, c0:W], lhsT=ones[:], rhs=pt[:, c0:W], start=(j == 0), stop=(j == nk - 1)),
                            reads=[Bones, Bpt], writes=[Bzb])

            (r0, Br0), (r1, Br1), (t0, Bt0), (t1, Bt1), (oo, Boo) = ep

            def epi1():
                for comp in range(1):
                    za, Bza = zacc[comp]
                    zb, Bzb = zbank[comp]
                    P.add("tensor", lambda e, za=za, zb=zb: e.matmul(zb[:], lhsT=ones32[:], rhs=za[:], start=True,
                                                                    stop=True),
                          reads=[Bones32, Bza], writes=[Bzb])
                P.add("vector", lambda e: e.reciprocal(out=r0[:], in_=zbank[0][0][:]), reads=[zbank[0][1]],
                      writes=[Br0])
                P.add("vector", lambda e: e.reciprocal(out=r1[:], in_=zbank[1][0][:]), reads=[zbank[1][1]],
                      writes=[Br1])
                P.add("vector", lambda e: e.tensor_tensor(out=t0[:], in0=obank[0][0][:], in1=r0[:], op=ALU.mult),
                      reads=[obank[0][1], Br0], writes=[Bt0])
                P.add("vector", lambda e: e.tensor_tensor(out=t1[:], in0=obank[1][0][:], in1=r1[:], op=ALU.mult),
                      reads=[obank[1][1], Br1], writes=[Bt1])
                P.add("vector", lambda e: e.scalar_tensor_tensor(out=oo[:], in0=t1[:], scalar=lamt[:, 4:5], in1=t0[:],
                                                                 op0=ALU.mult, op1=ALU.add),
                      reads=[Bt0, Bt1, Blam], writes=[Boo])
                P.add("scalar", lambda e: e.activation(out=sqb[:], in_=oo[:], func=AF.Square), reads=[Boo],
                      writes=[Bsqb])

            def epi2(i):
                ps, Bps = spairs.next()
                P.add("tensor", lambda e, ps=ps: e.matmul(ps[:, 0:W], lhsT=ones[:], rhs=sqb[:], start=True, stop=True),
                      reads=[Bones, Bsqb], writes=[Bps])
                P.add("scalar", lambda e, ps=ps: e.activation(out=r0[:], in_=ps[:, 0:W], func=AF.Sqrt, bias=c("eps2"),
                                                              scale=1.0 / (128 * (1.0 - lambda_init) ** 2)),
                      reads=[Bps, Bcst], writes=[Br0])
                P.add("vector", lambda e: e.reciprocal(out=r1[:], in_=r0[:]), reads=[Br0], writes=[Br1])
                on, Bon, kon = ons.next()
                P.add("vector", lambda e, on=on: e.scalar_tensor_tensor(out=on[:], in0=oo[:], scalar=c("subg"),
                                                                        in1=r1[:], op0=ALU.mult, op1=ALU.mult),
                      reads=[Boo, Br1, Bcst], writes=[Bon])
                store_o(hd, i, on, Bon, kon)

            cur = proj(0)
            for i in range(NQ):
                qt, Bqt = cur
                nk = 4 * (i + 1)
                sc = scores(i, 0, qt, Bqt)
                for j in range(nk):
                    sc_next = scores(i, j + 1, qt, Bqt) if j + 1 < nk else None
                    pcs = expo(i, j, sc)
                    pv(i, j, pcs)
                    sc = sc_next
                epi1()
                if i + 1 < NQ:
                    cur = proj(i + 1)
                epi2(i)

        def store_o(hd, i, on, Bon, kon):
            finals.append(P.dma("sync", [(oT[hd, :, i * W:(i + 1) * W], on[:])], kon + "o", reads=[Bon]))

        for hd_ in range(2):
            head_body(hd_)
        P.emit(final_waits=finals)
    return nc


def build_C(ncst, coff):
    nc = bass.Bass("TRN2", target_bir_lowering=False)
    x1h = nc.dram_tensor("x1h", [128, KC, 2], F32, kind="ExternalInput").ap()
    x1t = nc.dram_tensor("x1t", [NT, 128, KC, W], F32, kind="ExternalInput").ap()
    oh = nc.dram_tensor("oh", [128, KC, 2], BF16, kind="ExternalInput").ap()
    ot = nc.dram_tensor("ot", [NT, 128, KC, W], BF16, kind="ExternalInput").ap()
    cst = nc.dram_tensor("cst", [128, ncst], F32, kind="ExternalInput").ap()
    w_o = nc.dram_tensor("w_o", [16, 128, KC, 128], F32, kind="ExternalInput").ap()
    w_up = nc.dram_tensor("w_up", [NHT, 128, KC, 128], F32, kind="ExternalInput").ap()
    w_dn = nc.dram_tensor("w_dn", [G, 16, 128, PPG, 128], F32, kind="ExternalInput").ap()
    yt = nc.dram_tensor("yt", [NT, 128, KC, W], F32, kind="ExternalOutput").ap()
    P = Prog(nc)
    finals = []
    with contextlib.ExitStack() as st:
        dn = Dense(nc, P, st, cst, ncst, coff)
        sb = dn.sb
        osb = sb("osb", [128, KC, W], BF16)
        Bo = Buf("osb")
        ys = sb("ys", [128, KC, W], F32)
        Bys = Buf("ys")
        hsx = sb("hsx", [128, KC, 2], F32)
        Bhsx = Buf("hsx")
        hsoo = sb("hsoo", [128, KC, 2], BF16)
        Bhsoo = Buf("hsoo")
        tiles = [(0, 2, True)] + [(2 + i * W, W, False) for i in range(NT)]
        def tile_body(off, Wc, is_halo):
            ti = (off - 2) // W
            if is_halo:
                P.dma("sync", [(hsx[:], x1h)], "xhin", writes=[Bhsx])
                P.dma("sync", [(hsoo[:], oh)], "ohin", writes=[Bhsoo])
                P.add("vector", lambda e: e.tensor_copy(out=dn.x[:, :, 0:2], in_=hsx[:]), reads=[Bhsx], writes=dn.Bx)
                P.add("vector", lambda e: e.tensor_copy(out=osb[:, :, 0:2], in_=hsoo[:]), reads=[Bhsoo], writes=[Bo])
            else:
                P.dma("sync", [(dn.x[:], x1t[ti])], "xin", writes=dn.Bx)
                P.dma("sync", [(osb[:], ot[ti])], "oin", writes=[Bo])
            for m in range(KC):
                ws, Bw = dn.load_w(w_o[m])
                ps, Bps = dn.banks.next()
                dn.mm_group(ps, ws, osb, KC, Wc, [Bw, Bo], Bps)
                P.add("vector", lambda e, ps=ps, m=m: e.tensor_tensor(
                    out=dn.x[:, m, 0:Wc], in0=ps[:, 0:Wc], in1=dn.x[:, m, 0:Wc], op=ALU.add),
                    reads=[Bps, dn.Bx[m]], writes=[dn.Bx[m]])
            dn.ffn(1, Wc, w_up, w_dn, "fng1", is_halo)
            if not is_halo:
                dn.colsum_rstd([(dn.x[:, j, 0:Wc], dn.Bx[j]) for j in range(KC)], Wc, 1.0 / D,
                               dn.rstd[:, 0:Wc], dn.Brstd)
                for j in range(KC):
                    P.add("vector", lambda e, j=j: e.scalar_tensor_tensor(
                        out=ys[:, j, 0:Wc], in0=dn.x[:, j, 0:Wc], scalar=dn.c("fin", j), in1=dn.rstd[:, 0:Wc],
                        op0=ALU.mult, op1=ALU.mult), reads=[dn.Bx[j], dn.Brstd, dn.Bcst], writes=[Bys])
                finals.append(P.dma("sync", [(yt[ti], ys[:])], "yo", reads=[Bys]))
        for (off_, Wc_, ih_) in tiles:
            tile_body(off_, Wc_, ih_)
        P.emit(final_waits=finals)
    return nc


_CACHE = {}
_DBG = {}


def _tile_up(f, L):
    return _tile_w(f("ffn_w_up")[L], KC)


def _tile_dn(f, L):
    Wd = f("ffn_w_down")[L]
    return np.ascontiguousarray(Wd.reshape(G, PPG, 128, KC, 128).transpose(0, 3, 2, 1, 4))


def _phaseA(inputs):
    f = lambda k: np.asarray(inputs[k], np.float32)
    x = f("x")[0]
    cores = list(range(NCORE))

    cpA = CstPack()
    _cst_common(cpA, inputs)
    cpA.put("b_in", _cols(f("conv_b_in")[0]))
    cpA.put("cdw", np.ascontiguousarray(f("conv_dw_w")[0].T.reshape(KC, 128, CK).transpose(1, 0, 2)))
    cpA.put("cdb", _cols(f("conv_dw_b")[0]))
    cpA.put("lng", _cols(f("conv_ln_g")[0]))
    cpA.put("lnb", _cols(f("conv_ln_b")[0]))
    cpA.put("b_out", _cols(f("conv_b_out")[0]))
    cpA.put("flag", np.ones((128, 1), np.float32))
    cstA = cpA.build()
    w_in_t = _tile_w(f("conv_w_in")[0], KC)
    w_out_t = _tile_w(f("conv_w_out")[0], KC)
    w_up0, w_dn0 = _tile_up(f, 0), _tile_dn(f, 0)
    in_maps = []
    for c in cores:
        cc = cstA.copy()
        if c == 0:
            cc[:, cpA.off["flag"]] = 0.0
        seg = np.zeros((HALO + TPC, D), np.float32)
        lo = c * TPC - HALO
        if lo < 0:
            seg[HALO:] = x[0:TPC]
        else:
            seg[:] = x[lo:lo + HALO + TPC]
        fm = _fm(seg)
        in_maps.append({"xh": np.ascontiguousarray(fm[:, :, :HALO]),
                        "xt": np.ascontiguousarray(fm[:, :, HALO:].reshape(128, KC, NT, W).transpose(2, 0, 1, 3)),
                        "cst": cc, "w_in": w_in_t, "w_out": w_out_t, "w_up": w_up0, "w_dn": w_dn0})
    ncA = build_A(cstA.shape[1], cpA.off)
    resA = run_bass_kernel_spmd(ncA, in_maps, core_ids=cores).results
    x1T = [(np.asarray(r["x1h"]), np.asarray(r["x1t"])) for r in resA]
    hall = np.ascontiguousarray(np.concatenate([np.asarray(r["h1t"]) for r in resA], axis=0))
    return x1T, hall


def _phaseB(inputs, hall):
    f = lambda k: np.asarray(inputs[k], np.float32)
    cores = list(range(NCORE))
    lambda_init = 0.8 - 0.6 * math.exp(-0.3 * 1)
    used, planes = _bucket_planes()
    wqkv = f("attn_w_qkv")[0]
    rel_bias = f("rel_bias")
    lam_in = np.concatenate([f("attn_lambda_q1")[0], f("attn_lambda_k1")[0], f("attn_lambda_q2")[0],
                             f("attn_lambda_k2")[0]])[None, :].repeat(128, 0)
    in_maps = []
    cpB = None
    for c in cores:
        cpB = CstPack()
        cpB.put("eps2", np.full((128, 1), EPS / (1.0 - lambda_init) ** 2, np.float32))
        cpB.put("lam", lam_in)
        hs = [2 * c, 2 * c + 1]
        cpB.put("tab", np.concatenate([rel_bias[:, h] for h in hs])[None, :].repeat(128, 0))
        cpB.put("farb", np.array([rel_bias[15, h] for h in hs], np.float32)[None, :].repeat(128, 0))
        cpB.put("subg", (f("attn_subln_g")[0])[:, None])
        wc = np.stack([np.stack([_tile_w(wqkv[:, part * D + h * 128: part * D + (h + 1) * 128], KC)[0]
                                 for part in range(3)]) for h in hs])
        in_maps.append({"hall": hall, "cstb": cpB.build(), "wqkv": np.ascontiguousarray(wc), "planes": planes})
    ncB = build_B(cpB.n, cpB.off, used, lambda_init)
    resB = run_bass_kernel_spmd(ncB, in_maps, core_ids=cores).results
    oT_all = np.concatenate([np.asarray(r["oT"]).reshape(256, S) for r in resB], axis=0)
    return oT_all


def _phaseC(inputs, x1T, oT_all):
    f = lambda k: np.asarray(inputs[k], np.float32)
    cores = list(range(NCORE))
    cpC = CstPack()
    _cst_common(cpC, inputs)
    cpC.put("flag", np.ones((128, 1), np.float32))
    cstC = cpC.build()
    w_o_t = _tile_w(f("attn_w_o")[0], KC)
    w_up1, w_dn1 = _tile_up(f, 1), _tile_dn(f, 1)
    in_maps = []
    for c in cores:
        cc = cstC.copy()
        if c == 0:
            cc[:, cpC.off["flag"]] = 0.0
        oc = np.zeros((D, 2 + TPC), oT_all.dtype)
        if c == 0:
            oc[:, 2:] = oT_all[:, 0:TPC]
        else:
            oc[:] = oT_all[:, c * TPC - 2:(c + 1) * TPC]
        ofm = oc.reshape(KC, 128, 2 + TPC).transpose(1, 0, 2)
        in_maps.append({"x1h": x1T[c][0], "x1t": x1T[c][1], "oh": np.ascontiguousarray(ofm[:, :, :2]),
                        "ot": np.ascontiguousarray(ofm[:, :, 2:].reshape(128, KC, NT, W).transpose(2, 0, 1, 3)),
                        "cst": cc, "w_o": w_o_t, "w_up": w_up1, "w_dn": w_dn1})
    ncC = build_C(cstC.shape[1], cpC.off)
    resC = run_bass_kernel_spmd(ncC, in_maps, core_ids=cores).results
    out = np.empty((1, S, D), np.float32)
    for c in cores:
        yt = np.asarray(resC[c]["yt"])
        out[0, c * TPC:(c + 1) * TPC, :] = yt.transpose(0, 3, 2, 1).reshape(TPC, D)
    return out


def kernel(**inputs):
    x1T, hall = _phaseA(inputs)
    oT_all = _phaseB(inputs, hall)
    return _phaseC(inputs, x1T, oT_all)
```

```python
import math
import contextlib
import numpy as np
import ml_dtypes
import concourse.bass as bass
import concourse.mybir as mybir
from concourse.bass_utils import run_bass_kernel_spmd

F32 = mybir.dt.float32
BF16 = mybir.dt.bfloat16
AF = mybir.ActivationFunctionType
ALU = mybir.AluOpType
AX = mybir.AxisListType

D = 2048
S = 16384
NCORE = 8
TPC = S // NCORE
W = 512
NT = TPC // W
HALO = 34
WX = W + HALO
KC = D // 128
FF = 5632
NHT = 2 * FF // 128
NPAIR = FF // 128
G = 4
PPG = NPAIR // G
CK = 31
EPS = 1e-6
NEG = -30000.0
ENGS = ("tensor", "vector", "scalar", "gpsimd", "sync")


class Buf:
    __slots__ = ("name", "writer", "readers")

    def __init__(self, name):
        self.name = name
        self.writer = None
        self.readers = []


class Op:
    __slots__ = ("eng", "fn", "deps", "signaled", "sigval", "is_dma", "dkey", "dcount")

    def __init__(self, eng, fn, is_dma=False, dkey=None):
        self.eng = eng
        self.fn = fn
        self.deps = []
        self.signaled = False
        self.sigval = 0
        self.is_dma = is_dma
        self.dkey = dkey
        self.dcount = 0


class Prog:
    def __init__(self, nc):
        self.nc = nc
        self.ops = {e: [] for e in ENGS}
        self.dma_counts = {}

    def _hazards(self, op, reads, writes):
        deps = []
        for b in reads:
            if b.writer is not None:
                deps.append(b.writer)
        for b in writes:
            if b.writer is not None:
                deps.append(b.writer)
            deps.extend(b.readers)
        seen = set()
        for d in deps:
            if d is op or id(d) in seen:
                continue
            seen.add(id(d))
            if (not d.is_dma) and (not op.is_dma) and d.eng == op.eng:
                if op.eng == "tensor" or not any(b.writer is d for b in reads):
                    continue
            op.deps.append(d)
            d.signaled = True
        for b in reads:
            b.readers.append(op)
        for b in writes:
            b.writer = op
            b.readers = []

    def add(self, eng, fn, reads=(), writes=()):
        op = Op(eng, fn)
        self._hazards(op, reads, writes)
        self.ops[eng].append(op)
        return op

    def dma(self, eng, pairs, key, reads=(), writes=()):
        def fn(e, pairs=pairs):
            return [e.dma_start(out=o, in_=i) for (o, i) in pairs]
        op = Op(eng, fn, is_dma=True, dkey=key)
        self._hazards(op, reads, writes)
        c = self.dma_counts.get(key, 0) + len(pairs)
        self.dma_counts[key] = c
        op.dcount = c
        self.ops[eng].append(op)
        return op

    def emit(self, final_waits=()):
        nc = self.nc
        EPOCH = 1500
        DEPOCH = 64
        with contextlib.ExitStack() as st:
            for d in final_waits:
                d.signaled = True
            nep = {}
            for e in ENGS:
                c = 0
                for op in self.ops[e]:
                    if op.signaled and not op.is_dma:
                        op.sigval = c
                        c += 1
                nep[e] = (c + EPOCH - 1) // EPOCH
            esem = {e: [st.enter_context(nc.semaphore("es_%s_%d" % (e, i))) for i in range(nep[e])] for e in ENGS}
            dsem = {k: [st.enter_context(nc.semaphore("ds_%s_%d" % (k, i)))
                        for i in range((n + DEPOCH - 1) // DEPOCH)] for k, n in self.dma_counts.items()}
            if _DBG.get("verbose"):
                print("epochs", nep, "nops", {e: len(self.ops[e]) for e in ENGS}, "dma", max(self.dma_counts.values()))
            block = st.enter_context(nc.Block())

            def run(e, eng):
                waited = {}

                def wait(d):
                    if d.is_dma:
                        ep = (d.dcount - 1) // DEPOCH
                        sem, val = dsem[d.dkey][ep], 16 * (d.dcount - ep * DEPOCH)
                    else:
                        ep = d.sigval // EPOCH
                        sem, val = esem[d.eng][ep], d.sigval - ep * EPOCH + 1
                    if waited.get(id(sem), 0) >= val:
                        return
                    waited[id(sem)] = val
                    eng.wait_ge(sem, val)

                for op in self.ops[e]:
                    for d in op.deps:
                        wait(d)
                    r = op.fn(eng)
                    if op.is_dma:
                        ep = (op.dcount - 1) // DEPOCH
                        for ins in r:
                            ins.then_inc(dsem[op.dkey][ep], 16)
                    elif op.signaled:
                        r.then_inc(esem[e][op.sigval // EPOCH], 1)
                if e == "sync":
                    for d in final_waits:
                        wait(d)

            block.tensor(lambda eng: run("tensor", eng))
            block.vector(lambda eng: run("vector", eng))
            block.scalar(lambda eng: run("scalar", eng))
            block.gpsimd(lambda eng: run("gpsimd", eng))
            block.sync(lambda eng: run("sync", eng))


class Ring:
    def __init__(self, items):
        self.items = items
        self.i = 0

    def next(self):
        it = self.items[self.i % len(self.items)]
        self.i += 1
        return it


def _cols(v):
    v = np.asarray(v, np.float32)
    return np.ascontiguousarray(v.reshape(-1, 128).T)


def _fm(seg):
    return np.ascontiguousarray(seg.T.reshape(KC, 128, seg.shape[0]).transpose(1, 0, 2))


def _tile_w(Wm, kc):
    K, M = Wm.shape
    return np.ascontiguousarray(Wm.reshape(kc, 128, M // 128, 128).transpose(2, 1, 0, 3))


class CstPack:
    def __init__(self):
        self.parts = []
        self.off = {}
        self.n = 0

    def put(self, name, arr):
        arr = np.asarray(arr, np.float32)
        assert arr.shape[0] == 128
        arr = arr.reshape(128, -1)
        self.off[name] = self.n
        self.n += arr.shape[1]
        self.parts.append(arr)

    def build(self):
        return np.ascontiguousarray(np.concatenate(self.parts, axis=1))


class Dense:
    def __init__(self, nc, P, st, cst_ap, ncst, coff):
        self.nc, self.P, self.st, self.coff = nc, P, st, coff
        sb = self.sb
        self.cst = sb("cst_sb", [128, ncst], F32)
        self.Bcst = Buf("cst")
        P.dma("sync", [(self.cst[:], cst_ap)], "cst", writes=[self.Bcst])
        self.ones = sb("ones", [128, 128], BF16)
        self.Bones = Buf("ones")
        P.add("vector", lambda e: e.memset(self.ones[:], 1.0), writes=[self.Bones])
        self.banks = Ring([(st.enter_context(nc.psum_tensor("pb%d" % i, [128, 2 * W], F32)), Buf("pb%d" % i))
                           for i in range(4)])
        self.wslots = Ring([(sb("ws%d" % i, [128, KC, 128], BF16), Buf("ws%d" % i), "ws%d" % i) for i in range(6)])
        self.wdslots = Ring([(sb("wd%d" % i, [128, PPG, 128], BF16), Buf("wd%d" % i), "wd%d" % i) for i in range(4)])
        self.x = sb("x", [128, KC, WX], F32)
        self.Bx = [Buf("x%d" % j) for j in range(KC)]
        self.h = sb("h", [128, KC, WX], BF16)
        self.Bh = Buf("h")
        self.sq = [(sb("sq%d" % i, [128, WX], BF16), Buf("sq%d" % i)) for i in range(2)]
        self.rstd = sb("rstd", [128, WX], F32)
        self.Brstd = Buf("rstd")
        self.tmp = [(sb("tmp%d" % i, [128, WX], F32), Buf("tmp%d" % i)) for i in range(3)]
        self.hid = sb("hid", [128, PPG, WX], BF16)
        self.Bhid = [Buf("hid%d" % j) for j in range(PPG)]
        self.eb = Ring([(sb("e%d" % i, [128, WX + 2], F32), Buf("e%d" % i)) for i in range(4)])
        self.yb = Ring([(sb("y%d" % i, [128, WX], F32), Buf("y%d" % i)) for i in range(4)])
        self.fst = sb("fst", [128, NHT, 2], F32)
        self.Bfst = [Buf("fst%d" % j) for j in range(NHT)]
        Bf = self.Bfst
        P.add("vector", lambda e: e.memset(self.fst[:], 0.0), writes=Bf)

    def sb(self, name, shape, dt):
        return self.st.enter_context(self.nc.sbuf_tensor(name, shape, dt))

    def c(self, name, j=0, n=1):
        o = self.coff[name] + j
        return self.cst[:, o:o + n]

    def load_w(self, dram_tile):
        slot, B, key = self.wslots.next()
        self.P.dma("gpsimd", [(slot[:], dram_tile)], key, writes=[B])
        return slot, B

    def load_wd(self, dram_tile):
        slot, B, key = self.wdslots.next()
        self.P.dma("gpsimd", [(slot[:], dram_tile)], key, writes=[B])
        return slot, B

    @staticmethod
    def mm(e, ps, lhsT, rhs, Wc, start, stop):
        w0 = min(Wc, W)
        ins = e.matmul(ps[:, 0:w0], lhsT=lhsT, rhs=rhs[:, 0:w0], start=start, stop=stop)
        if Wc > W:
            ins = e.matmul(ps[:, W:Wc], lhsT=lhsT, rhs=rhs[:, W:Wc], start=start, stop=stop)
        return ins

    def mm_group(self, ps, wslot, act, nk, Wc, reads, Bps):
        def fn(e):
            ins = None
            for k in range(nk):
                ins = self.mm(e, ps, wslot[:, k, :], act[:, k, :], Wc, k == 0, k == nk - 1)
            return ins
        return self.P.add("tensor", fn, reads=reads, writes=[Bps])

    def colsum_rstd(self, srcs, Wc, scale, out_ap, Bout):
        P = self.P
        ps, Bps = self.banks.next()
        n = len(srcs)
        for j, (src, Bsrc) in enumerate(srcs):
            sq, Bsq = self.sq[j % 2]
            P.add("scalar", lambda e, sq=sq, src=src: e.activation(out=sq[:, 0:Wc], in_=src, func=AF.Square),
                  reads=[Bsrc], writes=[Bsq])
            P.add("tensor", lambda e, sq=sq, j=j: self.mm(e, ps, self.ones[:], sq, Wc, j == 0, j == n - 1),
                  reads=[Bsq, self.Bones], writes=[Bps])
        t, Bt = self.tmp[0]
        P.add("scalar", lambda e: e.activation(out=t[:, 0:Wc], in_=ps[:, 0:Wc], func=AF.Sqrt, bias=self.c("eps"),
                                               scale=scale),
              reads=[Bps, self.Bcst], writes=[Bt])
        P.add("vector", lambda e: e.reciprocal(out=out_ap, in_=t[:, 0:Wc]), reads=[Bt], writes=[Bout])

    def rmsnorm_to_h(self, gname, Wc):
        P = self.P
        self.colsum_rstd([(self.x[:, j, 0:Wc], self.Bx[j]) for j in range(KC)], Wc, 1.0 / D,
                         self.rstd[:, 0:Wc], self.Brstd)
        for j in range(KC):
            P.add("vector", lambda e, j=j: e.scalar_tensor_tensor(
                out=self.h[:, j, 0:Wc], in0=self.x[:, j, 0:Wc], scalar=self.c(gname, j), in1=self.rstd[:, 0:Wc],
                op0=ALU.mult, op1=ALU.mult), reads=[self.Bx[j], self.Brstd, self.Bcst], writes=[self.Bh])

    def ffn(self, layer, Wc, w_up, w_dn, gname, nhalo):
        P = self.P
        self.rmsnorm_to_h(gname, Wc)
        dww, dwb = "fdw%d" % layer, "fdb%d" % layer
        for g in range(G):
            for jj in range(PPG):
                pj = g * PPG + jj
                ys = []
                for half in range(2):
                    t = pj + half * NPAIR
                    ws, Bw = self.load_w(w_up[t])
                    ps, Bps = self.banks.next()
                    self.mm_group(ps, ws, self.h, KC, Wc, [Bw, self.Bh], Bps)
                    eb, Be = self.eb.next()
                    yb, By = self.yb.next()
                    st_ap = self.fst[:, t, :]
                    P.add("vector", lambda e, eb=eb, st_ap=st_ap: e.tensor_copy(out=eb[:, 0:2], in_=st_ap),
                          reads=[self.Bfst[t]], writes=[Be])
                    P.add("scalar", lambda e, eb=eb, ps=ps: e.activation(out=eb[:, 2:2 + Wc], in_=ps[:, 0:Wc],
                                                                         func=AF.Copy),
                          reads=[Bps], writes=[Be])
                    if nhalo:
                        P.add("vector", lambda e, eb=eb: e.tensor_scalar(
                            out=eb[:, 2:2 + nhalo], in0=eb[:, 2:2 + nhalo], scalar1=self.c("flag"), scalar2=None,
                            op0=ALU.mult), reads=[Be, self.Bcst], writes=[Be])
                    P.add("vector", lambda e, eb=eb, st_ap=st_ap: e.tensor_copy(out=st_ap, in_=eb[:, Wc:Wc + 2]),
                          reads=[Be], writes=[self.Bfst[t]])
                    P.add("vector", lambda e, eb=eb, yb=yb, t=t: e.tensor_scalar(
                        out=yb[:, 0:Wc], in0=eb[:, 2:2 + Wc], scalar1=self.c(dww, 3 * t + 2), scalar2=self.c(dwb, t),
                        op0=ALU.mult, op1=ALU.add), reads=[Be, self.Bcst], writes=[By])
                    for tap in (1, 0):
                        P.add("vector", lambda e, eb=eb, yb=yb, t=t, tap=tap: e.scalar_tensor_tensor(
                            out=yb[:, 0:Wc], in0=eb[:, tap:tap + Wc], scalar=self.c(dww, 3 * t + tap), in1=yb[:, 0:Wc],
                            op0=ALU.mult, op1=ALU.add), reads=[Be, By, self.Bcst], writes=[By])
                    ys.append((yb, By))
                (yg, Byg), (yv, Byv) = ys
                P.add("scalar", lambda e, yg=yg: e.activation(out=yg[:, 0:Wc], in_=yg[:, 0:Wc], func=AF.Silu),
                      reads=[Byg], writes=[Byg])
                P.add("vector", lambda e, yg=yg, yv=yv, jj=jj: e.tensor_tensor(
                    out=self.hid[:, jj, 0:Wc], in0=yg[:, 0:Wc], in1=yv[:, 0:Wc], op=ALU.mult),
                    reads=[Byg, Byv], writes=[self.Bhid[jj]])
            for m in range(KC):
                wd, Bwd = self.load_wd(w_dn[g, m])
                ps, Bps = self.banks.next()
                self.mm_group(ps, wd, self.hid, PPG, Wc, [Bwd] + self.Bhid, Bps)
                P.add("vector", lambda e, ps=ps, m=m: e.tensor_tensor(
                    out=self.x[:, m, 0:Wc], in0=ps[:, 0:Wc], in1=self.x[:, m, 0:Wc], op=ALU.add),
                    reads=[Bps, self.Bx[m]], writes=[self.Bx[m]])


def _cst_common(cp, inputs):
    cp.put("eps", np.full((128, 1), EPS, np.float32))
    for L in range(2):
        cp.put("fdw%d" % L, np.ascontiguousarray(
            np.asarray(inputs["ffn_dw_w"][L], np.float32).T.reshape(NHT, 128, 3).transpose(1, 0, 2)))
        cp.put("fdb%d" % L, _cols(inputs["ffn_dw_b"][L]))
        cp.put("fng%d" % L, _cols(inputs["ffn_norm"][L]))
        cp.put("mng%d" % L, _cols(inputs["mix_norm"][L]))
    cp.put("fin", _cols(inputs["final_norm_g"]))


def build_A(ncst, coff):
    nc = bass.Bass("TRN2", target_bir_lowering=False)
    xt0 = nc.dram_tensor("xt0", [128, KC, WX], F32, kind="ExternalInput").ap()
    xt = nc.dram_tensor("xt", [NT - 1, 128, KC, W], F32, kind="ExternalInput").ap()
    cst = nc.dram_tensor("cst", [128, ncst], F32, kind="ExternalInput").ap()
    w_in = nc.dram_tensor("w_in", [32, 128, KC, 128], F32, kind="ExternalInput").ap()
    w_out = nc.dram_tensor("w_out", [16, 128, KC, 128], F32, kind="ExternalInput").ap()
    w_up = nc.dram_tensor("w_up", [NHT, 128, KC, 128], F32, kind="ExternalInput").ap()
    w_dn = nc.dram_tensor("w_dn", [G, 16, 128, PPG, 128], F32, kind="ExternalInput").ap()
    x1h = nc.dram_tensor("x1h", [128, KC, 2], F32, kind="ExternalOutput").ap()
    x1t = nc.dram_tensor("x1t", [NT, 128, KC, W], F32, kind="ExternalOutput").ap()
    h1t = nc.dram_tensor("h1t", [NT, 128, KC, W], BF16, kind="ExternalOutput").ap()
    P = Prog(nc)
    finals = []
    with contextlib.ExitStack() as st:
        dn = Dense(nc, P, st, cst, ncst, coff)
        sb = dn.sb
        uext = sb("uext", [128, KC, WX + 30], BF16)
        Bu = [Buf("u%d" % j) for j in range(KC)]
        cv = sb("cv", [128, KC, WX], F32)
        Bcv = [Buf("cv%d" % j) for j in range(KC)]
        sg = [(sb("sg%d" % i, [128, WX], F32), Buf("sg%d" % i)) for i in range(2)]
        mu = sb("mu", [128, WX], F32)
        Bmu = Buf("mu")
        nmr = sb("nmr", [128, WX], F32)
        Bnmr = Buf("nmr")
        h1s, Bh1s = dn.h, dn.Bh
        P.add("vector", lambda e: e.memset(uext[:, :, 0:30], 0.0), writes=Bu)
        hso = sb("hso", [128, KC, 2], F32)
        Bhso = Buf("hso")

        tiles = [(0, WX, HALO)] + [(i, W, 0) for i in range(1, NT)]

        def tile_body(ti, Wc, nh):
            if nh:
                P.dma("sync", [(dn.x[:, :, 0:Wc], xt0)], "xin", writes=dn.Bx)
            else:
                P.dma("sync", [(dn.x[:, :, 0:Wc], xt[ti - 1])], "xin", writes=dn.Bx)
            dn.rmsnorm_to_h("mng0", Wc)
            for m in range(KC):
                wa, Bwa = dn.load_w(w_in[m])
                pa, Bpa = dn.banks.next()
                dn.mm_group(pa, wa, dn.h, KC, Wc, [Bwa, dn.Bh], Bpa)
                wg, Bwg = dn.load_w(w_in[m + KC])
                pg, Bpg = dn.banks.next()
                dn.mm_group(pg, wg, dn.h, KC, Wc, [Bwg, dn.Bh], Bpg)
                s, Bs = sg[m % 2]
                P.add("scalar", lambda e, s=s, pg=pg, m=m: e.activation(
                    out=s[:, 0:Wc], in_=pg[:, 0:Wc], func=AF.Sigmoid, bias=dn.c("b_in", KC + m)),
                    reads=[Bpg, dn.Bcst], writes=[Bs])
                P.add("vector", lambda e, s=s, pa=pa, m=m: e.scalar_tensor_tensor(
                    out=uext[:, m, 30:30 + Wc], in0=pa[:, 0:Wc], scalar=dn.c("b_in", m), in1=s[:, 0:Wc],
                    op0=ALU.add, op1=ALU.mult), reads=[Bpa, Bs, dn.Bcst], writes=[Bu[m]])
            if nh:
                P.add("vector", lambda e: e.tensor_scalar(
                    out=uext[:, :, 30:30 + nh], in0=uext[:, :, 30:30 + nh], scalar1=dn.c("flag"), scalar2=None,
                    op0=ALU.mult), reads=Bu + [dn.Bcst], writes=Bu)
            for m in range(KC):
                P.add("vector", lambda e, m=m: e.tensor_scalar(
                    out=cv[:, m, 0:Wc], in0=uext[:, m, 0:Wc], scalar1=dn.c("cdw", m * CK), scalar2=dn.c("cdb", m),
                    op0=ALU.mult, op1=ALU.add), reads=[Bu[m], dn.Bcst], writes=[Bcv[m]])
                for tap in range(1, CK):
                    P.add("vector", lambda e, m=m, tap=tap: e.scalar_tensor_tensor(
                        out=cv[:, m, 0:Wc], in0=uext[:, m, tap:tap + Wc], scalar=dn.c("cdw", m * CK + tap),
                        in1=cv[:, m, 0:Wc], op0=ALU.mult, op1=ALU.add),
                        reads=[Bu[m], Bcv[m], dn.Bcst], writes=[Bcv[m]])
            P.add("vector", lambda e: e.tensor_copy(out=uext[:, :, 0:30], in_=uext[:, :, Wc:Wc + 30]),
                  reads=Bu, writes=Bu)
            ps_s, Bps_s = dn.banks.next()
            ps_q, Bps_q = dn.banks.next()
            for j in range(KC):
                sq, Bsq = dn.sq[0]
                cb, Bcb = dn.sq[1]
                P.add("scalar", lambda e, j=j, cb=cb: e.activation(out=cb[:, 0:Wc], in_=cv[:, j, 0:Wc], func=AF.Copy),
                      reads=[Bcv[j]], writes=[Bcb])
                P.add("scalar", lambda e, j=j, sq=sq: e.activation(out=sq[:, 0:Wc], in_=cv[:, j, 0:Wc],
                                                                   func=AF.Square),
                      reads=[Bcv[j]], writes=[Bsq])
                P.add("tensor", lambda e, j=j, cb=cb: dn.mm(e, ps_s, dn.ones[:], cb, Wc, j == 0, j == KC - 1),
                      reads=[Bcb, dn.Bones], writes=[Bps_s])
                P.add("tensor", lambda e, j=j, sq=sq: dn.mm(e, ps_q, dn.ones[:], sq, Wc, j == 0, j == KC - 1),
                      reads=[Bsq, dn.Bones], writes=[Bps_q])
            t1, Bt1 = dn.tmp[1]
            t2, Bt2 = dn.tmp[2]
            t0, Bt0 = dn.tmp[0]
            P.add("vector", lambda e: e.tensor_scalar(out=mu[:, 0:Wc], in0=ps_s[:, 0:Wc], scalar1=1.0 / D, scalar2=None,
                                                      op0=ALU.mult), reads=[Bps_s], writes=[Bmu])
            P.add("vector", lambda e: e.tensor_tensor(out=t1[:, 0:Wc], in0=mu[:, 0:Wc], in1=mu[:, 0:Wc], op=ALU.mult),
                  reads=[Bmu], writes=[Bt1])
            P.add("vector", lambda e: e.scalar_tensor_tensor(out=t2[:, 0:Wc], in0=ps_q[:, 0:Wc], scalar=1.0 / D,
                                                             in1=t1[:, 0:Wc], op0=ALU.mult, op1=ALU.subtract),
                  reads=[Bps_q, Bt1], writes=[Bt2])
            P.add("scalar", lambda e: e.activation(out=t0[:, 0:Wc], in_=t2[:, 0:Wc], func=AF.Sqrt, bias=dn.c("eps"),
                                                   scale=1.0), reads=[Bt2, dn.Bcst], writes=[Bt0])
            P.add("vector", lambda e: e.reciprocal(out=dn.rstd[:, 0:Wc], in_=t0[:, 0:Wc]), reads=[Bt0],
                  writes=[dn.Brstd])
            P.add("vector", lambda e: e.scalar_tensor_tensor(out=nmr[:, 0:Wc], in0=mu[:, 0:Wc], scalar=-1.0,
                                                             in1=dn.rstd[:, 0:Wc], op0=ALU.mult, op1=ALU.mult),
                  reads=[Bmu, dn.Brstd], writes=[Bnmr])
            for j in range(KC):
                P.add("vector", lambda e, j=j: e.tensor_tensor(out=cv[:, j, 0:Wc], in0=cv[:, j, 0:Wc],
                                                               in1=dn.rstd[:, 0:Wc], op=ALU.mult),
                      reads=[Bcv[j], dn.Brstd], writes=[Bcv[j]])
                P.add("vector", lambda e, j=j: e.tensor_tensor(out=cv[:, j, 0:Wc], in0=cv[:, j, 0:Wc],
                                                               in1=nmr[:, 0:Wc], op=ALU.add),
                      reads=[Bcv[j], Bnmr], writes=[Bcv[j]])
                P.add("scalar", lambda e, j=j: e.activation(out=dn.h[:, j, 0:Wc], in_=cv[:, j, 0:Wc], func=AF.Silu,
                                                            bias=dn.c("lnb", j), scale=dn.c("lng", j)),
                      reads=[Bcv[j], dn.Bcst], writes=[dn.Bh])
            for m in range(KC):
                ws, Bw = dn.load_w(w_out[m])
                ps, Bps = dn.banks.next()
                dn.mm_group(ps, ws, dn.h, KC, Wc, [Bw, dn.Bh], Bps)
                P.add("vector", lambda e, ps=ps, m=m: e.scalar_tensor_tensor(
                    out=dn.x[:, m, 0:Wc], in0=ps[:, 0:Wc], scalar=dn.c("b_out", m), in1=dn.x[:, m, 0:Wc],
                    op0=ALU.add, op1=ALU.add), reads=[Bps, dn.Bx[m], dn.Bcst], writes=[dn.Bx[m]])
            dn.ffn(0, Wc, w_up, w_dn, "fng0", nh)
            if nh:
                P.add("vector", lambda e: e.tensor_copy(out=hso[:], in_=dn.x[:, :, nh - 2:nh]), reads=dn.Bx,
                      writes=[Bhso])
                finals.append(P.dma("sync", [(x1h, hso[:])], "x1ho", reads=[Bhso]))
            finals.append(P.dma("sync", [(x1t[ti], dn.x[:, :, nh:nh + W])], "x1o", reads=dn.Bx))
            dn.colsum_rstd([(dn.x[:, j, 0:Wc], dn.Bx[j]) for j in range(KC)], Wc, 1.0 / D,
                           dn.rstd[:, 0:Wc], dn.Brstd)
            for j in range(KC):
                P.add("vector", lambda e, j=j: e.scalar_tensor_tensor(
                    out=h1s[:, j, 0:Wc], in0=dn.x[:, j, 0:Wc], scalar=dn.c("mng1", j), in1=dn.rstd[:, 0:Wc],
                    op0=ALU.mult, op1=ALU.mult), reads=[dn.Bx[j], dn.Brstd, dn.Bcst], writes=[Bh1s])
            finals.append(P.dma("sync", [(h1t[ti], h1s[:, :, nh:nh + W])], "h1o", reads=[Bh1s]))
        for (ti_, Wc_, nh_) in tiles:
            tile_body(ti_, Wc_, nh_)
        P.emit(final_waits=finals)
    return nc


def _bucket_planes():
    kk = np.arange(128)[:, None]
    m = np.arange(1024)[None, :]
    rel = kk - m + 384
    allowed = (kk // 64) <= (m // 64 - 6)
    n = np.abs(rel)
    nf = np.maximum(n, 1).astype(np.float32)
    large = 8 + (np.log(nf / np.float32(8)) / np.float32(math.log(16.0)) * np.float32(8)).astype(np.int32)
    large = np.minimum(large, 15)
    bucket = np.where(rel > 0, 16, 0) + np.where(n < 8, n, large)
    used = sorted(set(bucket[allowed].tolist()))
    planes = np.zeros((len(used) + 1, 128, 1024), np.float32)
    planes[0] = np.where(allowed, 0.0, NEG)
    for i, b in enumerate(used):
        planes[i + 1] = ((bucket == b) & allowed).astype(np.float32)
    return used, planes


def build_B(nB, boff, used, lambda_init):
    nc = bass.Bass("TRN2", target_bir_lowering=False)
    hall = nc.dram_tensor("hall", [NCORE * NT, 128, KC, W], BF16, kind="ExternalInput").ap()
    cstb = nc.dram_tensor("cstb", [128, nB], F32, kind="ExternalInput").ap()
    wqkv = nc.dram_tensor("wqkv", [2, 3, 128, KC, 128], F32, kind="ExternalInput").ap()
    planes = nc.dram_tensor("planes", [len(used) + 1, 128, 1024], F32, kind="ExternalInput").ap()
    oT = nc.dram_tensor("oT", [2, 128, S], BF16, kind="ExternalOutput").ap()
    P = Prog(nc)
    finals = []
    NQ = S // W
    with contextlib.ExitStack() as st:
        def sb(name, shape, dt):
            return st.enter_context(nc.sbuf_tensor(name, shape, dt))
        cst = sb("cst", [128, nB], F32)
        Bcst = Buf("cst")
        P.dma("sync", [(cst[:], cstb)], "cst", writes=[Bcst])

        def c(name, j=0, n=1):
            o = boff[name] + j
            return cst[:, o:o + n]
        ones = sb("ones", [128, 128], BF16)
        Bones = Buf("ones")
        P.add("vector", lambda e: e.memset(ones[:], 1.0), writes=[Bones])
        sbanks = Ring([(st.enter_context(nc.psum_tensor("sb%d" % i, [128, 512], F32)), Buf("sb%d" % i))
                       for i in range(4)])
        obank = [(st.enter_context(nc.psum_tensor("ob%d" % i, [128, 512], F32)), Buf("ob%d" % i)) for i in range(2)]
        zbank = [(st.enter_context(nc.psum_tensor("zb%d" % i, [128, 512], F32)), Buf("zb%d" % i)) for i in range(2)]
        Kt = sb("Kt", [128, S], BF16)
        BK = [Buf("K%d" % i) for i in range(NQ)]
        Vt = sb("Vt", [128, S // 128, 128], BF16)
        BV = [Buf("V%d" % i) for i in range(NQ)]
        qts = Ring([(sb("qt%d" % i, [128, W], BF16), Buf("qt%d" % i)) for i in range(2)])
        hts = Ring([(sb("ht%d" % i, [128, KC, W], BF16), Buf("ht%d" % i), "ht%d" % i) for i in range(2)])
        wq = [(sb("wq%d" % i, [128, KC, 128], BF16), Buf("wq%d" % i)) for i in range(3)]
        strip = sb("strip", [128, 1024], F32)
        Bstrip = Buf("strip")
        pls = Ring([(sb("pl%d" % i, [128, 1024], F32), Buf("pl%d" % i), "pl%d" % i) for i in range(2)])
        pts = Ring([(sb("pt%d" % i, [128, W], BF16), Buf("pt%d" % i)) for i in range(6)])
        tns = Ring([(sb("tn%d" % i, [128, W], F32), Buf("tn%d" % i)) for i in range(2)])
        ep = [(sb("ep%d" % i, [128, W], F32), Buf("ep%d" % i)) for i in range(5)]
        sqb = sb("sqb", [128, W], BF16)
        Bsqb = Buf("sqb")
        ons = Ring([(sb("on%d" % i, [128, W], BF16), Buf("on%d" % i), "on%d" % i) for i in range(2)])
        zacc = [(sb("zacc%d" % i, [128, W], F32), Buf("zacc%d" % i)) for i in range(2)]
        ones32 = sb("ones32", [128, 128], F32)
        Bones32 = Buf("ones32")
        P.add("vector", lambda e: e.memset(ones32[:], 1.0), writes=[Bones32])
        lamt = sb("lamt", [128, 8], F32)
        Blam = Buf("lam")
        lprod = sb("lprod", [128, 2, 64], F32)
        Blprod = Buf("lprod")
        P.add("vector", lambda e: e.tensor_tensor(out=lprod[:, 0, :], in0=c("lam", 0, 64), in1=c("lam", 64, 64),
                                                  op=ALU.mult), reads=[Bcst], writes=[Blprod])
        P.add("vector", lambda e: e.tensor_tensor(out=lprod[:, 1, :], in0=c("lam", 128, 64), in1=c("lam", 192, 64),
                                                  op=ALU.mult), reads=[Bcst], writes=[Blprod])
        P.add("vector", lambda e: e.tensor_reduce(out=lamt[:, 0:2], in_=lprod[:], axis=AX.X, op=ALU.add),
              reads=[Blprod], writes=[Blam])
        P.add("scalar", lambda e: e.activation(out=lamt[:, 2:4], in_=lamt[:, 0:2], func=AF.Exp), reads=[Blam],
              writes=[Blam])
        P.add("vector", lambda e: e.scalar_tensor_tensor(out=lamt[:, 4:5], in0=lamt[:, 3:4], scalar=-lambda_init,
                                                         in1=lamt[:, 2:3], op0=ALU.add, op1=ALU.subtract),
              reads=[Blam], writes=[Blam])

        def head_body(hd):
            for i in range(3):
                P.dma("gpsimd", [(wq[i][0][:], wqkv[hd, i])], "wq%d" % i, writes=[wq[i][1]])
            pl, Bpl, kpl = pls.next()
            P.dma("sync", [(pl[:], planes[0])], kpl, writes=[Bpl])
            P.add("vector", lambda e, pl=pl: e.tensor_copy(out=strip[:], in_=pl[:]), reads=[Bpl], writes=[Bstrip])
            for bi, b in enumerate(used):
                pl, Bpl, kpl = pls.next()
                P.dma("sync", [(pl[:], planes[bi + 1])], kpl, writes=[Bpl])
                P.add("vector", lambda e, pl=pl, b=b: e.scalar_tensor_tensor(
                    out=strip[:], in0=pl[:], scalar=c("tab", hd * 32 + b), in1=strip[:], op0=ALU.mult, op1=ALU.add),
                    reads=[Bpl, Bstrip, Bcst], writes=[Bstrip])
            def proj(i):
                ht, Bht, kht = hts.next()
                P.dma("sync", [(ht[:], hall[i])], kht, writes=[Bht])
                ps, Bps = sbanks.next()
                P.add("tensor", lambda e, ps=ps, ht=ht: [e.matmul(ps[:], lhsT=wq[1][0][:, k, :], rhs=ht[:, k, :],
                                                                  start=(k == 0), stop=(k == KC - 1))
                                                         for k in range(KC)][-1],
                      reads=[wq[1][1], Bht], writes=[Bps])
                P.add("scalar", lambda e, ps=ps, i=i: e.activation(out=Kt[:, i * W:(i + 1) * W], in_=ps[:],
                                                                   func=AF.Copy),
                      reads=[Bps], writes=[BK[i]])
                ps, Bps = sbanks.next()
                qt, Bqt = qts.next()
                P.add("tensor", lambda e, ps=ps, ht=ht: [e.matmul(ps[:], lhsT=wq[0][0][:, k, :], rhs=ht[:, k, :],
                                                                  start=(k == 0), stop=(k == KC - 1))
                                                         for k in range(KC)][-1],
                      reads=[wq[0][1], Bht], writes=[Bps])
                P.add("scalar", lambda e, ps=ps, qt=qt: e.activation(out=qt[:], in_=ps[:], func=AF.Copy, scale=0.125),
                      reads=[Bps], writes=[Bqt])
                ps, Bps = sbanks.next()

                def vmm(e, ps=ps, ht=ht):
                    ins = None
                    for sub in range(4):
                        for k in range(KC):
                            ins = e.matmul(ps[:, sub * 128:(sub + 1) * 128], lhsT=ht[:, k, sub * 128:(sub + 1) * 128],
                                           rhs=wq[2][0][:, k, :], start=(k == 0), stop=(k == KC - 1))
                    return ins
                P.add("tensor", vmm, reads=[wq[2][1], Bht], writes=[Bps])
                P.add("vector", lambda e, ps=ps, i=i: e.tensor_copy(
                    out=Vt[:, 4 * i:4 * i + 4, :], in_=ps[:].rearrange("p (a b) -> p a b", a=4)),
                    reads=[Bps], writes=[BV[i]])
                return qt, Bqt

            def scores(i, j, qt, Bqt):
                d = 128 * j - W * i
                c0 = max(d, 0)
                out = []
                for comp in range(2):
                    ps, Bps = sbanks.next()
                    lo = 64 * comp
                    P.add("tensor", lambda e, ps=ps, j=j, lo=lo, qt=qt, c0=c0: e.matmul(
                        ps[:, c0:W], lhsT=Kt[lo:lo + 64, j * 128:(j + 1) * 128], rhs=qt[lo:lo + 64, c0:W],
                        start=True, stop=True), reads=[BK[j // 4], Bqt], writes=[Bps])
                    out.append((ps, Bps))
                return out

            def expo(i, j, sc):
                d = 128 * j - W * i
                c0 = max(d, 0)
                near = j >= 4 * i - 1
                pcs = []
                for comp in range(2):
                    ps, Bps = sc[comp]
                    pt, Bpt = pts.next()
                    if near:
                        tn, Btn = tns.next()
                        so = 384 - d
                        P.add("vector", lambda e, ps=ps, tn=tn, so=so, c0=c0: e.tensor_tensor(
                            out=tn[:, c0:W], in0=ps[:, c0:W], in1=strip[:, so + c0:so + W], op=ALU.add),
                            reads=[Bps, Bstrip], writes=[Btn])
                        P.add("scalar", lambda e, tn=tn, pt=pt, c0=c0: e.activation(
                            out=pt[:, c0:W], in_=tn[:, c0:W], func=AF.Exp), reads=[Btn], writes=[Bpt])
                    else:
                        P.add("scalar", lambda e, ps=ps, pt=pt: e.activation(
                            out=pt[:], in_=ps[:], func=AF.Exp, bias=c("farb", hd)), reads=[Bps, Bcst],
                            writes=[Bpt])
                    pcs.append((pt, Bpt))
                return pcs

            def pv(i, j, pcs):
                d = 128 * j - W * i
                c0 = max(d, 0)
                nk = 4 * (i + 1)
                for comp in range(2):
                    pt, Bpt = pcs[comp]
                    ob, Bob = obank[comp]
                    P.add("tensor", lambda e, pt=pt, ob=ob, j=j, c0=c0, nk=nk: e.matmul(
                        ob[:, c0:W], lhsT=Vt[:, j, :], rhs=pt[:, c0:W], start=(j == 0), stop=(j == nk - 1)),
                        reads=[BV[j // 4], Bpt], writes=[Bob])
                    if comp == 0:
                        za, Bza = zacc[0]
                        if j == 0:
                            P.add("vector", lambda e, pt=pt, za=za: e.tensor_copy(out=za[:], in_=pt[:]),
                                  reads=[Bpt], writes=[Bza])
                        else:
                            P.add("vector", lambda e, pt=pt, za=za, c0=c0: e.tensor_tensor(
                                out=za[:, c0:W], in0=za[:, c0:W], in1=pt[:, c0:W], op=ALU.add),
                                reads=[Bpt, Bza], writes=[Bza])
                    else:
                        zb, Bzb = zbank[1]
                        P.add("tensor", lambda e, pt=pt, zb=zb, j=j, c0=c0, nk=nk: e.matmul(
                            zb[:, c0:W], lhsT=ones[:], rhs=pt[:, c0:W], start=(j == 0), stop=(j == nk - 1)),
                            reads=[Bones, Bpt], writes=[Bzb])

            (r0, Br0), (r1, Br1), (t0, Bt0), (t1, Bt1), (oo, Boo) = ep

            def epi1():
                for comp in range(1):
                    za, Bza = zacc[comp]
                    zb, Bzb = zbank[comp]
                    P.add("tensor", lambda e, za=za, zb=zb: e.matmul(zb[:], lhsT=ones32[:], rhs=za[:], start=True,
                                                                    stop=True),
                          reads=[Bones32, Bza], writes=[Bzb])
                P.add("vector", lambda e: e.reciprocal(out=r0[:], in_=zbank[0][0][:]), reads=[zbank[0][1]],
                      writes=[Br0])
                P.add("vector", lambda e: e.reciprocal(out=r1[:], in_=zbank[1][0][:]), reads=[zbank[1][1]],
                      writes=[Br1])
                P.add("vector", lambda e: e.tensor_tensor(out=t0[:], in0=obank[0][0][:], in1=r0[:], op=ALU.mult),
                      reads=[obank[0][1], Br0], writes=[Bt0])
                P.add("vector", lambda e: e.tensor_tensor(out=t1[:], in0=obank[1][0][:], in1=r1[:], op=ALU.mult),
                      reads=[obank[1][1], Br1], writes=[Bt1])
                P.add("vector", lambda e: e.scalar_tensor_tensor(out=oo[:], in0=t1[:], scalar=lamt[:, 4:5], in1=t0[:],
                                                                 op0=ALU.mult, op1=ALU.add),
                      reads=[Bt0, Bt1, Blam], writes=[Boo])
                P.add("scalar", lambda e: e.activation(out=sqb[:], in_=oo[:], func=AF.Square), reads=[Boo],
                      writes=[Bsqb])

            def epi2(i):
                ps, Bps = sbanks.next()
                P.add("tensor", lambda e, ps=ps: e.matmul(ps[:], lhsT=ones[:], rhs=sqb[:], start=True, stop=True),
                      reads=[Bones, Bsqb], writes=[Bps])
                P.add("scalar", lambda e, ps=ps: e.activation(out=r0[:], in_=ps[:], func=AF.Sqrt, bias=c("eps2"),
                                                              scale=1.0 / (128 * (1.0 - lambda_init) ** 2)),
                      reads=[Bps, Bcst], writes=[Br0])
                P.add("vector", lambda e: e.reciprocal(out=r1[:], in_=r0[:]), reads=[Br0], writes=[Br1])
                on, Bon, kon = ons.next()
                P.add("vector", lambda e, on=on: e.scalar_tensor_tensor(out=on[:], in0=oo[:], scalar=c("subg"),
                                                                        in1=r1[:], op0=ALU.mult, op1=ALU.mult),
                      reads=[Boo, Br1, Bcst], writes=[Bon])
                store_o(hd, i, on, Bon, kon)

            cur = proj(0)
            for i in range(NQ):
                qt, Bqt = cur
                nk = 4 * (i + 1)
                sc = scores(i, 0, qt, Bqt)
                for j in range(nk):
                    sc_next = scores(i, j + 1, qt, Bqt) if j + 1 < nk else None
                    pcs = expo(i, j, sc)
                    pv(i, j, pcs)
                    sc = sc_next
                epi1()
                if i + 1 < NQ:
                    cur = proj(i + 1)
                epi2(i)

        def store_o(hd, i, on, Bon, kon):
            finals.append(P.dma("sync", [(oT[hd, :, i * W:(i + 1) * W], on[:])], kon + "o", reads=[Bon]))

        for hd_ in range(2):
            head_body(hd_)
        P.emit(final_waits=finals)
    return nc


def build_C(ncst, coff):
    nc = bass.Bass("TRN2", target_bir_lowering=False)
    x1h = nc.dram_tensor("x1h", [128, KC, 2], F32, kind="ExternalInput").ap()
    x1t = nc.dram_tensor("x1t", [NT, 128, KC, W], F32, kind="ExternalInput").ap()
    oh = nc.dram_tensor("oh", [128, KC, 2], BF16, kind="ExternalInput").ap()
    ot = nc.dram_tensor("ot", [NT, 128, KC, W], BF16, kind="ExternalInput").ap()
    cst = nc.dram_tensor("cst", [128, ncst], F32, kind="ExternalInput").ap()
    w_o = nc.dram_tensor("w_o", [16, 128, KC, 128], F32, kind="ExternalInput").ap()
    w_up = nc.dram_tensor("w_up", [NHT, 128, KC, 128], F32, kind="ExternalInput").ap()
    w_dn = nc.dram_tensor("w_dn", [G, 16, 128, PPG, 128], F32, kind="ExternalInput").ap()
    yt = nc.dram_tensor("yt", [NT, 128, KC, W], F32, kind="ExternalOutput").ap()
    P = Prog(nc)
    finals = []
    with contextlib.ExitStack() as st:
        dn = Dense(nc, P, st, cst, ncst, coff)
        sb = dn.sb
        osb = sb("osb", [128, KC, W + 2], BF16)
        Bo = Buf("osb")
        ys = sb("ys", [128, KC, W + 2], F32)
        Bys = Buf("ys")
        hsx = sb("hsx", [128, KC, 2], F32)
        Bhsx = Buf("hsx")
        hsoo = sb("hsoo", [128, KC, 2], BF16)
        Bhsoo = Buf("hsoo")
        tiles = [(0, W + 2, 2)] + [(i, W, 0) for i in range(1, NT)]

        def tile_body(ti, Wc, nh):
            if nh:
                P.dma("sync", [(hsx[:], x1h)], "xhin", writes=[Bhsx])
                P.dma("sync", [(hsoo[:], oh)], "ohin", writes=[Bhsoo])
            P.dma("sync", [(dn.x[:, :, nh:nh + W], x1t[ti])], "xin", writes=dn.Bx)
            P.dma("sync", [(osb[:, :, nh:nh + W], ot[ti])], "oin", writes=[Bo])
            if nh:
                P.add("vector", lambda e: e.tensor_copy(out=dn.x[:, :, 0:2], in_=hsx[:]), reads=[Bhsx] + dn.Bx,
                      writes=dn.Bx)
                P.add("vector", lambda e: e.tensor_copy(out=osb[:, :, 0:2], in_=hsoo[:]), reads=[Bhsoo, Bo],
                      writes=[Bo])
            for m in range(KC):
                ws, Bw = dn.load_w(w_o[m])
                ps, Bps = dn.banks.next()
                dn.mm_group(ps, ws, osb, KC, Wc, [Bw, Bo], Bps)
                P.add("vector", lambda e, ps=ps, m=m: e.tensor_tensor(
                    out=dn.x[:, m, 0:Wc], in0=ps[:, 0:Wc], in1=dn.x[:, m, 0:Wc], op=ALU.add),
                    reads=[Bps, dn.Bx[m]], writes=[dn.Bx[m]])
            dn.ffn(1, Wc, w_up, w_dn, "fng1", nh)
            dn.colsum_rstd([(dn.x[:, j, 0:Wc], dn.Bx[j]) for j in range(KC)], Wc, 1.0 / D,
                           dn.rstd[:, 0:Wc], dn.Brstd)
            for j in range(KC):
                P.add("vector", lambda e, j=j: e.scalar_tensor_tensor(
                    out=ys[:, j, 0:Wc], in0=dn.x[:, j, 0:Wc], scalar=dn.c("fin", j), in1=dn.rstd[:, 0:Wc],
                    op0=ALU.mult, op1=ALU.mult), reads=[dn.Bx[j], dn.Brstd, dn.Bcst], writes=[Bys])
            finals.append(P.dma("sync", [(yt[ti], ys[:, :, nh:nh + W])], "yo", reads=[Bys]))
        for (ti_, Wc_, nh_) in tiles:
            tile_body(ti_, Wc_, nh_)
        P.emit(final_waits=finals)
    return nc


_CACHE = {}
_DBG = {}


def _tile_up(f, L):
    return _tile_w(f("ffn_w_up")[L], KC)


def _tile_dn(f, L):
    Wd = f("ffn_w_down")[L]
    return np.ascontiguousarray(Wd.reshape(G, PPG, 128, KC, 128).transpose(0, 3, 2, 1, 4))


def _phaseA(inputs):
    f = lambda k: np.asarray(inputs[k], np.float32)
    x = f("x")[0]
    cores = list(range(NCORE))

    cpA = CstPack()
    _cst_common(cpA, inputs)
    cpA.put("b_in", _cols(f("conv_b_in")[0]))
    cpA.put("cdw", np.ascontiguousarray(f("conv_dw_w")[0].T.reshape(KC, 128, CK).transpose(1, 0, 2)))
    cpA.put("cdb", _cols(f("conv_dw_b")[0]))
    cpA.put("lng", _cols(f("conv_ln_g")[0]))
    cpA.put("lnb", _cols(f("conv_ln_b")[0]))
    cpA.put("b_out", _cols(f("conv_b_out")[0]))
    cpA.put("flag", np.ones((128, 1), np.float32))
    cstA = cpA.build()
    w_in_t = _tile_w(f("conv_w_in")[0], KC)
    w_out_t = _tile_w(f("conv_w_out")[0], KC)
    w_up0, w_dn0 = _tile_up(f, 0), _tile_dn(f, 0)
    in_maps = []
    for c in cores:
        cc = cstA.copy()
        if c == 0:
            cc[:, cpA.off["flag"]] = 0.0
        seg = np.zeros((HALO + TPC, D), np.float32)
        lo = c * TPC - HALO
        if lo < 0:
            seg[HALO:] = x[0:TPC]
        else:
            seg[:] = x[lo:lo + HALO + TPC]
        fm = _fm(seg)
        in_maps.append({"xt0": np.ascontiguousarray(fm[:, :, :WX]),
                        "xt": np.ascontiguousarray(fm[:, :, WX:].reshape(128, KC, NT - 1, W).transpose(2, 0, 1, 3)),
                        "cst": cc, "w_in": w_in_t, "w_out": w_out_t, "w_up": w_up0, "w_dn": w_dn0})
    ncA = build_A(cstA.shape[1], cpA.off)
    resA = run_bass_kernel_spmd(ncA, in_maps, core_ids=cores).results
    x1T = [(np.asarray(r["x1h"]), np.asarray(r["x1t"])) for r in resA]
    hall = np.ascontiguousarray(np.concatenate([np.asarray(r["h1t"]) for r in resA], axis=0))
    return x1T, hall


def _phaseB(inputs, hall):
    f = lambda k: np.asarray(inputs[k], np.float32)
    cores = list(range(NCORE))
    lambda_init = 0.8 - 0.6 * math.exp(-0.3 * 1)
    used, planes = _bucket_planes()
    wqkv = f("attn_w_qkv")[0]
    rel_bias = f("rel_bias")
    lam_in = np.concatenate([f("attn_lambda_q1")[0], f("attn_lambda_k1")[0], f("attn_lambda_q2")[0],
                             f("attn_lambda_k2")[0]])[None, :].repeat(128, 0)
    in_maps = []
    cpB = None
    for c in cores:
        cpB = CstPack()
        cpB.put("eps2", np.full((128, 1), EPS / (1.0 - lambda_init) ** 2, np.float32))
        cpB.put("lam", lam_in)
        hs = [2 * c, 2 * c + 1]
        cpB.put("tab", np.concatenate([rel_bias[:, h] for h in hs])[None, :].repeat(128, 0))
        cpB.put("farb", np.array([rel_bias[15, h] for h in hs], np.float32)[None, :].repeat(128, 0))
        cpB.put("subg", (f("attn_subln_g")[0])[:, None])
        wc = np.stack([np.stack([_tile_w(wqkv[:, part * D + h * 128: part * D + (h + 1) * 128], KC)[0]
                                 for part in range(3)]) for h in hs])
        in_maps.append({"hall": hall, "cstb": cpB.build(), "wqkv": np.ascontiguousarray(wc), "planes": planes})
    ncB = build_B(cpB.n, cpB.off, used, lambda_init)
    resB = run_bass_kernel_spmd(ncB, in_maps, core_ids=cores).results
    oT_all = np.concatenate([np.asarray(r["oT"]).reshape(256, S) for r in resB], axis=0)
    return oT_all


def _phaseC(inputs, x1T, oT_all):
    f = lambda k: np.asarray(inputs[k], np.float32)
    cores = list(range(NCORE))
    cpC = CstPack()
    _cst_common(cpC, inputs)
    cpC.put("flag", np.ones((128, 1), np.float32))
    cstC = cpC.build()
    w_o_t = _tile_w(f("attn_w_o")[0], KC)
    w_up1, w_dn1 = _tile_up(f, 1), _tile_dn(f, 1)
    in_maps = []
    for c in cores:
        cc = cstC.copy()
        if c == 0:
            cc[:, cpC.off["flag"]] = 0.0
        oc = np.zeros((D, 2 + TPC), oT_all.dtype)
        if c == 0:
            oc[:, 2:] = oT_all[:, 0:TPC]
        else:
            oc[:] = oT_all[:, c * TPC - 2:(c + 1) * TPC]
        ofm = oc.reshape(KC, 128, 2 + TPC).transpose(1, 0, 2)
        in_maps.append({"x1h": x1T[c][0], "x1t": x1T[c][1], "oh": np.ascontiguousarray(ofm[:, :, :2]),
                        "ot": np.ascontiguousarray(ofm[:, :, 2:].reshape(128, KC, NT, W).transpose(2, 0, 1, 3)),
                        "cst": cc, "w_o": w_o_t, "w_up": w_up1, "w_dn": w_dn1})
    ncC = build_C(cstC.shape[1], cpC.off)
    resC = run_bass_kernel_spmd(ncC, in_maps, core_ids=cores).results
    out = np.empty((1, S, D), np.float32)
    for c in cores:
        yt = np.asarray(resC[c]["yt"])
        out[0, c * TPC:(c + 1) * TPC, :] = yt.transpose(0, 3, 2, 1).reshape(TPC, D)
    return out


def kernel(**inputs):
    x1T, hall = _phaseA(inputs)
    oT_all = _phaseB(inputs, hall)
    return _phaseC(inputs, x1T, oT_all)
```

```python
import math
import contextlib
import numpy as np
import ml_dtypes
import concourse.bass as bass
import concourse.mybir as mybir
from concourse.bass_utils import run_bass_kernel_spmd

F32 = mybir.dt.float32
BF16 = mybir.dt.bfloat16
AF = mybir.ActivationFunctionType
ALU = mybir.AluOpType
AX = mybir.AxisListType

D = 2048
S = 16384
NCORE = 8
TPC = S // NCORE
W = 512
NT = TPC // W
HALO = 34
WX = W + HALO
KC = D // 128
FF = 5632
NHT = 2 * FF // 128
NPAIR = FF // 128
G = 4
PPG = NPAIR // G
CK = 31
EPS = 1e-6
NEG = -30000.0
ENGS = ("tensor", "vector", "scalar", "gpsimd", "sync")


class Buf:
    __slots__ = ("name", "writer", "readers")

    def __init__(self, name):
        self.name = name
        self.writer = None
        self.readers = []


class Op:
    __slots__ = ("eng", "fn", "deps", "signaled", "sigval", "is_dma", "dkey", "dcount")

    def __init__(self, eng, fn, is_dma=False, dkey=None):
        self.eng = eng
        self.fn = fn
        self.deps = []
        self.signaled = False
        self.sigval = 0
        self.is_dma = is_dma
        self.dkey = dkey
        self.dcount = 0


class Prog:
    def __init__(self, nc):
        self.nc = nc
        self.ops = {e: [] for e in ENGS}
        self.dma_counts = {}

    def _hazards(self, op, reads, writes):
        deps = []
        for b in reads:
            if b.writer is not None:
                deps.append(b.writer)
        for b in writes:
            if b.writer is not None:
                deps.append(b.writer)
            deps.extend(b.readers)
        seen = set()
        for d in deps:
            if d is op or id(d) in seen:
                continue
            seen.add(id(d))
            if (not d.is_dma) and (not op.is_dma) and d.eng == op.eng:
                if op.eng == "tensor" or not any(b.writer is d for b in reads):
                    continue
            op.deps.append(d)
            d.signaled = True
        for b in reads:
            b.readers.append(op)
        for b in writes:
            b.writer = op
            b.readers = []

    def add(self, eng, fn, reads=(), writes=()):
        op = Op(eng, fn)
        self._hazards(op, reads, writes)
        self.ops[eng].append(op)
        return op

    def dma(self, eng, pairs, key, reads=(), writes=()):
        def fn(e, pairs=pairs):
            return [e.dma_start(out=o, in_=i) for (o, i) in pairs]
        op = Op(eng, fn, is_dma=True, dkey=key)
        self._hazards(op, reads, writes)
        c = self.dma_counts.get(key, 0) + len(pairs)
        self.dma_counts[key] = c
        op.dcount = c
        self.ops[eng].append(op)
        return op

    def emit(self, final_waits=()):
        nc = self.nc
        EPOCH = 1500
        DEPOCH = 64
        with contextlib.ExitStack() as st:
            for d in final_waits:
                d.signaled = True
            nep = {}
            for e in ENGS:
                c = 0
                for op in self.ops[e]:
                    if op.signaled and not op.is_dma:
                        op.sigval = c
                        c += 1
                nep[e] = (c + EPOCH - 1) // EPOCH
            esem = {e: [st.enter_context(nc.semaphore("es_%s_%d" % (e, i))) for i in range(nep[e])] for e in ENGS}
            dsem = {k: [st.enter_context(nc.semaphore("ds_%s_%d" % (k, i)))
                        for i in range((n + DEPOCH - 1) // DEPOCH)] for k, n in self.dma_counts.items()}
            if _DBG.get("verbose"):
                print("epochs", nep, "nops", {e: len(self.ops[e]) for e in ENGS}, "dma", max(self.dma_counts.values()))
            block = st.enter_context(nc.Block())

            def run(e, eng):
                waited = {}

                def wait(d):
                    if d.is_dma:
                        ep = (d.dcount - 1) // DEPOCH
                        sem, val = dsem[d.dkey][ep], 16 * (d.dcount - ep * DEPOCH)
                    else:
                        ep = d.sigval // EPOCH
                        sem, val = esem[d.eng][ep], d.sigval - ep * EPOCH + 1
                    if waited.get(id(sem), 0) >= val:
                        return
                    waited[id(sem)] = val
                    eng.wait_ge(sem, val)

                for op in self.ops[e]:
                    for d in op.deps:
                        wait(d)
                    r = op.fn(eng)
                    if op.is_dma:
                        ep = (op.dcount - 1) // DEPOCH
                        for ins in r:
                            ins.then_inc(dsem[op.dkey][ep], 16)
                    elif op.signaled:
                        r.then_inc(esem[e][op.sigval // EPOCH], 1)
                if e == "sync":
                    for d in final_waits:
                        wait(d)

            block.tensor(lambda eng: run("tensor", eng))
            block.vector(lambda eng: run("vector", eng))
            block.scalar(lambda eng: run("scalar", eng))
            block.gpsimd(lambda eng: run("gpsimd", eng))
            block.sync(lambda eng: run("sync", eng))


class Ring:
    def __init__(self, items):
        self.items = items
        self.i = 0

    def next(self):
        it = self.items[self.i % len(self.items)]
        self.i += 1
        return it


def _cols(v):
    v = np.asarray(v, np.float32)
    return np.ascontiguousarray(v.reshape(-1, 128).T)


def _fm(seg):
    return np.ascontiguousarray(seg.T.reshape(KC, 128, seg.shape[0]).transpose(1, 0, 2))


def _tile_w(Wm, kc):
    K, M = Wm.shape
    return np.ascontiguousarray(Wm.reshape(kc, 128, M // 128, 128).transpose(2, 1, 0, 3))


class CstPack:
    def __init__(self):
        self.parts = []
        self.off = {}
        self.n = 0

    def put(self, name, arr):
        arr = np.asarray(arr, np.float32)
        assert arr.shape[0] == 128
        arr = arr.reshape(128, -1)
        self.off[name] = self.n
        self.n += arr.shape[1]
        self.parts.append(arr)

    def build(self):
        return np.ascontiguousarray(np.concatenate(self.parts, axis=1))


class Dense:
    def __init__(self, nc, P, st, cst_ap, ncst, coff):
        self.nc, self.P, self.st, self.coff = nc, P, st, coff
        sb = self.sb
        self.cst = sb("cst_sb", [128, ncst], F32)
        self.Bcst = Buf("cst")
        P.dma("sync", [(self.cst[:], cst_ap)], "cst", writes=[self.Bcst])
        self.ones = sb("ones", [128, 128], BF16)
        self.Bones = Buf("ones")
        P.add("vector", lambda e: e.memset(self.ones[:], 1.0), writes=[self.Bones])
        self.banks = Ring([(st.enter_context(nc.psum_tensor("pb%d" % i, [128, 2 * W], F32)), Buf("pb%d" % i))
                           for i in range(4)])
        self.wslots = Ring([(sb("ws%d" % i, [128, KC, 128], BF16), Buf("ws%d" % i), "ws%d" % i) for i in range(5)])
        self.wdslots = Ring([(sb("wd%d" % i, [128, PPG, 128], BF16), Buf("wd%d" % i), "wd%d" % i) for i in range(3)])
        self.x = sb("x", [128, KC, WX], F32)
        self.Bx = [Buf("x%d" % j) for j in range(KC)]
        self.h = sb("h", [128, KC, WX], BF16)
        self.Bh = Buf("h")
        self.sq = [(sb("sq%d" % i, [128, WX], BF16), Buf("sq%d" % i)) for i in range(2)]
        self.rstd = sb("rstd", [128, WX], F32)
        self.Brstd = Buf("rstd")
        self.tmp = [(sb("tmp%d" % i, [128, WX], F32), Buf("tmp%d" % i)) for i in range(3)]
        self.hid = sb("hid", [128, PPG, WX], BF16)
        self.Bhid = [Buf("hid%d" % j) for j in range(PPG)]
        self.eb = Ring([(sb("e%d" % i, [128, WX + 2], F32), Buf("e%d" % i)) for i in range(4)])
        self.yb = Ring([(sb("y%d" % i, [128, WX], F32), Buf("y%d" % i)) for i in range(4)])
        self.fst = sb("fst", [128, NHT, 2], F32)
        self.Bfst = [Buf("fst%d" % j) for j in range(NHT)]
        Bf = self.Bfst
        P.add("vector", lambda e: e.memset(self.fst[:], 0.0), writes=Bf)

    def sb(self, name, shape, dt):
        return self.st.enter_context(self.nc.sbuf_tensor(name, shape, dt))

    def c(self, name, j=0, n=1):
        o = self.coff[name] + j
        return self.cst[:, o:o + n]

    def load_w(self, dram_tile):
        slot, B, key = self.wslots.next()
        self.P.dma("gpsimd", [(slot[:], dram_tile)], key, writes=[B])
        return slot, B

    def load_wd(self, dram_tile):
        slot, B, key = self.wdslots.next()
        self.P.dma("gpsimd", [(slot[:], dram_tile)], key, writes=[B])
        return slot, B

    @staticmethod
    def mm(e, ps, lhsT, rhs, Wc, start, stop):
        w0 = min(Wc, W)
        ins = e.matmul(ps[:, 0:w0], lhsT=lhsT, rhs=rhs[:, 0:w0], start=start, stop=stop)
        if Wc > W:
            ins = e.matmul(ps[:, W:Wc], lhsT=lhsT, rhs=rhs[:, W:Wc], start=start, stop=stop)
        return ins

    def mm_group(self, ps, wslot, act, nk, Wc, reads, Bps):
        def fn(e):
            ins = None
            for k in range(nk):
                ins = self.mm(e, ps, wslot[:, k, :], act[:, k, :], Wc, k == 0, k == nk - 1)
            return ins
        return self.P.add("tensor", fn, reads=reads, writes=[Bps])

    def colsum_rstd(self, srcs, Wc, scale, out_ap, Bout):
        P = self.P
        ps, Bps = self.banks.next()
        n = len(srcs)
        for j, (src, Bsrc) in enumerate(srcs):
            sq, Bsq = self.sq[j % 2]
            P.add("scalar", lambda e, sq=sq, src=src: e.activation(out=sq[:, 0:Wc], in_=src, func=AF.Square),
                  reads=[Bsrc], writes=[Bsq])
            P.add("tensor", lambda e, sq=sq, j=j: self.mm(e, ps, self.ones[:], sq, Wc, j == 0, j == n - 1),
                  reads=[Bsq, self.Bones], writes=[Bps])
        t, Bt = self.tmp[0]
        P.add("scalar", lambda e: e.activation(out=t[:, 0:Wc], in_=ps[:, 0:Wc], func=AF.Sqrt, bias=self.c("eps"),
                                               scale=scale),
              reads=[Bps, self.Bcst], writes=[Bt])
        P.add("vector", lambda e: e.reciprocal(out=out_ap, in_=t[:, 0:Wc]), reads=[Bt], writes=[Bout])

    def rmsnorm_to_h(self, gname, Wc):
        P = self.P
        self.colsum_rstd([(self.x[:, j, 0:Wc], self.Bx[j]) for j in range(KC)], Wc, 1.0 / D,
                         self.rstd[:, 0:Wc], self.Brstd)
        for j in range(KC):
            P.add("vector", lambda e, j=j: e.scalar_tensor_tensor(
                out=self.h[:, j, 0:Wc], in0=self.x[:, j, 0:Wc], scalar=self.c(gname, j), in1=self.rstd[:, 0:Wc],
                op0=ALU.mult, op1=ALU.mult), reads=[self.Bx[j], self.Brstd, self.Bcst], writes=[self.Bh])

    def ffn(self, layer, Wc, w_up, w_dn, gname, nhalo):
        P = self.P
        self.rmsnorm_to_h(gname, Wc)
        dww, dwb = "fdw%d" % layer, "fdb%d" % layer
        for g in range(G):
            for jj in range(PPG):
                pj = g * PPG + jj
                ys = []
                for half in range(2):
                    t = pj + half * NPAIR
                    ws, Bw = self.load_w(w_up[t])
                    ps, Bps = self.banks.next()
                    self.mm_group(ps, ws, self.h, KC, Wc, [Bw, self.Bh], Bps)
                    eb, Be = self.eb.next()
                    yb, By = self.yb.next()
                    st_ap = self.fst[:, t, :]
                    P.add("vector", lambda e, eb=eb, st_ap=st_ap: e.tensor_copy(out=eb[:, 0:2], in_=st_ap),
                          reads=[self.Bfst[t]], writes=[Be])
                    P.add("scalar", lambda e, eb=eb, ps=ps: e.activation(out=eb[:, 2:2 + Wc], in_=ps[:, 0:Wc],
                                                                         func=AF.Copy),
                          reads=[Bps], writes=[Be])
                    if nhalo:
                        P.add("vector", lambda e, eb=eb: e.tensor_scalar(
                            out=eb[:, 2:2 + nhalo], in0=eb[:, 2:2 + nhalo], scalar1=self.c("flag"), scalar2=None,
                            op0=ALU.mult), reads=[Be, self.Bcst], writes=[Be])
                    P.add("vector", lambda e, eb=eb, st_ap=st_ap: e.tensor_copy(out=st_ap, in_=eb[:, Wc:Wc + 2]),
                          reads=[Be], writes=[self.Bfst[t]])
                    P.add("vector", lambda e, eb=eb, yb=yb, t=t: e.tensor_scalar(
                        out=yb[:, 0:Wc], in0=eb[:, 2:2 + Wc], scalar1=self.c(dww, 3 * t + 2), scalar2=self.c(dwb, t),
                        op0=ALU.mult, op1=ALU.add), reads=[Be, self.Bcst], writes=[By])
                    for tap in (1, 0):
                        P.add("vector", lambda e, eb=eb, yb=yb, t=t, tap=tap: e.scalar_tensor_tensor(
                            out=yb[:, 0:Wc], in0=eb[:, tap:tap + Wc], scalar=self.c(dww, 3 * t + tap), in1=yb[:, 0:Wc],
                            op0=ALU.mult, op1=ALU.add), reads=[Be, By, self.Bcst], writes=[By])
                    ys.append((yb, By))
                (yg, Byg), (yv, Byv) = ys
                P.add("scalar", lambda e, yg=yg: e.activation(out=yg[:, 0:Wc], in_=yg[:, 0:Wc], func=AF.Silu),
                      reads=[Byg], writes=[Byg])
                P.add("vector", lambda e, yg=yg, yv=yv, jj=jj: e.tensor_tensor(
                    out=self.hid[:, jj, 0:Wc], in0=yg[:, 0:Wc], in1=yv[:, 0:Wc], op=ALU.mult),
                    reads=[Byg, Byv], writes=[self.Bhid[jj]])
            for m in range(KC):
                wd, Bwd = self.load_wd(w_dn[g, m])
                ps, Bps = self.banks.next()
                self.mm_group(ps, wd, self.hid, PPG, Wc, [Bwd] + self.Bhid, Bps)
                P.add("vector", lambda e, ps=ps, m=m: e.tensor_tensor(
                    out=self.x[:, m, 0:Wc], in0=ps[:, 0:Wc], in1=self.x[:, m, 0:Wc], op=ALU.add),
                    reads=[Bps, self.Bx[m]], writes=[self.Bx[m]])


def _cst_common(cp, inputs):
    cp.put("eps", np.full((128, 1), EPS, np.float32))
    for L in range(2):
        cp.put("fdw%d" % L, np.ascontiguousarray(
            np.asarray(inputs["ffn_dw_w"][L], np.float32).T.reshape(NHT, 128, 3).transpose(1, 0, 2)))
        cp.put("fdb%d" % L, _cols(inputs["ffn_dw_b"][L]))
        cp.put("fng%d" % L, _cols(inputs["ffn_norm"][L]))
        cp.put("mng%d" % L, _cols(inputs["mix_norm"][L]))
    cp.put("fin", _cols(inputs["final_norm_g"]))


def build_A(ncst, coff):
    nc = bass.Bass("TRN2", target_bir_lowering=False)
    xt0 = nc.dram_tensor("xt0", [128, KC, WX], F32, kind="ExternalInput").ap()
    xt = nc.dram_tensor("xt", [NT - 1, 128, KC, W], F32, kind="ExternalInput").ap()
    cst = nc.dram_tensor("cst", [128, ncst], F32, kind="ExternalInput").ap()
    w_in = nc.dram_tensor("w_in", [32, 128, KC, 128], F32, kind="ExternalInput").ap()
    w_out = nc.dram_tensor("w_out", [16, 128, KC, 128], F32, kind="ExternalInput").ap()
    w_up = nc.dram_tensor("w_up", [NHT, 128, KC, 128], F32, kind="ExternalInput").ap()
    w_dn = nc.dram_tensor("w_dn", [G, 16, 128, PPG, 128], F32, kind="ExternalInput").ap()
    x1h = nc.dram_tensor("x1h", [128, KC, 2], F32, kind="ExternalOutput").ap()
    x1t = nc.dram_tensor("x1t", [NT, 128, KC, W], F32, kind="ExternalOutput").ap()
    h1t = nc.dram_tensor("h1t", [NT, 128, KC, W], BF16, kind="ExternalOutput").ap()
    P = Prog(nc)
    finals = []
    with contextlib.ExitStack() as st:
        dn = Dense(nc, P, st, cst, ncst, coff)
        sb = dn.sb
        uext = sb("uext", [128, KC, WX + 30], BF16)
        Bu = [Buf("u%d" % j) for j in range(KC)]
        cv = sb("cv", [128, KC, WX], F32)
        Bcv = [Buf("cv%d" % j) for j in range(KC)]
        sg = [(sb("sg%d" % i, [128, WX], F32), Buf("sg%d" % i)) for i in range(2)]
        mu = sb("mu", [128, WX], F32)
        Bmu = Buf("mu")
        nmr = sb("nmr", [128, WX], F32)
        Bnmr = Buf("nmr")
        h1s, Bh1s = dn.h, dn.Bh
        P.add("vector", lambda e: e.memset(uext[:, :, 0:30], 0.0), writes=Bu)
        hso = sb("hso", [128, KC, 2], F32)
        Bhso = Buf("hso")
        ident = sb("ident", [128, 128], BF16)
        Bident = Buf("ident")
        P.add("vector", lambda e: e.tensor_copy(out=ident[:], in_=dn.c("ident", 0, 128)), reads=[dn.Bcst],
              writes=[Bident])
        dgs = Ring([(sb("dg%d" % i, [128, CK, 128], BF16), Buf("dg%d" % i)) for i in range(2)])

        tiles = [(0, WX, HALO)] + [(i, W, 0) for i in range(1, NT)]

        def tile_body(ti, Wc, nh):
            if nh:
                P.dma("sync", [(dn.x[:, :, 0:Wc], xt0)], "xin", writes=dn.Bx)
            else:
                P.dma("sync", [(dn.x[:, :, 0:Wc], xt[ti - 1])], "xin", writes=dn.Bx)
            dn.rmsnorm_to_h("mng0", Wc)
            for m in range(KC):
                wa, Bwa = dn.load_w(w_in[m])
                pa, Bpa = dn.banks.next()
                dn.mm_group(pa, wa, dn.h, KC, Wc, [Bwa, dn.Bh], Bpa)
                wg, Bwg = dn.load_w(w_in[m + KC])
                pg, Bpg = dn.banks.next()
                dn.mm_group(pg, wg, dn.h, KC, Wc, [Bwg, dn.Bh], Bpg)
                s, Bs = sg[m % 2]
                P.add("scalar", lambda e, s=s, pg=pg, m=m: e.activation(
                    out=s[:, 0:Wc], in_=pg[:, 0:Wc], func=AF.Sigmoid, bias=dn.c("b_in", KC + m)),
                    reads=[Bpg, dn.Bcst], writes=[Bs])
                P.add("vector", lambda e, s=s, pa=pa, m=m: e.scalar_tensor_tensor(
                    out=uext[:, m, 30:30 + Wc], in0=pa[:, 0:Wc], scalar=dn.c("b_in", m), in1=s[:, 0:Wc],
                    op0=ALU.add, op1=ALU.mult), reads=[Bpa, Bs, dn.Bcst], writes=[Bu[m]])
            if nh:
                P.add("vector", lambda e: e.tensor_scalar(
                    out=uext[:, :, 30:30 + nh], in0=uext[:, :, 30:30 + nh], scalar1=dn.c("flag"), scalar2=None,
                    op0=ALU.mult), reads=Bu + [dn.Bcst], writes=Bu)
            for m in range(KC):
                dgt, Bdg = dgs.next()
                for tap in range(CK):
                    P.add("vector", lambda e, dgt=dgt, m=m, tap=tap: e.tensor_scalar(
                        out=dgt[:, tap, :], in0=ident[:], scalar1=dn.c("cdw", m * CK + tap), scalar2=None,
                        op0=ALU.mult), reads=[Bident, dn.Bcst], writes=[Bdg])
                ps, Bps = dn.banks.next()

                def convmm(e, dgt=dgt, m=m, ps=ps):
                    ins = None
                    for tap in range(CK):
                        ins = dn.mm(e, ps, dgt[:, tap, :], uext[:, m, tap:tap + Wc], Wc, tap == 0, tap == CK - 1)
                    return ins
                P.add("tensor", convmm, reads=[Bdg, Bu[m]], writes=[Bps])
                P.add("scalar", lambda e, m=m, ps=ps: e.activation(out=cv[:, m, 0:Wc], in_=ps[:, 0:Wc],
                                                                   func=AF.Identity, bias=dn.c("cdb", m)),
                      reads=[Bps, dn.Bcst], writes=[Bcv[m]])
            P.add("vector", lambda e: e.tensor_copy(out=uext[:, :, 0:30], in_=uext[:, :, Wc:Wc + 30]),
                  reads=Bu, writes=Bu)
            ps_s, Bps_s = dn.banks.next()
            ps_q, Bps_q = dn.banks.next()
            for j in range(KC):
                sq, Bsq = dn.sq[0]
                cb, Bcb = dn.sq[1]
                P.add("scalar", lambda e, j=j, cb=cb: e.activation(out=cb[:, 0:Wc], in_=cv[:, j, 0:Wc], func=AF.Copy),
                      reads=[Bcv[j]], writes=[Bcb])
                P.add("scalar", lambda e, j=j, sq=sq: e.activation(out=sq[:, 0:Wc], in_=cv[:, j, 0:Wc],
                                                                   func=AF.Square),
                      reads=[Bcv[j]], writes=[Bsq])
                P.add("tensor", lambda e, j=j, cb=cb: dn.mm(e, ps_s, dn.ones[:], cb, Wc, j == 0, j == KC - 1),
                      reads=[Bcb, dn.Bones], writes=[Bps_s])
                P.add("tensor", lambda e, j=j, sq=sq: dn.mm(e, ps_q, dn.ones[:], sq, Wc, j == 0, j == KC - 1),
                      reads=[Bsq, dn.Bones], writes=[Bps_q])
            t1, Bt1 = dn.tmp[1]
            t2, Bt2 = dn.tmp[2]
            t0, Bt0 = dn.tmp[0]
            P.add("vector", lambda e: e.tensor_scalar(out=mu[:, 0:Wc], in0=ps_s[:, 0:Wc], scalar1=1.0 / D, scalar2=None,
                                                      op0=ALU.mult), reads=[Bps_s], writes=[Bmu])
            P.add("vector", lambda e: e.tensor_tensor(out=t1[:, 0:Wc], in0=mu[:, 0:Wc], in1=mu[:, 0:Wc], op=ALU.mult),
                  reads=[Bmu], writes=[Bt1])
            P.add("vector", lambda e: e.scalar_tensor_tensor(out=t2[:, 0:Wc], in0=ps_q[:, 0:Wc], scalar=1.0 / D,
                                                             in1=t1[:, 0:Wc], op0=ALU.mult, op1=ALU.subtract),
                  reads=[Bps_q, Bt1], writes=[Bt2])
            P.add("scalar", lambda e: e.activation(out=t0[:, 0:Wc], in_=t2[:, 0:Wc], func=AF.Sqrt, bias=dn.c("eps"),
                                                   scale=1.0), reads=[Bt2, dn.Bcst], writes=[Bt0])
            P.add("vector", lambda e: e.reciprocal(out=dn.rstd[:, 0:Wc], in_=t0[:, 0:Wc]), reads=[Bt0],
                  writes=[dn.Brstd])
            P.add("vector", lambda e: e.scalar_tensor_tensor(out=nmr[:, 0:Wc], in0=mu[:, 0:Wc], scalar=-1.0,
                                                             in1=dn.rstd[:, 0:Wc], op0=ALU.mult, op1=ALU.mult),
                  reads=[Bmu, dn.Brstd], writes=[Bnmr])
            for j in range(KC):
                P.add("vector", lambda e, j=j: e.tensor_tensor(out=cv[:, j, 0:Wc], in0=cv[:, j, 0:Wc],
                                                               in1=dn.rstd[:, 0:Wc], op=ALU.mult),
                      reads=[Bcv[j], dn.Brstd], writes=[Bcv[j]])
                P.add("vector", lambda e, j=j: e.tensor_tensor(out=cv[:, j, 0:Wc], in0=cv[:, j, 0:Wc],
                                                               in1=nmr[:, 0:Wc], op=ALU.add),
                      reads=[Bcv[j], Bnmr], writes=[Bcv[j]])
                P.add("scalar", lambda e, j=j: e.activation(out=dn.h[:, j, 0:Wc], in_=cv[:, j, 0:Wc], func=AF.Silu,
                                                            bias=dn.c("lnb", j), scale=dn.c("lng", j)),
                      reads=[Bcv[j], dn.Bcst], writes=[dn.Bh])
            for m in range(KC):
                ws, Bw = dn.load_w(w_out[m])
                ps, Bps = dn.banks.next()
                dn.mm_group(ps, ws, dn.h, KC, Wc, [Bw, dn.Bh], Bps)
                P.add("vector", lambda e, ps=ps, m=m: e.scalar_tensor_tensor(
                    out=dn.x[:, m, 0:Wc], in0=ps[:, 0:Wc], scalar=dn.c("b_out", m), in1=dn.x[:, m, 0:Wc],
                    op0=ALU.add, op1=ALU.add), reads=[Bps, dn.Bx[m], dn.Bcst], writes=[dn.Bx[m]])
            dn.ffn(0, Wc, w_up, w_dn, "fng0", nh)
            if nh:
                P.add("vector", lambda e: e.tensor_copy(out=hso[:], in_=dn.x[:, :, nh - 2:nh]), reads=dn.Bx,
                      writes=[Bhso])
                finals.append(P.dma("sync", [(x1h, hso[:])], "x1ho", reads=[Bhso]))
            finals.append(P.dma("sync", [(x1t[ti], dn.x[:, :, nh:nh + W])], "x1o", reads=dn.Bx))
            dn.colsum_rstd([(dn.x[:, j, 0:Wc], dn.Bx[j]) for j in range(KC)], Wc, 1.0 / D,
                           dn.rstd[:, 0:Wc], dn.Brstd)
            for j in range(KC):
                P.add("vector", lambda e, j=j: e.scalar_tensor_tensor(
                    out=h1s[:, j, 0:Wc], in0=dn.x[:, j, 0:Wc], scalar=dn.c("mng1", j), in1=dn.rstd[:, 0:Wc],
                    op0=ALU.mult, op1=ALU.mult), reads=[dn.Bx[j], dn.Brstd, dn.Bcst], writes=[Bh1s])
            finals.append(P.dma("sync", [(h1t[ti], h1s[:, :, nh:nh + W])], "h1o", reads=[Bh1s]))
        for (ti_, Wc_, nh_) in tiles:
            tile_body(ti_, Wc_, nh_)
        P.emit(final_waits=finals)
    return nc


def _bucket_planes():
    kk = np.arange(128)[:, None]
    m = np.arange(1024)[None, :]
    rel = kk - m + 384
    allowed = (kk // 64) <= (m // 64 - 6)
    n = np.abs(rel)
    nf = np.maximum(n, 1).astype(np.float32)
    large = 8 + (np.log(nf / np.float32(8)) / np.float32(math.log(16.0)) * np.float32(8)).astype(np.int32)
    large = np.minimum(large, 15)
    bucket = np.where(rel > 0, 16, 0) + np.where(n < 8, n, large)
    used = sorted(set(bucket[allowed].tolist()))
    planes = np.zeros((len(used) + 1, 128, 1024), np.float32)
    planes[0] = np.where(allowed, 0.0, NEG)
    for i, b in enumerate(used):
        planes[i + 1] = ((bucket == b) & allowed).astype(np.float32)
    return used, planes


def build_B(nB, boff, used, lambda_init):
    nc = bass.Bass("TRN2", target_bir_lowering=False)
    hall = nc.dram_tensor("hall", [NCORE * NT, 128, KC, W], BF16, kind="ExternalInput").ap()
    cstb = nc.dram_tensor("cstb", [128, nB], F32, kind="ExternalInput").ap()
    wqkv = nc.dram_tensor("wqkv", [2, 3, 128, KC, 128], F32, kind="ExternalInput").ap()
    planes = nc.dram_tensor("planes", [len(used) + 1, 128, 1024], F32, kind="ExternalInput").ap()
    oT = nc.dram_tensor("oT", [2, 128, S], BF16, kind="ExternalOutput").ap()
    P = Prog(nc)
    finals = []
    NQ = S // W
    with contextlib.ExitStack() as st:
        def sb(name, shape, dt):
            return st.enter_context(nc.sbuf_tensor(name, shape, dt))
        cst = sb("cst", [128, nB], F32)
        Bcst = Buf("cst")
        P.dma("sync", [(cst[:], cstb)], "cst", writes=[Bcst])

        def c(name, j=0, n=1):
            o = boff[name] + j
            return cst[:, o:o + n]
        ones = sb("ones", [128, 128], BF16)
        Bones = Buf("ones")
        P.add("vector", lambda e: e.memset(ones[:], 1.0), writes=[Bones])
        sbanks = Ring([(st.enter_context(nc.psum_tensor("sb%d" % i, [128, 512], F32)), Buf("sb%d" % i))
                       for i in range(4)])
        obank = [(st.enter_context(nc.psum_tensor("ob%d" % i, [128, 512], F32)), Buf("ob%d" % i)) for i in range(2)]
        zbank = [(st.enter_context(nc.psum_tensor("zb%d" % i, [128, 512], F32)), Buf("zb%d" % i)) for i in range(2)]
        Kt = sb("Kt", [128, S], BF16)
        BK = [Buf("K%d" % i) for i in range(NQ)]
        Vt = sb("Vt", [128, S // 128, 128], BF16)
        BV = [Buf("V%d" % i) for i in range(NQ)]
        qts = Ring([(sb("qt%d" % i, [128, W], BF16), Buf("qt%d" % i)) for i in range(2)])
        hts = Ring([(sb("ht%d" % i, [128, KC, W], BF16), Buf("ht%d" % i), "ht%d" % i) for i in range(2)])
        wq = [(sb("wq%d" % i, [128, KC, 128], BF16), Buf("wq%d" % i)) for i in range(3)]
        strip = sb("strip", [128, 1024], F32)
        Bstrip = Buf("strip")
        pls = Ring([(sb("pl%d" % i, [128, 1024], F32), Buf("pl%d" % i), "pl%d" % i) for i in range(2)])
        pts = Ring([(sb("pt%d" % i, [128, W], BF16), Buf("pt%d" % i)) for i in range(6)])
        tns = Ring([(sb("tn%d" % i, [128, W], F32), Buf("tn%d" % i)) for i in range(2)])
        ep = [(sb("ep%d" % i, [128, W], F32), Buf("ep%d" % i)) for i in range(5)]
        sqb = sb("sqb", [128, W], BF16)
        Bsqb = Buf("sqb")
        ons = Ring([(sb("on%d" % i, [128, W], BF16), Buf("on%d" % i), "on%d" % i) for i in range(2)])
        zacc = [(sb("zacc%d" % i, [128, W], F32), Buf("zacc%d" % i)) for i in range(2)]
        ones32 = sb("ones32", [128, 128], F32)
        Bones32 = Buf("ones32")
        P.add("vector", lambda e: e.memset(ones32[:], 1.0), writes=[Bones32])
        lamt = sb("lamt", [128, 8], F32)
        Blam = Buf("lam")
        lprod = sb("lprod", [128, 2, 64], F32)
        Blprod = Buf("lprod")
        P.add("vector", lambda e: e.tensor_tensor(out=lprod[:, 0, :], in0=c("lam", 0, 64), in1=c("lam", 64, 64),
                                                  op=ALU.mult), reads=[Bcst], writes=[Blprod])
        P.add("vector", lambda e: e.tensor_tensor(out=lprod[:, 1, :], in0=c("lam", 128, 64), in1=c("lam", 192, 64),
                                                  op=ALU.mult), reads=[Bcst], writes=[Blprod])
        P.add("vector", lambda e: e.tensor_reduce(out=lamt[:, 0:2], in_=lprod[:], axis=AX.X, op=ALU.add),
              reads=[Blprod], writes=[Blam])
        P.add("scalar", lambda e: e.activation(out=lamt[:, 2:4], in_=lamt[:, 0:2], func=AF.Exp), reads=[Blam],
              writes=[Blam])
        P.add("vector", lambda e: e.scalar_tensor_tensor(out=lamt[:, 4:5], in0=lamt[:, 3:4], scalar=-lambda_init,
                                                         in1=lamt[:, 2:3], op0=ALU.add, op1=ALU.subtract),
              reads=[Blam], writes=[Blam])

        def head_body(hd):
            for i in range(3):
                P.dma("gpsimd", [(wq[i][0][:], wqkv[hd, i])], "wq%d" % i, writes=[wq[i][1]])
            pl, Bpl, kpl = pls.next()
            P.dma("sync", [(pl[:], planes[0])], kpl, writes=[Bpl])
            P.add("vector", lambda e, pl=pl: e.tensor_copy(out=strip[:], in_=pl[:]), reads=[Bpl], writes=[Bstrip])
            for bi, b in enumerate(used):
                pl, Bpl, kpl = pls.next()
                P.dma("sync", [(pl[:], planes[bi + 1])], kpl, writes=[Bpl])
                P.add("vector", lambda e, pl=pl, b=b: e.scalar_tensor_tensor(
                    out=strip[:], in0=pl[:], scalar=c("tab", hd * 32 + b), in1=strip[:], op0=ALU.mult, op1=ALU.add),
                    reads=[Bpl, Bstrip, Bcst], writes=[Bstrip])
            def proj(i):
                ht, Bht, kht = hts.next()
                P.dma("sync", [(ht[:], hall[i])], kht, writes=[Bht])
                ps, Bps = sbanks.next()
                P.add("tensor", lambda e, ps=ps, ht=ht: [e.matmul(ps[:], lhsT=wq[1][0][:, k, :], rhs=ht[:, k, :],
                                                                  start=(k == 0), stop=(k == KC - 1))
                                                         for k in range(KC)][-1],
                      reads=[wq[1][1], Bht], writes=[Bps])
                P.add("scalar", lambda e, ps=ps, i=i: e.activation(out=Kt[:, i * W:(i + 1) * W], in_=ps[:],
                                                                   func=AF.Copy),
                      reads=[Bps], writes=[BK[i]])
                ps, Bps = sbanks.next()
                qt, Bqt = qts.next()
                P.add("tensor", lambda e, ps=ps, ht=ht: [e.matmul(ps[:], lhsT=wq[0][0][:, k, :], rhs=ht[:, k, :],
                                                                  start=(k == 0), stop=(k == KC - 1))
                                                         for k in range(KC)][-1],
                      reads=[wq[0][1], Bht], writes=[Bps])
                P.add("scalar", lambda e, ps=ps, qt=qt: e.activation(out=qt[:], in_=ps[:], func=AF.Copy, scale=0.125),
                      reads=[Bps], writes=[Bqt])
                ps, Bps = sbanks.next()

                def vmm(e, ps=ps, ht=ht):
                    ins = None
                    for sub in range(4):
                        for k in range(KC):
                            ins = e.matmul(ps[:, sub * 128:(sub + 1) * 128], lhsT=ht[:, k, sub * 128:(sub + 1) * 128],
                                           rhs=wq[2][0][:, k, :], start=(k == 0), stop=(k == KC - 1))
                    return ins
                P.add("tensor", vmm, reads=[wq[2][1], Bht], writes=[Bps])
                P.add("vector", lambda e, ps=ps, i=i: e.tensor_copy(
                    out=Vt[:, 4 * i:4 * i + 4, :], in_=ps[:].rearrange("p (a b) -> p a b", a=4)),
                    reads=[Bps], writes=[BV[i]])
                return qt, Bqt

            def scores(i, j, qt, Bqt):
                d = 128 * j - W * i
                c0 = max(d, 0)
                out = []
                for comp in range(2):
                    ps, Bps = sbanks.next()
                    lo = 64 * comp
                    P.add("tensor", lambda e, ps=ps, j=j, lo=lo, qt=qt, c0=c0: e.matmul(
                        ps[:, c0:W], lhsT=Kt[lo:lo + 64, j * 128:(j + 1) * 128], rhs=qt[lo:lo + 64, c0:W],
                        start=True, stop=True), reads=[BK[j // 4], Bqt], writes=[Bps])
                    out.append((ps, Bps))
                return out

            def expo(i, j, sc):
                d = 128 * j - W * i
                c0 = max(d, 0)
                near = j >= 4 * i - 1
                pcs = []
                for comp in range(2):
                    ps, Bps = sc[comp]
                    pt, Bpt = pts.next()
                    if near:
                        tn, Btn = tns.next()
                        so = 384 - d
                        P.add("vector", lambda e, ps=ps, tn=tn, so=so, c0=c0: e.tensor_tensor(
                            out=tn[:, c0:W], in0=ps[:, c0:W], in1=strip[:, so + c0:so + W], op=ALU.add),
                            reads=[Bps, Bstrip], writes=[Btn])
                        P.add("scalar", lambda e, tn=tn, pt=pt, c0=c0: e.activation(
                            out=pt[:, c0:W], in_=tn[:, c0:W], func=AF.Exp), reads=[Btn], writes=[Bpt])
                    else:
                        P.add("scalar", lambda e, ps=ps, pt=pt: e.activation(
                            out=pt[:], in_=ps[:], func=AF.Exp, bias=c("farb", hd)), reads=[Bps, Bcst],
                            writes=[Bpt])
                    pcs.append((pt, Bpt))
                return pcs

            def pv(i, j, pcs):
                d = 128 * j - W * i
                c0 = max(d, 0)
                nk = 4 * (i + 1)
                for comp in range(2):
                    pt, Bpt = pcs[comp]
                    ob, Bob = obank[comp]
                    P.add("tensor", lambda e, pt=pt, ob=ob, j=j, c0=c0, nk=nk: e.matmul(
                        ob[:, c0:W], lhsT=Vt[:, j, :], rhs=pt[:, c0:W], start=(j == 0), stop=(j == nk - 1)),
                        reads=[BV[j // 4], Bpt], writes=[Bob])
                    if comp == 0:
                        za, Bza = zacc[0]
                        if j == 0:
                            P.add("vector", lambda e, pt=pt, za=za: e.tensor_copy(out=za[:], in_=pt[:]),
                                  reads=[Bpt], writes=[Bza])
                        else:
                            P.add("vector", lambda e, pt=pt, za=za, c0=c0: e.tensor_tensor(
                                out=za[:, c0:W], in0=za[:, c0:W], in1=pt[:, c0:W], op=ALU.add),
                                reads=[Bpt, Bza], writes=[Bza])
                    else:
                        zb, Bzb = zbank[1]
                        P.add("tensor", lambda e, pt=pt, zb=zb, j=j, c0=c0, nk=nk: e.matmul(
                            zb[:, c0:W], lhsT=ones[:], rhs=pt[:, c0:W], start=(j == 0), stop=(j == nk - 1)),
                            reads=[Bones, Bpt], writes=[Bzb])

            (r0, Br0), (r1, Br1), (t0, Bt0), (t1, Bt1), (oo, Boo) = ep

            def epi1():
                for comp in range(1):
                    za, Bza = zacc[comp]
                    zb, Bzb = zbank[comp]
                    P.add("tensor", lambda e, za=za, zb=zb: e.matmul(zb[:], lhsT=ones32[:], rhs=za[:], start=True,
                                                                    stop=True),
                          reads=[Bones32, Bza], writes=[Bzb])
                P.add("vector", lambda e: e.reciprocal(out=r0[:], in_=zbank[0][0][:]), reads=[zbank[0][1]],
                      writes=[Br0])
                P.add("vector", lambda e: e.reciprocal(out=r1[:], in_=zbank[1][0][:]), reads=[zbank[1][1]],
                      writes=[Br1])
                P.add("vector", lambda e: e.tensor_tensor(out=t0[:], in0=obank[0][0][:], in1=r0[:], op=ALU.mult),
                      reads=[obank[0][1], Br0], writes=[Bt0])
                P.add("vector", lambda e: e.tensor_tensor(out=t1[:], in0=obank[1][0][:], in1=r1[:], op=ALU.mult),
                      reads=[obank[1][1], Br1], writes=[Bt1])
                P.add("vector", lambda e: e.scalar_tensor_tensor(out=oo[:], in0=t1[:], scalar=lamt[:, 4:5], in1=t0[:],
                                                                 op0=ALU.mult, op1=ALU.add),
                      reads=[Bt0, Bt1, Blam], writes=[Boo])
                P.add("scalar", lambda e: e.activation(out=sqb[:], in_=oo[:], func=AF.Square), reads=[Boo],
                      writes=[Bsqb])

            def epi2(i):
                ps, Bps = sbanks.next()
                P.add("tensor", lambda e, ps=ps: e.matmul(ps[:], lhsT=ones[:], rhs=sqb[:], start=True, stop=True),
                      reads=[Bones, Bsqb], writes=[Bps])
                P.add("scalar", lambda e, ps=ps: e.activation(out=r0[:], in_=ps[:], func=AF.Sqrt, bias=c("eps2"),
                                                              scale=1.0 / (128 * (1.0 - lambda_init) ** 2)),
                      reads=[Bps, Bcst], writes=[Br0])
                P.add("vector", lambda e: e.reciprocal(out=r1[:], in_=r0[:]), reads=[Br0], writes=[Br1])
                on, Bon, kon = ons.next()
                P.add("vector", lambda e, on=on: e.scalar_tensor_tensor(out=on[:], in0=oo[:], scalar=c("subg"),
                                                                        in1=r1[:], op0=ALU.mult, op1=ALU.mult),
                      reads=[Boo, Br1, Bcst], writes=[Bon])
                store_o(hd, i, on, Bon, kon)

            cur = proj(0)
            for i in range(NQ):
                qt, Bqt = cur
                nk = 4 * (i + 1)
                sc = scores(i, 0, qt, Bqt)
                for j in range(nk):
                    sc_next = scores(i, j + 1, qt, Bqt) if j + 1 < nk else None
                    pcs = expo(i, j, sc)
                    pv(i, j, pcs)
                    sc = sc_next
                epi1()
                if i + 1 < NQ:
                    cur = proj(i + 1)
                epi2(i)

        def store_o(hd, i, on, Bon, kon):
            finals.append(P.dma("sync", [(oT[hd, :, i * W:(i + 1) * W], on[:])], kon + "o", reads=[Bon]))

        for hd_ in range(2):
            head_body(hd_)
        P.emit(final_waits=finals)
    return nc


def build_C(ncst, coff):
    nc = bass.Bass("TRN2", target_bir_lowering=False)
    x1h = nc.dram_tensor("x1h", [128, KC, 2], F32, kind="ExternalInput").ap()
    x1t = nc.dram_tensor("x1t", [NT, 128, KC, W], F32, kind="ExternalInput").ap()
    oh = nc.dram_tensor("oh", [128, KC, 2], BF16, kind="ExternalInput").ap()
    ot = nc.dram_tensor("ot", [NT, 128, KC, W], BF16, kind="ExternalInput").ap()
    cst = nc.dram_tensor("cst", [128, ncst], F32, kind="ExternalInput").ap()
    w_o = nc.dram_tensor("w_o", [16, 128, KC, 128], F32, kind="ExternalInput").ap()
    w_up = nc.dram_tensor("w_up", [NHT, 128, KC, 128], F32, kind="ExternalInput").ap()
    w_dn = nc.dram_tensor("w_dn", [G, 16, 128, PPG, 128], F32, kind="ExternalInput").ap()
    yt = nc.dram_tensor("yt", [NT, 128, KC, W], F32, kind="ExternalOutput").ap()
    P = Prog(nc)
    finals = []
    with contextlib.ExitStack() as st:
        dn = Dense(nc, P, st, cst, ncst, coff)
        sb = dn.sb
        osb = sb("osb", [128, KC, W + 2], BF16)
        Bo = Buf("osb")
        ys = sb("ys", [128, KC, W + 2], F32)
        Bys = Buf("ys")
        hsx = sb("hsx", [128, KC, 2], F32)
        Bhsx = Buf("hsx")
        hsoo = sb("hsoo", [128, KC, 2], BF16)
        Bhsoo = Buf("hsoo")
        tiles = [(0, W + 2, 2)] + [(i, W, 0) for i in range(1, NT)]

        def tile_body(ti, Wc, nh):
            if nh:
                P.dma("sync", [(hsx[:], x1h)], "xhin", writes=[Bhsx])
                P.dma("sync", [(hsoo[:], oh)], "ohin", writes=[Bhsoo])
            P.dma("sync", [(dn.x[:, :, nh:nh + W], x1t[ti])], "xin", writes=dn.Bx)
            P.dma("sync", [(osb[:, :, nh:nh + W], ot[ti])], "oin", writes=[Bo])
            if nh:
                P.add("vector", lambda e: e.tensor_copy(out=dn.x[:, :, 0:2], in_=hsx[:]), reads=[Bhsx] + dn.Bx,
                      writes=dn.Bx)
                P.add("vector", lambda e: e.tensor_copy(out=osb[:, :, 0:2], in_=hsoo[:]), reads=[Bhsoo, Bo],
                      writes=[Bo])
            for m in range(KC):
                ws, Bw = dn.load_w(w_o[m])
                ps, Bps = dn.banks.next()
                dn.mm_group(ps, ws, osb, KC, Wc, [Bw, Bo], Bps)
                P.add("vector", lambda e, ps=ps, m=m: e.tensor_tensor(
                    out=dn.x[:, m, 0:Wc], in0=ps[:, 0:Wc], in1=dn.x[:, m, 0:Wc], op=ALU.add),
                    reads=[Bps, dn.Bx[m]], writes=[dn.Bx[m]])
            dn.ffn(1, Wc, w_up, w_dn, "fng1", nh)
            dn.colsum_rstd([(dn.x[:, j, 0:Wc], dn.Bx[j]) for j in range(KC)], Wc, 1.0 / D,
                           dn.rstd[:, 0:Wc], dn.Brstd)
            for j in range(KC):
                P.add("vector", lambda e, j=j: e.scalar_tensor_tensor(
                    out=ys[:, j, 0:Wc], in0=dn.x[:, j, 0:Wc], scalar=dn.c("fin", j), in1=dn.rstd[:, 0:Wc],
                    op0=ALU.mult, op1=ALU.mult), reads=[dn.Bx[j], dn.Brstd, dn.Bcst], writes=[Bys])
            finals.append(P.dma("sync", [(yt[ti], ys[:, :, nh:nh + W])], "yo", reads=[Bys]))
        for (ti_, Wc_, nh_) in tiles:
            tile_body(ti_, Wc_, nh_)
        P.emit(final_waits=finals)
    return nc


_CACHE = {}
_DBG = {}


def _tile_up(f, L):
    return _tile_w(f("ffn_w_up")[L], KC)


def _tile_dn(f, L):
    Wd = f("ffn_w_down")[L]
    return np.ascontiguousarray(Wd.reshape(G, PPG, 128, KC, 128).transpose(0, 3, 2, 1, 4))


def _phaseA(inputs):
    f = lambda k: np.asarray(inputs[k], np.float32)
    x = f("x")[0]
    cores = list(range(NCORE))

    cpA = CstPack()
    _cst_common(cpA, inputs)
    cpA.put("b_in", _cols(f("conv_b_in")[0]))
    cpA.put("cdw", np.ascontiguousarray(f("conv_dw_w")[0].T.reshape(KC, 128, CK).transpose(1, 0, 2)))
    cpA.put("cdb", _cols(f("conv_dw_b")[0]))
    cpA.put("lng", _cols(f("conv_ln_g")[0]))
    cpA.put("lnb", _cols(f("conv_ln_b")[0]))
    cpA.put("b_out", _cols(f("conv_b_out")[0]))
    cpA.put("flag", np.ones((128, 1), np.float32))
    cpA.put("ident", np.eye(128, dtype=np.float32))
    cstA = cpA.build()
    w_in_t = _tile_w(f("conv_w_in")[0], KC)
    w_out_t = _tile_w(f("conv_w_out")[0], KC)
    w_up0, w_dn0 = _tile_up(f, 0), _tile_dn(f, 0)
    in_maps = []
    for c in cores:
        cc = cstA.copy()
        if c == 0:
            cc[:, cpA.off["flag"]] = 0.0
        seg = np.zeros((HALO + TPC, D), np.float32)
        lo = c * TPC - HALO
        if lo < 0:
            seg[HALO:] = x[0:TPC]
        else:
            seg[:] = x[lo:lo + HALO + TPC]
        fm = _fm(seg)
        in_maps.append({"xt0": np.ascontiguousarray(fm[:, :, :WX]),
                        "xt": np.ascontiguousarray(fm[:, :, WX:].reshape(128, KC, NT - 1, W).transpose(2, 0, 1, 3)),
                        "cst": cc, "w_in": w_in_t, "w_out": w_out_t, "w_up": w_up0, "w_dn": w_dn0})
    ncA = build_A(cstA.shape[1], cpA.off)
    resA = run_bass_kernel_spmd(ncA, in_maps, core_ids=cores).results
    x1T = [(np.asarray(r["x1h"]), np.asarray(r["x1t"])) for r in resA]
    hall = np.ascontiguousarray(np.concatenate([np.asarray(r["h1t"]) for r in resA], axis=0))
    return x1T, hall


def _phaseB(inputs, hall):
    f = lambda k: np.asarray(inputs[k], np.float32)
    cores = list(range(NCORE))
    lambda_init = 0.8 - 0.6 * math.exp(-0.3 * 1)
    used, planes = _bucket_planes()
    wqkv = f("attn_w_qkv")[0]
    rel_bias = f("rel_bias")
    lam_in = np.concatenate([f("attn_lambda_q1")[0], f("attn_lambda_k1")[0], f("attn_lambda_q2")[0],
                             f("attn_lambda_k2")[0]])[None, :].repeat(128, 0)
    in_maps = []
    cpB = None
    for c in cores:
        cpB = CstPack()
        cpB.put("eps2", np.full((128, 1), EPS / (1.0 - lambda_init) ** 2, np.float32))
        cpB.put("lam", lam_in)
        hs = [2 * c, 2 * c + 1]
        cpB.put("tab", np.concatenate([rel_bias[:, h] for h in hs])[None, :].repeat(128, 0))
        cpB.put("farb", np.array([rel_bias[15, h] for h in hs], np.float32)[None, :].repeat(128, 0))
        cpB.put("subg", (f("attn_subln_g")[0])[:, None])
        wc = np.stack([np.stack([_tile_w(wqkv[:, part * D + h * 128: part * D + (h + 1) * 128], KC)[0]
                                 for part in range(3)]) for h in hs])
        in_maps.append({"hall": hall, "cstb": cpB.build(), "wqkv": np.ascontiguousarray(wc), "planes": planes})
    ncB = build_B(cpB.n, cpB.off, used, lambda_init)
    resB = run_bass_kernel_spmd(ncB, in_maps, core_ids=cores).results
    oT_all = np.concatenate([np.asarray(r["oT"]).reshape(256, S) for r in resB], axis=0)
    return oT_all


def _phaseC(inputs, x1T, oT_all):
    f = lambda k: np.asarray(inputs[k], np.float32)
    cores = list(range(NCORE))
    cpC = CstPack()
    _cst_common(cpC, inputs)
    cpC.put("flag", np.ones((128, 1), np.float32))
    cstC = cpC.build()
    w_o_t = _tile_w(f("attn_w_o")[0], KC)
    w_up1, w_dn1 = _tile_up(f, 1), _tile_dn(f, 1)
    in_maps = []
    for c in cores:
        cc = cstC.copy()
        if c == 0:
            cc[:, cpC.off["flag"]] = 0.0
        oc = np.zeros((D, 2 + TPC), oT_all.dtype)
        if c == 0:
            oc[:, 2:] = oT_all[:, 0:TPC]
        else:
            oc[:] = oT_all[:, c * TPC - 2:(c + 1) * TPC]
        ofm = oc.reshape(KC, 128, 2 + TPC).transpose(1, 0, 2)
        in_maps.append({"x1h": x1T[c][0], "x1t": x1T[c][1], "oh": np.ascontiguousarray(ofm[:, :, :2]),
                        "ot": np.ascontiguousarray(ofm[:, :, 2:].reshape(128, KC, NT, W).transpose(2, 0, 1, 3)),
                        "cst": cc, "w_o": w_o_t, "w_up": w_up1, "w_dn": w_dn1})
    ncC = build_C(cstC.shape[1], cpC.off)
    resC = run_bass_kernel_spmd(ncC, in_maps, core_ids=cores).results
    out = np.empty((1, S, D), np.float32)
    for c in cores:
        yt = np.asarray(resC[c]["yt"])
        out[0, c * TPC:(c + 1) * TPC, :] = yt.transpose(0, 3, 2, 1).reshape(TPC, D)
    return out


def kernel(**inputs):
    x1T, hall = _phaseA(inputs)
    oT_all = _phaseB(inputs, hall)
    return _phaseC(inputs, x1T, oT_all)
```

```python
import math
import contextlib
import numpy as np
import ml_dtypes
import concourse.bass as bass
import concourse.mybir as mybir
from concourse.bass_utils import run_bass_kernel_spmd

F32 = mybir.dt.float32
BF16 = mybir.dt.bfloat16
AF = mybir.ActivationFunctionType
ALU = mybir.AluOpType
AX = mybir.AxisListType

D = 2048
S = 16384
NCORE = 8
TPC = S // NCORE
W = 512
NT = TPC // W
HALO = 34
WX = W + HALO
KC = D // 128
FF = 5632
NHT = 2 * FF // 128
NPAIR = FF // 128
G = 4
PPG = NPAIR // G
CK = 31
EPS = 1e-6
NEG = -30000.0
ENGS = ("tensor", "vector", "scalar", "gpsimd", "sync")


class Buf:
    __slots__ = ("name", "writer", "readers")

    def __init__(self, name):
        self.name = name
        self.writer = None
        self.readers = []


class Op:
    __slots__ = ("eng", "fn", "deps", "signaled", "sigval", "is_dma", "dkey", "dcount")

    def __init__(self, eng, fn, is_dma=False, dkey=None):
        self.eng = eng
        self.fn = fn
        self.deps = []
        self.signaled = False
        self.sigval = 0
        self.is_dma = is_dma
        self.dkey = dkey
        self.dcount = 0


class Prog:
    def __init__(self, nc):
        self.nc = nc
        self.ops = {e: [] for e in ENGS}
        self.dma_counts = {}

    def _hazards(self, op, reads, writes):
        deps = []
        for b in reads:
            if b.writer is not None:
                deps.append(b.writer)
        for b in writes:
            if b.writer is not None:
                deps.append(b.writer)
            deps.extend(b.readers)
        seen = set()
        for d in deps:
            if d is op or id(d) in seen:
                continue
            seen.add(id(d))
            if (not d.is_dma) and (not op.is_dma) and d.eng == op.eng:
                if op.eng == "tensor" or not any(b.writer is d for b in reads):
                    continue
            op.deps.append(d)
            d.signaled = True
        for b in reads:
            b.readers.append(op)
        for b in writes:
            b.writer = op
            b.readers = []

    def add(self, eng, fn, reads=(), writes=()):
        op = Op(eng, fn)
        self._hazards(op, reads, writes)
        self.ops[eng].append(op)
        return op

    def dma(self, eng, pairs, key, reads=(), writes=()):
        def fn(e, pairs=pairs):
            return [e.dma_start(out=o, in_=i) for (o, i) in pairs]
        op = Op(eng, fn, is_dma=True, dkey=key)
        self._hazards(op, reads, writes)
        c = self.dma_counts.get(key, 0) + len(pairs)
        self.dma_counts[key] = c
        op.dcount = c
        self.ops[eng].append(op)
        return op

    def emit(self, final_waits=()):
        nc = self.nc
        EPOCH = 1500
        DEPOCH = 64
        with contextlib.ExitStack() as st:
            for d in final_waits:
                d.signaled = True
            nep = {}
            for e in ENGS:
                c = 0
                for op in self.ops[e]:
                    if op.signaled and not op.is_dma:
                        op.sigval = c
                        c += 1
                nep[e] = (c + EPOCH - 1) // EPOCH
            esem = {e: [st.enter_context(nc.semaphore("es_%s_%d" % (e, i))) for i in range(nep[e])] for e in ENGS}
            dsem = {k: [st.enter_context(nc.semaphore("ds_%s_%d" % (k, i)))
                        for i in range((n + DEPOCH - 1) // DEPOCH)] for k, n in self.dma_counts.items()}
            if _DBG.get("verbose"):
                print("epochs", nep, "nops", {e: len(self.ops[e]) for e in ENGS}, "dma", max(self.dma_counts.values()))
            block = st.enter_context(nc.Block())

            def run(e, eng):
                waited = {}

                def wait(d):
                    if d.is_dma:
                        ep = (d.dcount - 1) // DEPOCH
                        sem, val = dsem[d.dkey][ep], 16 * (d.dcount - ep * DEPOCH)
                    else:
                        ep = d.sigval // EPOCH
                        sem, val = esem[d.eng][ep], d.sigval - ep * EPOCH + 1
                    if waited.get(id(sem), 0) >= val:
                        return
                    waited[id(sem)] = val
                    eng.wait_ge(sem, val)

                for op in self.ops[e]:
                    for d in op.deps:
                        wait(d)
                    r = op.fn(eng)
                    if op.is_dma:
                        ep = (op.dcount - 1) // DEPOCH
                        for ins in r:
                            ins.then_inc(dsem[op.dkey][ep], 16)
                    elif op.signaled:
                        r.then_inc(esem[e][op.sigval // EPOCH], 1)
                if e == "sync":
                    for d in final_waits:
                        wait(d)

            block.tensor(lambda eng: run("tensor", eng))
            block.vector(lambda eng: run("vector", eng))
            block.scalar(lambda eng: run("scalar", eng))
            block.gpsimd(lambda eng: run("gpsimd", eng))
            block.sync(lambda eng: run("sync", eng))


class Ring:
    def __init__(self, items):
        self.items = items
        self.i = 0

    def next(self):
        it = self.items[self.i % len(self.items)]
        self.i += 1
        return it


def _cols(v):
    v = np.asarray(v, np.float32)
    return np.ascontiguousarray(v.reshape(-1, 128).T)


def _fm(seg):
    return np.ascontiguousarray(seg.T.reshape(KC, 128, seg.shape[0]).transpose(1, 0, 2))


def _tile_w(Wm, kc):
    K, M = Wm.shape
    return np.ascontiguousarray(Wm.reshape(kc, 128, M // 128, 128).transpose(2, 1, 0, 3))


class CstPack:
    def __init__(self):
        self.parts = []
        self.off = {}
        self.n = 0

    def put(self, name, arr):
        arr = np.asarray(arr, np.float32)
        assert arr.shape[0] == 128
        arr = arr.reshape(128, -1)
        self.off[name] = self.n
        self.n += arr.shape[1]
        self.parts.append(arr)

    def build(self):
        return np.ascontiguousarray(np.concatenate(self.parts, axis=1))


class Dense:
    def __init__(self, nc, P, st, cst_ap, ncst, coff, nws=5, nwd=3):
        self.nc, self.P, self.st, self.coff = nc, P, st, coff
        sb = self.sb
        self.cst = sb("cst_sb", [128, ncst], F32)
        self.Bcst = Buf("cst")
        P.dma("sync", [(self.cst[:], cst_ap)], "cst", writes=[self.Bcst])
        self.ones = sb("ones", [128, 128], BF16)
        self.Bones = Buf("ones")
        P.add("vector", lambda e: e.memset(self.ones[:], 1.0), writes=[self.Bones])
        self.banks = Ring([(st.enter_context(nc.psum_tensor("pb%d" % i, [128, 2 * W], F32)), Buf("pb%d" % i))
                           for i in range(4)])
        self.wslots = Ring([(sb("ws%d" % i, [128, KC, 128], BF16), Buf("ws%d" % i), "ws%d" % i) for i in range(nws)])
        self.wdslots = Ring([(sb("wd%d" % i, [128, PPG, 128], BF16), Buf("wd%d" % i), "wd%d" % i) for i in range(nwd)])
        self.x = sb("x", [128, KC, WX], F32)
        self.Bx = [Buf("x%d" % j) for j in range(KC)]
        self.h = sb("h", [128, KC, WX], BF16)
        self.Bh = Buf("h")
        self.sq = [(sb("sq%d" % i, [128, WX], BF16), Buf("sq%d" % i)) for i in range(2)]
        self.rstd = sb("rstd", [128, WX], F32)
        self.Brstd = Buf("rstd")
        self.tmp = [(sb("tmp%d" % i, [128, WX], F32), Buf("tmp%d" % i)) for i in range(3)]
        self.hid = sb("hid", [128, PPG, WX], BF16)
        self.Bhid = [Buf("hid%d" % j) for j in range(PPG)]
        self.eb = Ring([(sb("e%d" % i, [128, WX + 2], F32), Buf("e%d" % i)) for i in range(4)])
        self.yb = Ring([(sb("y%d" % i, [128, WX], F32), Buf("y%d" % i)) for i in range(4)])
        self.fst = sb("fst", [128, NHT, 2], F32)
        self.Bfst = [Buf("fst%d" % j) for j in range(NHT)]
        Bf = self.Bfst
        P.add("vector", lambda e: e.memset(self.fst[:], 0.0), writes=Bf)

    def sb(self, name, shape, dt):
        return self.st.enter_context(self.nc.sbuf_tensor(name, shape, dt))

    def c(self, name, j=0, n=1):
        o = self.coff[name] + j
        return self.cst[:, o:o + n]

    def load_w(self, dram_tile):
        slot, B, key = self.wslots.next()
        self.P.dma("gpsimd", [(slot[:], dram_tile)], key, writes=[B])
        return slot, B

    def load_wd(self, dram_tile):
        slot, B, key = self.wdslots.next()
        self.P.dma("gpsimd", [(slot[:], dram_tile)], key, writes=[B])
        return slot, B

    @staticmethod
    def mm(e, ps, lhsT, rhs, Wc, start, stop):
        w0 = min(Wc, W)
        ins = e.matmul(ps[:, 0:w0], lhsT=lhsT, rhs=rhs[:, 0:w0], start=start, stop=stop)
        if Wc > W:
            ins = e.matmul(ps[:, W:Wc], lhsT=lhsT, rhs=rhs[:, W:Wc], start=start, stop=stop)
        return ins

    def mm_group(self, ps, wslot, act, nk, Wc, reads, Bps):
        def fn(e):
            ins = None
            for k in range(nk):
                ins = self.mm(e, ps, wslot[:, k, :], act[:, k, :], Wc, k == 0, k == nk - 1)
            return ins
        return self.P.add("tensor", fn, reads=reads, writes=[Bps])

    def colsum_rstd(self, srcs, Wc, scale, out_ap, Bout):
        P = self.P
        ps, Bps = self.banks.next()
        n = len(srcs)
        for j, (src, Bsrc) in enumerate(srcs):
            sq, Bsq = self.sq[j % 2]
            P.add("scalar", lambda e, sq=sq, src=src: e.activation(out=sq[:, 0:Wc], in_=src, func=AF.Square),
                  reads=[Bsrc], writes=[Bsq])
            P.add("tensor", lambda e, sq=sq, j=j: self.mm(e, ps, self.ones[:], sq, Wc, j == 0, j == n - 1),
                  reads=[Bsq, self.Bones], writes=[Bps])
        t, Bt = self.tmp[0]
        P.add("scalar", lambda e: e.activation(out=t[:, 0:Wc], in_=ps[:, 0:Wc], func=AF.Sqrt, bias=self.c("eps"),
                                               scale=scale),
              reads=[Bps, self.Bcst], writes=[Bt])
        P.add("vector", lambda e: e.reciprocal(out=out_ap, in_=t[:, 0:Wc]), reads=[Bt], writes=[Bout])

    def rmsnorm_to_h(self, gname, Wc):
        P = self.P
        self.colsum_rstd([(self.x[:, j, 0:Wc], self.Bx[j]) for j in range(KC)], Wc, 1.0 / D,
                         self.rstd[:, 0:Wc], self.Brstd)
        for j in range(KC):
            P.add("vector", lambda e, j=j: e.scalar_tensor_tensor(
                out=self.h[:, j, 0:Wc], in0=self.x[:, j, 0:Wc], scalar=self.c(gname, j), in1=self.rstd[:, 0:Wc],
                op0=ALU.mult, op1=ALU.mult), reads=[self.Bx[j], self.Brstd, self.Bcst], writes=[self.Bh])

    def ffn(self, layer, Wc, w_up, w_dn, gname, nhalo):
        P = self.P
        self.rmsnorm_to_h(gname, Wc)
        dww, dwb = "fdw%d" % layer, "fdb%d" % layer
        for g in range(G):
            for jj in range(PPG):
                pj = g * PPG + jj
                ys = []
                for half in range(2):
                    t = pj + half * NPAIR
                    ws, Bw = self.load_w(w_up[t])
                    ps, Bps = self.banks.next()
                    self.mm_group(ps, ws, self.h, KC, Wc, [Bw, self.Bh], Bps)
                    eb, Be = self.eb.next()
                    yb, By = self.yb.next()
                    st_ap = self.fst[:, t, :]
                    P.add("vector", lambda e, eb=eb, st_ap=st_ap: e.tensor_copy(out=eb[:, 0:2], in_=st_ap),
                          reads=[self.Bfst[t]], writes=[Be])
                    P.add("scalar", lambda e, eb=eb, ps=ps: e.activation(out=eb[:, 2:2 + Wc], in_=ps[:, 0:Wc],
                                                                         func=AF.Copy),
                          reads=[Bps], writes=[Be])
                    if nhalo:
                        P.add("vector", lambda e, eb=eb: e.tensor_scalar(
                            out=eb[:, 2:2 + nhalo], in0=eb[:, 2:2 + nhalo], scalar1=self.c("flag"), scalar2=None,
                            op0=ALU.mult), reads=[Be, self.Bcst], writes=[Be])
                    P.add("vector", lambda e, eb=eb, st_ap=st_ap: e.tensor_copy(out=st_ap, in_=eb[:, Wc:Wc + 2]),
                          reads=[Be], writes=[self.Bfst[t]])
                    P.add("vector", lambda e, eb=eb, yb=yb, t=t: e.tensor_scalar(
                        out=yb[:, 0:Wc], in0=eb[:, 2:2 + Wc], scalar1=self.c(dww, 3 * t + 2), scalar2=self.c(dwb, t),
                        op0=ALU.mult, op1=ALU.add), reads=[Be, self.Bcst], writes=[By])
                    for tap in (1, 0):
                        P.add("vector", lambda e, eb=eb, yb=yb, t=t, tap=tap: e.scalar_tensor_tensor(
                            out=yb[:, 0:Wc], in0=eb[:, tap:tap + Wc], scalar=self.c(dww, 3 * t + tap), in1=yb[:, 0:Wc],
                            op0=ALU.mult, op1=ALU.add), reads=[Be, By, self.Bcst], writes=[By])
                    ys.append((yb, By))
                (yg, Byg), (yv, Byv) = ys
                P.add("scalar", lambda e, yg=yg: e.activation(out=yg[:, 0:Wc], in_=yg[:, 0:Wc], func=AF.Silu),
                      reads=[Byg], writes=[Byg])
                P.add("vector", lambda e, yg=yg, yv=yv, jj=jj: e.tensor_tensor(
                    out=self.hid[:, jj, 0:Wc], in0=yg[:, 0:Wc], in1=yv[:, 0:Wc], op=ALU.mult),
                    reads=[Byg, Byv], writes=[self.Bhid[jj]])
            for m in range(KC):
                wd, Bwd = self.load_wd(w_dn[g, m])
                ps, Bps = self.banks.next()
                self.mm_group(ps, wd, self.hid, PPG, Wc, [Bwd] + self.Bhid, Bps)
                P.add("vector", lambda e, ps=ps, m=m: e.tensor_tensor(
                    out=self.x[:, m, 0:Wc], in0=ps[:, 0:Wc], in1=self.x[:, m, 0:Wc], op=ALU.add),
                    reads=[Bps, self.Bx[m]], writes=[self.Bx[m]])


def _cst_common(cp, inputs):
    cp.put("eps", np.full((128, 1), EPS, np.float32))
    for L in range(2):
        cp.put("fdw%d" % L, np.ascontiguousarray(
            np.asarray(inputs["ffn_dw_w"][L], np.float32).T.reshape(NHT, 128, 3).transpose(1, 0, 2)))
        cp.put("fdb%d" % L, _cols(inputs["ffn_dw_b"][L]))
        cp.put("fng%d" % L, _cols(inputs["ffn_norm"][L]))
        cp.put("mng%d" % L, _cols(inputs["mix_norm"][L]))
    cp.put("fin", _cols(inputs["final_norm_g"]))


def build_A(ncst, coff):
    nc = bass.Bass("TRN2", target_bir_lowering=False)
    xt0 = nc.dram_tensor("xt0", [128, KC, WX], F32, kind="ExternalInput").ap()
    xt = nc.dram_tensor("xt", [NT - 1, 128, KC, W], F32, kind="ExternalInput").ap()
    cst = nc.dram_tensor("cst", [128, ncst], F32, kind="ExternalInput").ap()
    w_in = nc.dram_tensor("w_in", [32, 128, KC, 128], F32, kind="ExternalInput").ap()
    w_out = nc.dram_tensor("w_out", [16, 128, KC, 128], F32, kind="ExternalInput").ap()
    w_up = nc.dram_tensor("w_up", [NHT, 128, KC, 128], F32, kind="ExternalInput").ap()
    w_dn = nc.dram_tensor("w_dn", [G, 16, 128, PPG, 128], F32, kind="ExternalInput").ap()
    x1h = nc.dram_tensor("x1h", [128, KC, 2], F32, kind="ExternalOutput").ap()
    x1t = nc.dram_tensor("x1t", [NT, 128, KC, W], F32, kind="ExternalOutput").ap()
    h1t = nc.dram_tensor("h1t", [NT, 128, KC, W], BF16, kind="ExternalOutput").ap()
    P = Prog(nc)
    finals = []
    with contextlib.ExitStack() as st:
        dn = Dense(nc, P, st, cst, ncst, coff)
        sb = dn.sb
        uext = sb("uext", [128, KC, WX + 30], BF16)
        Bu = [Buf("u%d" % j) for j in range(KC)]
        cv = sb("cv", [128, KC, WX], F32)
        Bcv = [Buf("cv%d" % j) for j in range(KC)]
        sg = [(sb("sg%d" % i, [128, WX], F32), Buf("sg%d" % i)) for i in range(2)]
        mu = sb("mu", [128, WX], F32)
        Bmu = Buf("mu")
        nmr = sb("nmr", [128, WX], F32)
        Bnmr = Buf("nmr")
        h1s, Bh1s = dn.h, dn.Bh
        P.add("vector", lambda e: e.memset(uext[:, :, 0:30], 0.0), writes=Bu)
        hso = sb("hso", [128, KC, 2], F32)
        Bhso = Buf("hso")
        ident = sb("ident", [128, 128], BF16)
        Bident = Buf("ident")
        P.add("vector", lambda e: e.tensor_copy(out=ident[:], in_=dn.c("ident", 0, 128)), reads=[dn.Bcst],
              writes=[Bident])
        dgs = Ring([(sb("dg%d" % i, [128, CK, 128], BF16), Buf("dg%d" % i)) for i in range(2)])

        tiles = [(0, WX, HALO)] + [(i, W, 0) for i in range(1, NT)]

        def tile_body(ti, Wc, nh):
            if nh:
                P.dma("sync", [(dn.x[:, :, 0:Wc], xt0)], "xin", writes=dn.Bx)
            else:
                P.dma("sync", [(dn.x[:, :, 0:Wc], xt[ti - 1])], "xin", writes=dn.Bx)
            dn.rmsnorm_to_h("mng0", Wc)
            for m in range(KC):
                wa, Bwa = dn.load_w(w_in[m])
                pa, Bpa = dn.banks.next()
                dn.mm_group(pa, wa, dn.h, KC, Wc, [Bwa, dn.Bh], Bpa)
                wg, Bwg = dn.load_w(w_in[m + KC])
                pg, Bpg = dn.banks.next()
                dn.mm_group(pg, wg, dn.h, KC, Wc, [Bwg, dn.Bh], Bpg)
                s, Bs = sg[m % 2]
                P.add("scalar", lambda e, s=s, pg=pg, m=m: e.activation(
                    out=s[:, 0:Wc], in_=pg[:, 0:Wc], func=AF.Sigmoid, bias=dn.c("b_in", KC + m)),
                    reads=[Bpg, dn.Bcst], writes=[Bs])
                P.add("vector", lambda e, s=s, pa=pa, m=m: e.scalar_tensor_tensor(
                    out=uext[:, m, 30:30 + Wc], in0=pa[:, 0:Wc], scalar=dn.c("b_in", m), in1=s[:, 0:Wc],
                    op0=ALU.add, op1=ALU.mult), reads=[Bpa, Bs, dn.Bcst], writes=[Bu[m]])
            if nh:
                P.add("vector", lambda e: e.tensor_scalar(
                    out=uext[:, :, 30:30 + nh], in0=uext[:, :, 30:30 + nh], scalar1=dn.c("flag"), scalar2=None,
                    op0=ALU.mult), reads=Bu + [dn.Bcst], writes=Bu)
            for m in range(KC):
                dgt, Bdg = dgs.next()
                for tap in range(CK):
                    P.add("vector", lambda e, dgt=dgt, m=m, tap=tap: e.tensor_scalar(
                        out=dgt[:, tap, :], in0=ident[:], scalar1=dn.c("cdw", m * CK + tap), scalar2=None,
                        op0=ALU.mult), reads=[Bident, dn.Bcst], writes=[Bdg])
                ps, Bps = dn.banks.next()

                def convmm(e, dgt=dgt, m=m, ps=ps):
                    ins = None
                    for tap in range(CK):
                        ins = dn.mm(e, ps, dgt[:, tap, :], uext[:, m, tap:tap + Wc], Wc, tap == 0, tap == CK - 1)
                    return ins
                P.add("tensor", convmm, reads=[Bdg, Bu[m]], writes=[Bps])
                P.add("scalar", lambda e, m=m, ps=ps: e.activation(out=cv[:, m, 0:Wc], in_=ps[:, 0:Wc],
                                                                   func=AF.Identity, bias=dn.c("cdb", m)),
                      reads=[Bps, dn.Bcst], writes=[Bcv[m]])
            P.add("vector", lambda e: e.tensor_copy(out=uext[:, :, 0:30], in_=uext[:, :, Wc:Wc + 30]),
                  reads=Bu, writes=Bu)
            ps_s, Bps_s = dn.banks.next()
            ps_q, Bps_q = dn.banks.next()
            for j in range(KC):
                sq, Bsq = dn.sq[0]
                cb, Bcb = dn.sq[1]
                P.add("scalar", lambda e, j=j, cb=cb: e.activation(out=cb[:, 0:Wc], in_=cv[:, j, 0:Wc], func=AF.Copy),
                      reads=[Bcv[j]], writes=[Bcb])
                P.add("scalar", lambda e, j=j, sq=sq: e.activation(out=sq[:, 0:Wc], in_=cv[:, j, 0:Wc],
                                                                   func=AF.Square),
                      reads=[Bcv[j]], writes=[Bsq])
                P.add("tensor", lambda e, j=j, cb=cb: dn.mm(e, ps_s, dn.ones[:], cb, Wc, j == 0, j == KC - 1),
                      reads=[Bcb, dn.Bones], writes=[Bps_s])
                P.add("tensor", lambda e, j=j, sq=sq: dn.mm(e, ps_q, dn.ones[:], sq, Wc, j == 0, j == KC - 1),
                      reads=[Bsq, dn.Bones], writes=[Bps_q])
            t1, Bt1 = dn.tmp[1]
            t2, Bt2 = dn.tmp[2]
            t0, Bt0 = dn.tmp[0]
            P.add("vector", lambda e: e.tensor_scalar(out=mu[:, 0:Wc], in0=ps_s[:, 0:Wc], scalar1=1.0 / D, scalar2=None,
                                                      op0=ALU.mult), reads=[Bps_s], writes=[Bmu])
            P.add("vector", lambda e: e.tensor_tensor(out=t1[:, 0:Wc], in0=mu[:, 0:Wc], in1=mu[:, 0:Wc], op=ALU.mult),
                  reads=[Bmu], writes=[Bt1])
            P.add("vector", lambda e: e.scalar_tensor_tensor(out=t2[:, 0:Wc], in0=ps_q[:, 0:Wc], scalar=1.0 / D,
                                                             in1=t1[:, 0:Wc], op0=ALU.mult, op1=ALU.subtract),
                  reads=[Bps_q, Bt1], writes=[Bt2])
            P.add("scalar", lambda e: e.activation(out=t0[:, 0:Wc], in_=t2[:, 0:Wc], func=AF.Sqrt, bias=dn.c("eps"),
                                                   scale=1.0), reads=[Bt2, dn.Bcst], writes=[Bt0])
            P.add("vector", lambda e: e.reciprocal(out=dn.rstd[:, 0:Wc], in_=t0[:, 0:Wc]), reads=[Bt0],
                  writes=[dn.Brstd])
            P.add("vector", lambda e: e.scalar_tensor_tensor(out=nmr[:, 0:Wc], in0=mu[:, 0:Wc], scalar=-1.0,
                                                             in1=dn.rstd[:, 0:Wc], op0=ALU.mult, op1=ALU.mult),
                  reads=[Bmu, dn.Brstd], writes=[Bnmr])
            for j in range(KC):
                P.add("vector", lambda e, j=j: e.tensor_tensor(out=cv[:, j, 0:Wc], in0=cv[:, j, 0:Wc],
                                                               in1=dn.rstd[:, 0:Wc], op=ALU.mult),
                      reads=[Bcv[j], dn.Brstd], writes=[Bcv[j]])
                P.add("vector", lambda e, j=j: e.tensor_tensor(out=cv[:, j, 0:Wc], in0=cv[:, j, 0:Wc],
                                                               in1=nmr[:, 0:Wc], op=ALU.add),
                      reads=[Bcv[j], Bnmr], writes=[Bcv[j]])
                P.add("scalar", lambda e, j=j: e.activation(out=dn.h[:, j, 0:Wc], in_=cv[:, j, 0:Wc], func=AF.Silu,
                                                            bias=dn.c("lnb", j), scale=dn.c("lng", j)),
                      reads=[Bcv[j], dn.Bcst], writes=[dn.Bh])
            for m in range(KC):
                ws, Bw = dn.load_w(w_out[m])
                ps, Bps = dn.banks.next()
                dn.mm_group(ps, ws, dn.h, KC, Wc, [Bw, dn.Bh], Bps)
                P.add("vector", lambda e, ps=ps, m=m: e.scalar_tensor_tensor(
                    out=dn.x[:, m, 0:Wc], in0=ps[:, 0:Wc], scalar=dn.c("b_out", m), in1=dn.x[:, m, 0:Wc],
                    op0=ALU.add, op1=ALU.add), reads=[Bps, dn.Bx[m], dn.Bcst], writes=[dn.Bx[m]])
            dn.ffn(0, Wc, w_up, w_dn, "fng0", nh)
            if nh:
                P.add("vector", lambda e: e.tensor_copy(out=hso[:], in_=dn.x[:, :, nh - 2:nh]), reads=dn.Bx,
                      writes=[Bhso])
                finals.append(P.dma("sync", [(x1h, hso[:])], "x1ho", reads=[Bhso]))
            finals.append(P.dma("sync", [(x1t[ti], dn.x[:, :, nh:nh + W])], "x1o", reads=dn.Bx))
            dn.colsum_rstd([(dn.x[:, j, 0:Wc], dn.Bx[j]) for j in range(KC)], Wc, 1.0 / D,
                           dn.rstd[:, 0:Wc], dn.Brstd)
            for j in range(KC):
                P.add("vector", lambda e, j=j: e.scalar_tensor_tensor(
                    out=h1s[:, j, 0:Wc], in0=dn.x[:, j, 0:Wc], scalar=dn.c("mng1", j), in1=dn.rstd[:, 0:Wc],
                    op0=ALU.mult, op1=ALU.mult), reads=[dn.Bx[j], dn.Brstd, dn.Bcst], writes=[Bh1s])
            finals.append(P.dma("sync", [(h1t[ti], h1s[:, :, nh:nh + W])], "h1o", reads=[Bh1s]))
        for (ti_, Wc_, nh_) in tiles:
            tile_body(ti_, Wc_, nh_)
        P.emit(final_waits=finals)
    return nc


def _bucket_planes():
    kk = np.arange(128)[:, None]
    m = np.arange(1024)[None, :]
    rel = kk - m + 384
    allowed = (kk // 64) <= (m // 64 - 6)
    n = np.abs(rel)
    nf = np.maximum(n, 1).astype(np.float32)
    large = 8 + (np.log(nf / np.float32(8)) / np.float32(math.log(16.0)) * np.float32(8)).astype(np.int32)
    large = np.minimum(large, 15)
    bucket = np.where(rel > 0, 16, 0) + np.where(n < 8, n, large)
    used = sorted(set(bucket[allowed].tolist()))
    planes = np.zeros((len(used) + 1, 128, 1024), np.float32)
    planes[0] = np.where(allowed, 0.0, NEG)
    for i, b in enumerate(used):
        planes[i + 1] = ((bucket == b) & allowed).astype(np.float32)
    return used, planes


def build_B(nB, boff, used, lambda_init):
    nc = bass.Bass("TRN2", target_bir_lowering=False)
    hall = nc.dram_tensor("hall", [NCORE * NT, 128, KC, W], BF16, kind="ExternalInput").ap()
    cstb = nc.dram_tensor("cstb", [128, nB], F32, kind="ExternalInput").ap()
    wqkv = nc.dram_tensor("wqkv", [2, 3, 128, KC, 128], F32, kind="ExternalInput").ap()
    planes = nc.dram_tensor("planes", [len(used) + 1, 128, 1024], F32, kind="ExternalInput").ap()
    oT = nc.dram_tensor("oT", [2, 128, S], BF16, kind="ExternalOutput").ap()
    P = Prog(nc)
    finals = []
    NQ = S // W
    with contextlib.ExitStack() as st:
        def sb(name, shape, dt):
            return st.enter_context(nc.sbuf_tensor(name, shape, dt))
        cst = sb("cst", [128, nB], F32)
        Bcst = Buf("cst")
        P.dma("sync", [(cst[:], cstb)], "cst", writes=[Bcst])

        def c(name, j=0, n=1):
            o = boff[name] + j
            return cst[:, o:o + n]
        ones = sb("ones", [128, 128], BF16)
        Bones = Buf("ones")
        P.add("vector", lambda e: e.memset(ones[:], 1.0), writes=[Bones])
        sbanks = Ring([(st.enter_context(nc.psum_tensor("sb%d" % i, [128, 512], F32)), Buf("sb%d" % i))
                       for i in range(4)])
        obank = [(st.enter_context(nc.psum_tensor("ob%d" % i, [128, 512], F32)), Buf("ob%d" % i)) for i in range(2)]
        zbank = [(st.enter_context(nc.psum_tensor("zb%d" % i, [128, 512], F32)), Buf("zb%d" % i)) for i in range(2)]
        Kt = sb("Kt", [128, S], BF16)
        BK = [Buf("K%d" % i) for i in range(NQ)]
        Vt = sb("Vt", [128, S // 128, 128], BF16)
        BV = [Buf("V%d" % i) for i in range(NQ)]
        qts = Ring([(sb("qt%d" % i, [128, W], BF16), Buf("qt%d" % i)) for i in range(2)])
        hts = Ring([(sb("ht%d" % i, [128, KC, W], BF16), Buf("ht%d" % i), "ht%d" % i) for i in range(2)])
        wq = [(sb("wq%d" % i, [128, KC, 128], BF16), Buf("wq%d" % i)) for i in range(3)]
        strip = sb("strip", [128, 1024], F32)
        Bstrip = Buf("strip")
        pls = Ring([(sb("pl%d" % i, [128, 1024], F32), Buf("pl%d" % i), "pl%d" % i) for i in range(2)])
        pts = Ring([(sb("pt%d" % i, [128, W], BF16), Buf("pt%d" % i)) for i in range(6)])
        tns = Ring([(sb("tn%d" % i, [128, W], F32), Buf("tn%d" % i)) for i in range(2)])
        ep = [(sb("ep%d" % i, [128, W], F32), Buf("ep%d" % i)) for i in range(5)]
        sqb = sb("sqb", [128, W], BF16)
        Bsqb = Buf("sqb")
        ons = Ring([(sb("on%d" % i, [128, W], BF16), Buf("on%d" % i), "on%d" % i) for i in range(2)])
        zacc = [(sb("zacc%d" % i, [128, W], F32), Buf("zacc%d" % i)) for i in range(2)]
        ones32 = sb("ones32", [128, 128], F32)
        Bones32 = Buf("ones32")
        P.add("vector", lambda e: e.memset(ones32[:], 1.0), writes=[Bones32])
        lamt = sb("lamt", [128, 8], F32)
        Blam = Buf("lam")
        lprod = sb("lprod", [128, 2, 64], F32)
        Blprod = Buf("lprod")
        P.add("vector", lambda e: e.tensor_tensor(out=lprod[:, 0, :], in0=c("lam", 0, 64), in1=c("lam", 64, 64),
                                                  op=ALU.mult), reads=[Bcst], writes=[Blprod])
        P.add("vector", lambda e: e.tensor_tensor(out=lprod[:, 1, :], in0=c("lam", 128, 64), in1=c("lam", 192, 64),
                                                  op=ALU.mult), reads=[Bcst], writes=[Blprod])
        P.add("vector", lambda e: e.tensor_reduce(out=lamt[:, 0:2], in_=lprod[:], axis=AX.X, op=ALU.add),
              reads=[Blprod], writes=[Blam])
        P.add("scalar", lambda e: e.activation(out=lamt[:, 2:4], in_=lamt[:, 0:2], func=AF.Exp), reads=[Blam],
              writes=[Blam])
        P.add("vector", lambda e: e.scalar_tensor_tensor(out=lamt[:, 4:5], in0=lamt[:, 3:4], scalar=-lambda_init,
                                                         in1=lamt[:, 2:3], op0=ALU.add, op1=ALU.subtract),
              reads=[Blam], writes=[Blam])

        def head_body(hd):
            for i in range(3):
                P.dma("gpsimd", [(wq[i][0][:], wqkv[hd, i])], "wq%d" % i, writes=[wq[i][1]])
            pl, Bpl, kpl = pls.next()
            P.dma("sync", [(pl[:], planes[0])], kpl, writes=[Bpl])
            P.add("vector", lambda e, pl=pl: e.tensor_copy(out=strip[:], in_=pl[:]), reads=[Bpl], writes=[Bstrip])
            for bi, b in enumerate(used):
                pl, Bpl, kpl = pls.next()
                P.dma("sync", [(pl[:], planes[bi + 1])], kpl, writes=[Bpl])
                P.add("vector", lambda e, pl=pl, b=b: e.scalar_tensor_tensor(
                    out=strip[:], in0=pl[:], scalar=c("tab", hd * 32 + b), in1=strip[:], op0=ALU.mult, op1=ALU.add),
                    reads=[Bpl, Bstrip, Bcst], writes=[Bstrip])
            def proj(i):
                ht, Bht, kht = hts.next()
                P.dma("sync", [(ht[:], hall[i])], kht, writes=[Bht])
                ps, Bps = sbanks.next()
                P.add("tensor", lambda e, ps=ps, ht=ht: [e.matmul(ps[:], lhsT=wq[1][0][:, k, :], rhs=ht[:, k, :],
                                                                  start=(k == 0), stop=(k == KC - 1))
                                                         for k in range(KC)][-1],
                      reads=[wq[1][1], Bht], writes=[Bps])
                P.add("scalar", lambda e, ps=ps, i=i: e.activation(out=Kt[:, i * W:(i + 1) * W], in_=ps[:],
                                                                   func=AF.Copy),
                      reads=[Bps], writes=[BK[i]])
                ps, Bps = sbanks.next()
                qt, Bqt = qts.next()
                P.add("tensor", lambda e, ps=ps, ht=ht: [e.matmul(ps[:], lhsT=wq[0][0][:, k, :], rhs=ht[:, k, :],
                                                                  start=(k == 0), stop=(k == KC - 1))
                                                         for k in range(KC)][-1],
                      reads=[wq[0][1], Bht], writes=[Bps])
                P.add("scalar", lambda e, ps=ps, qt=qt: e.activation(out=qt[:], in_=ps[:], func=AF.Copy, scale=0.125),
                      reads=[Bps], writes=[Bqt])
                ps, Bps = sbanks.next()

                def vmm(e, ps=ps, ht=ht):
                    ins = None
                    for sub in range(4):
                        for k in range(KC):
                            ins = e.matmul(ps[:, sub * 128:(sub + 1) * 128], lhsT=ht[:, k, sub * 128:(sub + 1) * 128],
                                           rhs=wq[2][0][:, k, :], start=(k == 0), stop=(k == KC - 1))
                    return ins
                P.add("tensor", vmm, reads=[wq[2][1], Bht], writes=[Bps])
                P.add("vector", lambda e, ps=ps, i=i: e.tensor_copy(
                    out=Vt[:, 4 * i:4 * i + 4, :], in_=ps[:].rearrange("p (a b) -> p a b", a=4)),
                    reads=[Bps], writes=[BV[i]])
                return qt, Bqt

            def scores(i, j, qt, Bqt):
                d = 128 * j - W * i
                c0 = max(d, 0)
                out = []
                for comp in range(2):
                    ps, Bps = sbanks.next()
                    lo = 64 * comp
                    P.add("tensor", lambda e, ps=ps, j=j, lo=lo, qt=qt, c0=c0: e.matmul(
                        ps[:, c0:W], lhsT=Kt[lo:lo + 64, j * 128:(j + 1) * 128], rhs=qt[lo:lo + 64, c0:W],
                        start=True, stop=True), reads=[BK[j // 4], Bqt], writes=[Bps])
                    out.append((ps, Bps))
                return out

            def expo(i, j, sc):
                d = 128 * j - W * i
                c0 = max(d, 0)
                near = j >= 4 * i - 1
                pcs = []
                for comp in range(2):
                    ps, Bps = sc[comp]
                    pt, Bpt = pts.next()
                    if near:
                        tn, Btn = tns.next()
                        so = 384 - d
                        P.add("vector", lambda e, ps=ps, tn=tn, so=so, c0=c0: e.tensor_tensor(
                            out=tn[:, c0:W], in0=ps[:, c0:W], in1=strip[:, so + c0:so + W], op=ALU.add),
                            reads=[Bps, Bstrip], writes=[Btn])
                        P.add("scalar", lambda e, tn=tn, pt=pt, c0=c0: e.activation(
                            out=pt[:, c0:W], in_=tn[:, c0:W], func=AF.Exp), reads=[Btn], writes=[Bpt])
                    else:
                        P.add("scalar", lambda e, ps=ps, pt=pt: e.activation(
                            out=pt[:], in_=ps[:], func=AF.Exp, bias=c("farb", hd)), reads=[Bps, Bcst],
                            writes=[Bpt])
                    pcs.append((pt, Bpt))
                return pcs

            def pv(i, j, pcs):
                d = 128 * j - W * i
                c0 = max(d, 0)
                nk = 4 * (i + 1)
                for comp in range(2):
                    pt, Bpt = pcs[comp]
                    ob, Bob = obank[comp]
                    P.add("tensor", lambda e, pt=pt, ob=ob, j=j, c0=c0, nk=nk: e.matmul(
                        ob[:, c0:W], lhsT=Vt[:, j, :], rhs=pt[:, c0:W], start=(j == 0), stop=(j == nk - 1)),
                        reads=[BV[j // 4], Bpt], writes=[Bob])
                    if comp == 0:
                        za, Bza = zacc[0]
                        if j == 0:
                            P.add("vector", lambda e, pt=pt, za=za: e.tensor_copy(out=za[:], in_=pt[:]),
                                  reads=[Bpt], writes=[Bza])
                        else:
                            P.add("vector", lambda e, pt=pt, za=za, c0=c0: e.tensor_tensor(
                                out=za[:, c0:W], in0=za[:, c0:W], in1=pt[:, c0:W], op=ALU.add),
                                reads=[Bpt, Bza], writes=[Bza])
                    else:
                        zb, Bzb = zbank[1]
                        P.add("tensor", lambda e, pt=pt, zb=zb, j=j, c0=c0, nk=nk: e.matmul(
                            zb[:, c0:W], lhsT=ones[:], rhs=pt[:, c0:W], start=(j == 0), stop=(j == nk - 1)),
                            reads=[Bones, Bpt], writes=[Bzb])

            (r0, Br0), (r1, Br1), (t0, Bt0), (t1, Bt1), (oo, Boo) = ep

            def epi1():
                for comp in range(1):
                    za, Bza = zacc[comp]
                    zb, Bzb = zbank[comp]
                    P.add("tensor", lambda e, za=za, zb=zb: e.matmul(zb[:], lhsT=ones32[:], rhs=za[:], start=True,
                                                                    stop=True),
                          reads=[Bones32, Bza], writes=[Bzb])
                P.add("vector", lambda e: e.reciprocal(out=r0[:], in_=zbank[0][0][:]), reads=[zbank[0][1]],
                      writes=[Br0])
                P.add("vector", lambda e: e.reciprocal(out=r1[:], in_=zbank[1][0][:]), reads=[zbank[1][1]],
                      writes=[Br1])
                P.add("vector", lambda e: e.tensor_tensor(out=t0[:], in0=obank[0][0][:], in1=r0[:], op=ALU.mult),
                      reads=[obank[0][1], Br0], writes=[Bt0])
                P.add("vector", lambda e: e.tensor_tensor(out=t1[:], in0=obank[1][0][:], in1=r1[:], op=ALU.mult),
                      reads=[obank[1][1], Br1], writes=[Bt1])
                P.add("vector", lambda e: e.scalar_tensor_tensor(out=oo[:], in0=t1[:], scalar=lamt[:, 4:5], in1=t0[:],
                                                                 op0=ALU.mult, op1=ALU.add),
                      reads=[Bt0, Bt1, Blam], writes=[Boo])
                P.add("scalar", lambda e: e.activation(out=sqb[:], in_=oo[:], func=AF.Square), reads=[Boo],
                      writes=[Bsqb])

            def epi2(i):
                ps, Bps = sbanks.next()
                P.add("tensor", lambda e, ps=ps: e.matmul(ps[:], lhsT=ones[:], rhs=sqb[:], start=True, stop=True),
                      reads=[Bones, Bsqb], writes=[Bps])
                P.add("scalar", lambda e, ps=ps: e.activation(out=r0[:], in_=ps[:], func=AF.Sqrt, bias=c("eps2"),
                                                              scale=1.0 / (128 * (1.0 - lambda_init) ** 2)),
                      reads=[Bps, Bcst], writes=[Br0])
                P.add("vector", lambda e: e.reciprocal(out=r1[:], in_=r0[:]), reads=[Br0], writes=[Br1])
                on, Bon, kon = ons.next()
                P.add("vector", lambda e, on=on: e.scalar_tensor_tensor(out=on[:], in0=oo[:], scalar=c("subg"),
                                                                        in1=r1[:], op0=ALU.mult, op1=ALU.mult),
                      reads=[Boo, Br1, Bcst], writes=[Bon])
                store_o(hd, i, on, Bon, kon)

            cur = proj(0)
            for i in range(NQ):
                qt, Bqt = cur
                nk = 4 * (i + 1)
                sc = scores(i, 0, qt, Bqt)
                for j in range(nk):
                    sc_next = scores(i, j + 1, qt, Bqt) if j + 1 < nk else None
                    pcs = expo(i, j, sc)
                    pv(i, j, pcs)
                    sc = sc_next
                epi1()
                if i + 1 < NQ:
                    cur = proj(i + 1)
                epi2(i)

        def store_o(hd, i, on, Bon, kon):
            finals.append(P.dma("sync", [(oT[hd, :, i * W:(i + 1) * W], on[:])], kon + "o", reads=[Bon]))

        for hd_ in range(2):
            head_body(hd_)
        P.emit(final_waits=finals)
    return nc


def build_C(ncst, coff):
    nc = bass.Bass("TRN2", target_bir_lowering=False)
    x1h = nc.dram_tensor("x1h", [128, KC, 2], F32, kind="ExternalInput").ap()
    x1t = nc.dram_tensor("x1t", [NT, 128, KC, W], F32, kind="ExternalInput").ap()
    oh = nc.dram_tensor("oh", [128, KC, 2], BF16, kind="ExternalInput").ap()
    ot = nc.dram_tensor("ot", [NT, 128, KC, W], BF16, kind="ExternalInput").ap()
    cst = nc.dram_tensor("cst", [128, ncst], F32, kind="ExternalInput").ap()
    w_o = nc.dram_tensor("w_o", [16, 128, KC, 128], F32, kind="ExternalInput").ap()
    w_up = nc.dram_tensor("w_up", [NHT, 128, KC, 128], F32, kind="ExternalInput").ap()
    w_dn = nc.dram_tensor("w_dn", [G, 16, 128, PPG, 128], F32, kind="ExternalInput").ap()
    yt = nc.dram_tensor("yt", [NT, 128, KC, W], F32, kind="ExternalOutput").ap()
    P = Prog(nc)
    finals = []
    with contextlib.ExitStack() as st:
        dn = Dense(nc, P, st, cst, ncst, coff, nws=7, nwd=4)
        sb = dn.sb
        osb = sb("osb", [128, KC, W + 2], BF16)
        Bo = Buf("osb")
        ys = sb("ys", [128, KC, W + 2], F32)
        Bys = Buf("ys")
        hsx = sb("hsx", [128, KC, 2], F32)
        Bhsx = Buf("hsx")
        hsoo = sb("hsoo", [128, KC, 2], BF16)
        Bhsoo = Buf("hsoo")
        tiles = [(0, W + 2, 2)] + [(i, W, 0) for i in range(1, NT)]

        def tile_body(ti, Wc, nh):
            if nh:
                P.dma("sync", [(hsx[:], x1h)], "xhin", writes=[Bhsx])
                P.dma("sync", [(hsoo[:], oh)], "ohin", writes=[Bhsoo])
            P.dma("sync", [(dn.x[:, :, nh:nh + W], x1t[ti])], "xin", writes=dn.Bx)
            P.dma("sync", [(osb[:, :, nh:nh + W], ot[ti])], "oin", writes=[Bo])
            if nh:
                P.add("vector", lambda e: e.tensor_copy(out=dn.x[:, :, 0:2], in_=hsx[:]), reads=[Bhsx] + dn.Bx,
                      writes=dn.Bx)
                P.add("vector", lambda e: e.tensor_copy(out=osb[:, :, 0:2], in_=hsoo[:]), reads=[Bhsoo, Bo],
                      writes=[Bo])
            for m in range(KC):
                ws, Bw = dn.load_w(w_o[m])
                ps, Bps = dn.banks.next()
                dn.mm_group(ps, ws, osb, KC, Wc, [Bw, Bo], Bps)
                P.add("vector", lambda e, ps=ps, m=m: e.tensor_tensor(
                    out=dn.x[:, m, 0:Wc], in0=ps[:, 0:Wc], in1=dn.x[:, m, 0:Wc], op=ALU.add),
                    reads=[Bps, dn.Bx[m]], writes=[dn.Bx[m]])
            dn.ffn(1, Wc, w_up, w_dn, "fng1", nh)
            dn.colsum_rstd([(dn.x[:, j, 0:Wc], dn.Bx[j]) for j in range(KC)], Wc, 1.0 / D,
                           dn.rstd[:, 0:Wc], dn.Brstd)
            for j in range(KC):
                P.add("vector", lambda e, j=j: e.scalar_tensor_tensor(
                    out=ys[:, j, 0:Wc], in0=dn.x[:, j, 0:Wc], scalar=dn.c("fin", j), in1=dn.rstd[:, 0:Wc],
                    op0=ALU.mult, op1=ALU.mult), reads=[dn.Bx[j], dn.Brstd, dn.Bcst], writes=[Bys])
            finals.append(P.dma("sync", [(yt[ti], ys[:, :, nh:nh + W])], "yo", reads=[Bys]))
        for (ti_, Wc_, nh_) in tiles:
            tile_body(ti_, Wc_, nh_)
        P.emit(final_waits=finals)
    return nc


_CACHE = {}
_DBG = {}


def _tile_up(f, L):
    return _tile_w(f("ffn_w_up")[L], KC)


def _tile_dn(f, L):
    Wd = f("ffn_w_down")[L]
    return np.ascontiguousarray(Wd.reshape(G, PPG, 128, KC, 128).transpose(0, 3, 2, 1, 4))


def _phaseA(inputs):
    f = lambda k: np.asarray(inputs[k], np.float32)
    x = f("x")[0]
    cores = list(range(NCORE))

    cpA = CstPack()
    _cst_common(cpA, inputs)
    cpA.put("b_in", _cols(f("conv_b_in")[0]))
    cpA.put("cdw", np.ascontiguousarray(f("conv_dw_w")[0].T.reshape(KC, 128, CK).transpose(1, 0, 2)))
    cpA.put("cdb", _cols(f("conv_dw_b")[0]))
    cpA.put("lng", _cols(f("conv_ln_g")[0]))
    cpA.put("lnb", _cols(f("conv_ln_b")[0]))
    cpA.put("b_out", _cols(f("conv_b_out")[0]))
    cpA.put("flag", np.ones((128, 1), np.float32))
    cpA.put("ident", np.eye(128, dtype=np.float32))
    cstA = cpA.build()
    w_in_t = _tile_w(f("conv_w_in")[0], KC)
    w_out_t = _tile_w(f("conv_w_out")[0], KC)
    w_up0, w_dn0 = _tile_up(f, 0), _tile_dn(f, 0)
    in_maps = []
    for c in cores:
        cc = cstA.copy()
        if c == 0:
            cc[:, cpA.off["flag"]] = 0.0
        seg = np.zeros((HALO + TPC, D), np.float32)
        lo = c * TPC - HALO
        if lo < 0:
            seg[HALO:] = x[0:TPC]
        else:
            seg[:] = x[lo:lo + HALO + TPC]
        fm = _fm(seg)
        in_maps.append({"xt0": np.ascontiguousarray(fm[:, :, :WX]),
                        "xt": np.ascontiguousarray(fm[:, :, WX:].reshape(128, KC, NT - 1, W).transpose(2, 0, 1, 3)),
                        "cst": cc, "w_in": w_in_t, "w_out": w_out_t, "w_up": w_up0, "w_dn": w_dn0})
    ncA = build_A(cstA.shape[1], cpA.off)
    resA = run_bass_kernel_spmd(ncA, in_maps, core_ids=cores).results
    x1T = [(np.asarray(r["x1h"]), np.asarray(r["x1t"])) for r in resA]
    hall = np.ascontiguousarray(np.concatenate([np.asarray(r["h1t"]) for r in resA], axis=0))
    return x1T, hall


def _phaseB(inputs, hall):
    f = lambda k: np.asarray(inputs[k], np.float32)
    cores = list(range(NCORE))
    lambda_init = 0.8 - 0.6 * math.exp(-0.3 * 1)
    used, planes = _bucket_planes()
    wqkv = f("attn_w_qkv")[0]
    rel_bias = f("rel_bias")
    lam_in = np.concatenate([f("attn_lambda_q1")[0], f("attn_lambda_k1")[0], f("attn_lambda_q2")[0],
                             f("attn_lambda_k2")[0]])[None, :].repeat(128, 0)
    in_maps = []
    cpB = None
    for c in cores:
        cpB = CstPack()
        cpB.put("eps2", np.full((128, 1), EPS / (1.0 - lambda_init) ** 2, np.float32))
        cpB.put("lam", lam_in)
        hs = [2 * c, 2 * c + 1]
        cpB.put("tab", np.concatenate([rel_bias[:, h] for h in hs])[None, :].repeat(128, 0))
        cpB.put("farb", np.array([rel_bias[15, h] for h in hs], np.float32)[None, :].repeat(128, 0))
        cpB.put("subg", (f("attn_subln_g")[0])[:, None])
        wc = np.stack([np.stack([_tile_w(wqkv[:, part * D + h * 128: part * D + (h + 1) * 128], KC)[0]
                                 for part in range(3)]) for h in hs])
        in_maps.append({"hall": hall, "cstb": cpB.build(), "wqkv": np.ascontiguousarray(wc), "planes": planes})
    ncB = build_B(cpB.n, cpB.off, used, lambda_init)
    resB = run_bass_kernel_spmd(ncB, in_maps, core_ids=cores).results
    oT_all = np.concatenate([np.asarray(r["oT"]).reshape(256, S) for r in resB], axis=0)
    return oT_all


def _phaseC(inputs, x1T, oT_all):
    f = lambda k: np.asarray(inputs[k], np.float32)
    cores = list(range(NCORE))
    cpC = CstPack()
    _cst_common(cpC, inputs)
    cpC.put("flag", np.ones((128, 1), np.float32))
    cstC = cpC.build()
    w_o_t = _tile_w(f("attn_w_o")[0], KC)
    w_up1, w_dn1 = _tile_up(f, 1), _tile_dn(f, 1)
    in_maps = []
    for c in cores:
        cc = cstC.copy()
        if c == 0:
            cc[:, cpC.off["flag"]] = 0.0
        oc = np.zeros((D, 2 + TPC), oT_all.dtype)
        if c == 0:
            oc[:, 2:] = oT_all[:, 0:TPC]
        else:
            oc[:] = oT_all[:, c * TPC - 2:(c + 1) * TPC]
        ofm = oc.reshape(KC, 128, 2 + TPC).transpose(1, 0, 2)
        in_maps.append({"x1h": x1T[c][0], "x1t": x1T[c][1], "oh": np.ascontiguousarray(ofm[:, :, :2]),
                        "ot": np.ascontiguousarray(ofm[:, :, 2:].reshape(128, KC, NT, W).transpose(2, 0, 1, 3)),
                        "cst": cc, "w_o": w_o_t, "w_up": w_up1, "w_dn": w_dn1})
    ncC = build_C(cstC.shape[1], cpC.off)
    resC = run_bass_kernel_spmd(ncC, in_maps, core_ids=cores).results
    out = np.empty((1, S, D), np.float32)
    for c in cores:
        yt = np.asarray(resC[c]["yt"])
        out[0, c * TPC:(c + 1) * TPC, :] = yt.transpose(0, 3, 2, 1).reshape(TPC, D)
    return out


def kernel(**inputs):
    x1T, hall = _phaseA(inputs)
    oT_all = _phaseB(inputs, hall)
    return _phaseC(inputs, x1T, oT_all)
```
